# Optimizing a Trainium2 kernel written in Bass

```python
import jax, jax.numpy as jnp
from jax import lax
import numpy as np

D_MODEL = 2048
BATCH = 4
SEQ = 4096
DEPTH = 4

GRID_W = 64
CTX_LEN = 256
N_MIXERS = 2
N_RET_LAYERS = (DEPTH + 1) // 2
N_MLA_LAYERS = DEPTH // 2

RET_HEADS = 8
RET_DK = D_MODEL // RET_HEADS
RET_DV = 2 * D_MODEL // RET_HEADS
RET_CHUNK = 128
RET_ROPE_BASE = 10000.0
GN_EPS = 1e-6

MLA_HEADS = 16
MLA_Q_LORA = 512
MLA_KV_LORA = 512
MLA_D_NOPE = 128
MLA_D_ROPE = 64
MLA_D_V = 128
MLA_Q_BLOCK = 128
MLA_SCALE = (MLA_D_NOPE + MLA_D_ROPE) ** -0.5
AXIAL_ROPE_BASE = 10000.0
RMS_EPS = 1e-6

FFN_HIDDEN = (((8 * D_MODEL + 2) // 3) + 255) // 256 * 256

DEEPNORM_ALPHA = (2 * DEPTH) ** 0.25
DEEPNORM_BETA = (8 * DEPTH) ** -0.25
LN_EPS = 1e-5

kernel_name = 'hybrid_retention_mla_dit_trunk'


def layer_norm(x, g, b):
    xf = x.astype(jnp.float32)
    mu = jnp.mean(xf, axis=-1, keepdims=True)
    var = jnp.mean(jnp.square(xf - mu), axis=-1, keepdims=True)
    return ((xf - mu) * lax.rsqrt(var + LN_EPS) * g + b).astype(x.dtype)


def rms_norm(x, g):
    xf = x.astype(jnp.float32)
    return (xf * lax.rsqrt(jnp.mean(jnp.square(xf), axis=-1, keepdims=True) + RMS_EPS) * g).astype(x.dtype)


def rope_tables(pos, inv_freq):
    ang = pos.astype(jnp.float32)[:, None] * inv_freq[None, :]
    return jnp.cos(ang), jnp.sin(ang)


def apply_rope(x, cos, sin):
    x1, x2 = jnp.split(x, 2, axis=-1)
    c = cos[:, None, :]
    s = sin[:, None, :]
    return jnp.concatenate([x1 * c - x2 * s, x1 * s + x2 * c], axis=-1).astype(x.dtype)


def apply_rope_2d(x, row_cos, row_sin, col_cos, col_sin):
    xr, xc = jnp.split(x, 2, axis=-1)
    return jnp.concatenate([apply_rope(xr, row_cos, row_sin), apply_rope(xc, col_cos, col_sin)], axis=-1)


def modulation(cond, w, b):
    return jnp.split(jax.nn.silu(cond) @ w + b, 6, axis=-1)


def retention_chunkwise(q, k, v, log_gamma, s0):
    B, H, N, dk = q.shape
    dv = v.shape[-1]
    C = RET_CHUNK
    nc = N // C

    def to_chunks(t):
        return t.reshape(B, H, nc, C, t.shape[-1]).transpose(2, 0, 1, 3, 4)

    idx = jnp.arange(C, dtype=jnp.float32)
    diff = idx[:, None] - idx[None, :]
    intra = jnp.where(diff >= 0, jnp.exp(log_gamma[:, None, None] * jnp.maximum(diff, 0.0)), 0.0)
    q_dec = jnp.exp(log_gamma[:, None] * (idx + 1.0))[:, :, None]
    k_dec = jnp.exp(log_gamma[:, None] * (C - 1.0 - idx))[:, :, None]
    c_dec = jnp.exp(log_gamma * C)[:, None, None]

    def step(S, qkv):
        qc, kc, vc = qkv
        scores = jnp.einsum('bhid,bhjd->bhij', qc, kc) * intra
        o = jnp.einsum('bhij,bhjv->bhiv', scores, vc) + jnp.einsum('bhid,bhdv->bhiv', qc * q_dec, S)
        S = S * c_dec + jnp.einsum('bhjd,bhjv->bhdv', kc * k_dec, vc)
        return S, o

    S, o = lax.scan(step, s0, (to_chunks(q), to_chunks(k), to_chunks(v)))
    return o.transpose(1, 2, 0, 3, 4).reshape(B, H, N, dv), S


def head_group_norm(o):
    mu = jnp.mean(o, axis=-1, keepdims=True)
    var = jnp.mean(jnp.square(o - mu), axis=-1, keepdims=True)
    o = (o - mu) * lax.rsqrt(var + GN_EPS)
    B, H, N, dv = o.shape
    return o.transpose(0, 2, 1, 3).reshape(B, N, H * dv)


def retention_mixer(u_x, u_c, ret_cos, ret_sin, w_qkv, w_g, decay_logit, w_o, ctx_out):
    H, dk, dv = RET_HEADS, RET_DK, RET_DV

    def project(u):
        B, N, _ = u.shape
        q, k, v = jnp.split(u @ w_qkv, [H * dk, 2 * H * dk], axis=-1)
        return q.reshape(B, N, H, dk), k.reshape(B, N, H, dk) * (dk ** -0.5), v.reshape(B, N, H, dv)

    def heads_first(t):
        return t.astype(jnp.float32).transpose(0, 2, 1, 3)

    q_x, k_x, v_x = project(u_x)
    q_x = apply_rope(q_x, ret_cos, ret_sin)
    k_x = apply_rope(k_x, ret_cos, ret_sin)
    q_x, k_x, v_x = heads_first(q_x), heads_first(k_x), heads_first(v_x)
    q_c, k_c, v_c = project(u_c)
    q_c, k_c, v_c = heads_first(q_c), heads_first(k_c), heads_first(v_c)

    log_gamma = jax.nn.log_sigmoid(decay_logit.astype(jnp.float32))
    s0 = jnp.zeros((q_x.shape[0], H, dk, dv), jnp.float32)
    flip = lambda t: jnp.flip(t, axis=2)
    oc_f, sc_f = retention_chunkwise(q_c, k_c, v_c, log_gamma[0], s0)
    ox_f, _ = retention_chunkwise(q_x, k_x, v_x, log_gamma[0], sc_f)
    oc_b, sc_b = retention_chunkwise(flip(q_c), flip(k_c), flip(v_c), log_gamma[1], s0)
    ox_b, _ = retention_chunkwise(flip(q_x), flip(k_x), flip(v_x), log_gamma[1], sc_b)

    def combine(u, o_f, o_b):
        g_f, g_b = jnp.split(u @ w_g, 2, axis=-1)
        y = (jax.nn.silu(g_f) * head_group_norm(o_f).astype(u.dtype)
             + jax.nn.silu(g_b) * head_group_norm(o_b).astype(u.dtype))
        return y @ w_o

    y_x = combine(u_x, ox_f, flip(ox_b))
    y_c = combine(u_c, oc_f, flip(oc_b)) if ctx_out else None
    return y_x, y_c


def mla_attend(qn, qr, kn, kr, v):
    s = jnp.einsum('bqhd,bkhd->bhqk', qn, kn) + jnp.einsum('bqhd,bkd->bhqk', qr, kr)
    p = jax.nn.softmax(s.astype(jnp.float32) * MLA_SCALE, axis=-1)
    return jnp.einsum('bhqk,bkhd->bqhd', p.astype(v.dtype), v)


def mla_block_attention(qn, qr, kn, kr, v):
    B, N, H, _ = qn.shape
    nb = N // MLA_Q_BLOCK

    def blocks(t):
        return t.reshape(B, nb, MLA_Q_BLOCK, *t.shape[2:]).swapaxes(0, 1)

    o = lax.map(lambda qs: mla_attend(qs[0], qs[1], kn, kr, v), (blocks(qn), blocks(qr)))
    return o.swapaxes(0, 1).reshape(B, N, H, MLA_D_V)


def mla_mixer(u_x, u_c, row_cos, row_sin, col_cos, col_sin,
              w_dq, g_q, w_uq, w_dkv, g_kv, w_ukv, w_o, ctx_out):
    H, dn, dr, dv = MLA_HEADS, MLA_D_NOPE, MLA_D_ROPE, MLA_D_V

    def queries(u):
        B, N, _ = u.shape
        q = (rms_norm(u @ w_dq, g_q) @ w_uq).reshape(B, N, H, dn + dr)
        return q[..., :dn], q[..., dn:]

    def keys_values(u):
        B, N, _ = u.shape
        ckv = u @ w_dkv
        c_kv = rms_norm(ckv[..., :MLA_KV_LORA], g_kv)
        kr = ckv[..., MLA_KV_LORA:]
        kv = (c_kv @ w_ukv).reshape(B, N, H, dn + dv)
        return kv[..., :dn], kr, kv[..., dn:]

    qn_x, qr_x = queries(u_x)
    qr_x = apply_rope_2d(qr_x, row_cos, row_sin, col_cos, col_sin)
    kn_x, kr_x, v_x = keys_values(u_x)
    kr_x = apply_rope_2d(kr_x[:, :, None, :], row_cos, row_sin, col_cos, col_sin)[:, :, 0, :]
    kn_c, kr_c, v_c = keys_values(u_c)

    kn = jnp.concatenate([kn_x, kn_c], axis=1)
    kr = jnp.concatenate([kr_x, kr_c], axis=1)
    v = jnp.concatenate([v_x, v_c], axis=1)
    B, N, _ = u_x.shape
    o_x = mla_block_attention(qn_x, qr_x, kn, kr, v)
    y_x = o_x.reshape(B, N, H * dv) @ w_o
    y_c = None
    if ctx_out:
        qn_c, qr_c = queries(u_c)
        o_c = mla_attend(qn_c, qr_c, kn_c, kr_c, v_c)
        y_c = o_c.reshape(B, u_c.shape[1], H * dv) @ w_o
    return y_x, y_c


def swiglu(u, w_in, w_out):
    a, b = jnp.split(u @ w_in, 2, axis=-1)
    return (jax.nn.silu(a) * b) @ w_out


def setup_inputs(seed: int = 0) -> dict:
    key = jax.random.key(seed)
    ks = jax.random.split(key, 21)
    f32 = jnp.float32
    D, F = D_MODEL, FFN_HIDDEN

    def nrm(k, shape, scale):
        return jax.random.normal(k, shape, f32) * scale

    ret_qkv_w = 2 * RET_HEADS * RET_DK + RET_HEADS * RET_DV
    base_logit = jnp.log(2.0 ** (5.0 + jnp.arange(RET_HEADS, dtype=f32)) - 1.0)
    return {
        'x': nrm(ks[0], (BATCH, SEQ, D), 1.0),
        'c': nrm(ks[1], (BATCH, D), 1.0),
        'ctx': nrm(ks[2], (BATCH, CTX_LEN, D), 1.0),
        'c_ctx': nrm(ks[3], (D,), 1.0),
        'ada_w': nrm(ks[4], (DEPTH, D, 6 * D), 0.5 * D ** -0.5),
        'ada_b': nrm(ks[5], (DEPTH, 6 * D), 0.02),
        'ln_g': 1.0 + nrm(ks[6], (DEPTH, 2, D), 0.02),
        'ln_b': nrm(ks[7], (DEPTH, 2, D), 0.02),
        'ret_w_qkv': nrm(ks[8], (N_RET_LAYERS, D, ret_qkv_w), D ** -0.5),
        'ret_w_g': nrm(ks[9], (N_RET_LAYERS, D, 2 * RET_HEADS * RET_DV), D ** -0.5),
        'ret_decay_logit': base_logit + nrm(ks[10], (N_RET_LAYERS, 2, RET_HEADS), 0.1),
        'ret_w_o': nrm(ks[11], (N_RET_LAYERS, RET_HEADS * RET_DV, D), DEEPNORM_BETA * (RET_HEADS * RET_DV) ** -0.5),
        'mla_w_dq': nrm(ks[12], (N_MLA_LAYERS, D, MLA_Q_LORA), D ** -0.5),
        'mla_g_q': 1.0 + nrm(ks[13], (N_MLA_LAYERS, MLA_Q_LORA), 0.02),
        'mla_w_uq': nrm(ks[14], (N_MLA_LAYERS, MLA_Q_LORA, MLA_HEADS * (MLA_D_NOPE + MLA_D_ROPE)), MLA_Q_LORA ** -0.5),
        'mla_w_dkv': nrm(ks[15], (N_MLA_LAYERS, D, MLA_KV_LORA + MLA_D_ROPE), D ** -0.5),
        'mla_g_kv': 1.0 + nrm(ks[16], (N_MLA_LAYERS, MLA_KV_LORA), 0.02),
        'mla_w_ukv': nrm(ks[17], (N_MLA_LAYERS, MLA_KV_LORA, MLA_HEADS * (MLA_D_NOPE + MLA_D_V)), MLA_KV_LORA ** -0.5),
        'mla_w_o': nrm(ks[18], (N_MLA_LAYERS, MLA_HEADS * MLA_D_V, D), DEEPNORM_BETA * (MLA_HEADS * MLA_D_V) ** -0.5),
        'ffn_w_in': nrm(ks[19], (DEPTH, D, 2 * F), D ** -0.5),
        'ffn_w_out': nrm(ks[20], (DEPTH, F, D), DEEPNORM_BETA * F ** -0.5),
    }


def reference(x, c, ctx, c_ctx, ada_w, ada_b, ln_g, ln_b,
              ret_w_qkv, ret_w_g, ret_decay_logit, ret_w_o,
              mla_w_dq, mla_g_q, mla_w_uq, mla_w_dkv, mla_g_kv, mla_w_ukv, mla_w_o,
              ffn_w_in, ffn_w_out):
    N = x.shape[1]
    ROWS = N // GRID_W
    rows = jnp.repeat(jnp.arange(ROWS), GRID_W)
    cols = jnp.tile(jnp.arange(GRID_W), ROWS)
    t = jnp.arange(N)

    axial_dim = MLA_D_ROPE // 2
    axial_inv = AXIAL_ROPE_BASE ** (-jnp.arange(axial_dim // 2, dtype=jnp.float32) * 2.0 / axial_dim)
    row_cos, row_sin = rope_tables(rows, axial_inv)
    col_cos, col_sin = rope_tables(cols, axial_inv)
    ret_inv = RET_ROPE_BASE ** (-jnp.linspace(0.0, 1.0, RET_DK // 2, dtype=jnp.float32))
    ret_cos, ret_sin = rope_tables(t, ret_inv)

    h_x, h_c = x, ctx
    for i in range(DEPTH):
        ctx_out = i < DEPTH - 1
        sh_a, sc_a, g_a, sh_f, sc_f, g_f = [m[:, None, :] for m in modulation(c, ada_w[i], ada_b[i])]
        csh_a, csc_a, cg_a, csh_f, csc_f, cg_f = modulation(c_ctx, ada_w[i], ada_b[i])

        u_x = h_x * (1.0 + sc_a) + sh_a
        u_c = h_c * (1.0 + csc_a) + csh_a
        j = i // N_MIXERS
        if i % N_MIXERS == 0:
            y_x, y_c = retention_mixer(u_x, u_c, ret_cos, ret_sin, ret_w_qkv[j], ret_w_g[j],
                                       ret_decay_logit[j], ret_w_o[j], ctx_out)
        else:
            y_x, y_c = mla_mixer(u_x, u_c, row_cos, row_sin, col_cos, col_sin,
                                 mla_w_dq[j], mla_g_q[j], mla_w_uq[j], mla_w_dkv[j], mla_g_kv[j],
                                 mla_w_ukv[j], mla_w_o[j], ctx_out)
        h_x = layer_norm(DEEPNORM_ALPHA * h_x + g_a * y_x, ln_g[i, 0], ln_b[i, 0])
        f_x = swiglu(h_x * (1.0 + sc_f) + sh_f, ffn_w_in[i], ffn_w_out[i])
        h_x = layer_norm(DEEPNORM_ALPHA * h_x + g_f * f_x, ln_g[i, 1], ln_b[i, 1])
        if ctx_out:
            h_c = layer_norm(DEEPNORM_ALPHA * h_c + cg_a * y_c, ln_g[i, 0], ln_b[i, 0])
            f_c = swiglu(h_c * (1.0 + csc_f) + csh_f, ffn_w_in[i], ffn_w_out[i])
            h_c = layer_norm(DEEPNORM_ALPHA * h_c + cg_f * f_c, ln_g[i, 1], ln_b[i, 1])
    return h_x
```

```python
import contextlib
import numpy as np
import concourse.bass as bass
import concourse.mybir as mybir
from concourse.bass_utils import run_bass_kernel_spmd

F32 = mybir.dt.float32
BF16 = mybir.dt.bfloat16
I32 = mybir.dt.int32
AF = mybir.ActivationFunctionType
ALU = mybir.AluOpType

D = 2048
NLAT = 4096
NCTX = 256
TOK = NLAT + NCTX
DEPTH = 4
FH = 5632
ALPHA = (2 * DEPTH) ** 0.25
LN_EPS_EFF = 1e-5 / (ALPHA * ALPHA)
GN_EPS = 1e-6
RMS_EPS = 1e-6
MLA_SCALE = (128 + 64) ** -0.5
TILES = [(i * 512, 512) for i in range(8)] + [(NLAT, NCTX)]
ENG = ["pe", "act", "dve", "pool", "sp"]
BLK = {"pe": "tensor", "act": "scalar", "dve": "vector", "pool": "gpsimd", "sp": "sync"}
ARENA_BYTES = 200 * 1024


class Buf:
    __slots__ = ("name", "lw", "rd", "sem", "cnt", "persist")

    def __init__(self, name, persist=False):
        self.name = name
        self.lw = None
        self.rd = []
        self.sem = None
        self.cnt = 0
        self.persist = persist


class KB:
    def __init__(self, nc, stack):
        self.nc = nc
        self.stack = stack
        self.ins = {e: [] for e in ENG}
        self.esem = {e: stack.enter_context(nc.semaphore("es_" + e)) for e in ENG}
        self.bufs = []
        self.free_sems = []
        self.sem_cnt = {}
        self.nsem = 0
        self.arena = stack.enter_context(nc.sbuf_tensor("arena", [128, ARENA_BYTES // 2], BF16))
        self.psum = stack.enter_context(nc.psum_tensor("psum", [128, 8, 512], F32))
        self.pbufs = [Buf("ps%d" % i, True) for i in range(8)]
        self.bufs += self.pbufs
        self.pidx = 0
        self.base = 0
        self.ptr = 0

    def alloc(self, nbytes, name="t", persist=False):
        req = nbytes
        nbytes = (nbytes + 63) // 64 * 64
        off = self.ptr
        self.ptr += nbytes
        assert self.ptr <= ARENA_BYTES, ("arena overflow", name, self.ptr)
        b = Buf(name, persist)
        self.bufs.append(b)
        return b, self.arena[:, off // 2:(off + req) // 2]

    def alloc32(self, ncols, name="t", persist=False):
        b, ap = self.alloc(ncols * 4, name, persist)
        return b, ap.bitcast(F32)

    def alloc16(self, ncols, name="t", persist=False):
        return self.alloc(ncols * 2, name, persist)

    def ps(self):
        i = self.pidx
        self.pidx = (self.pidx + 1) % 8
        return self.pbufs[i], self.psum[:, i, :]

    def end_persist(self):
        self.base = self.ptr

    def _deps(self, eng, reads, writes):
        deps = []
        for b in reads:
            if b.lw is not None:
                deps.append(b.lw)
        for b in writes:
            if b.lw is not None:
                deps.append(b.lw)
            deps.extend(b.rd)
        if eng == "pe":
            deps = [d for d in deps if not (d[0] == "e" and d[1] == "pe")]
        return deps

    def op(self, eng, fn, reads=(), writes=()):
        idx = len(self.ins[eng])
        self.ins[eng].append(dict(fn=fn, deps=self._deps(eng, reads, writes)))
        ev = ("e", eng, idx)
        for b in reads:
            b.rd.append(ev)
        for b in writes:
            b.lw = ev
            b.rd = []

    def dma(self, q, out, in_, reads=(), writes=(), sb=None):
        own = sb or (writes[0] if writes else reads[0])
        if own.sem is None:
            if self.free_sems:
                own.sem = self.free_sems.pop()
            else:
                own.sem = self.stack.enter_context(self.nc.semaphore("ds%d" % self.nsem))
                self.nsem += 1
                self.sem_cnt[id(own.sem)] = 0
        self.sem_cnt[id(own.sem)] += 16
        val = self.sem_cnt[id(own.sem)]
        self.ins[q].append(dict(fn=lambda e: e.dma_start(out=out, in_=in_), deps=self._deps(q, reads, writes),
                                dsem=own.sem))
        ev = ("d", own.sem, val)
        for b in reads:
            b.rd.append(ev)
        for b in writes:
            b.lw = ev
            b.rd = []
        if own not in reads and own not in writes:
            own.rd.append(ev)

    def barrier(self):
        evs = []
        for e in ENG:
            for i in range(len(self.ins[e]) - 1, -1, -1):
                r = self.ins[e][i]
                if r["fn"] is not None and "dsem" not in r:
                    evs.append(("e", e, i))
                    break
        seen = set()
        for b in self.bufs:
            if b.sem is not None and id(b.sem) not in seen:
                seen.add(id(b.sem))
                evs.append(("d", b.sem, self.sem_cnt[id(b.sem)]))
        for e in ENG:
            self.ins[e].append(dict(fn=None, deps=list(evs)))
        keep = []
        for b in self.bufs:
            b.lw = None
            b.rd = []
            if b.sem is not None:
                self.free_sems.append(b.sem)
                b.sem = None
            if b.persist:
                keep.append(b)
        self.bufs = keep
        self.ptr = self.base

    def emit(self):
        ins = self.ins
        for e in ENG:
            for r in ins[e]:
                for d in r["deps"]:
                    if d[0] == "e":
                        ins[d[1]][d[2]]["ms"] = True
        for e in ENG:
            c = 0
            for r in ins[e]:
                if r.get("ms"):
                    c += 1
                    r["msv"] = c
        with self.nc.Block() as block:
            for e in ENG:
                def body(eng, e=e):
                    known = {}
                    for r in ins[e]:
                        for d in r["deps"]:
                            if d[0] == "e":
                                sem, val = self.esem[d[1]], ins[d[1]][d[2]]["msv"]
                            else:
                                sem, val = d[1], d[2]
                            if known.get(id(sem), 0) >= val:
                                continue
                            eng.wait_ge(sem, val)
                            known[id(sem)] = val
                        if r["fn"] is not None:
                            i = r["fn"](eng)
                            if "dsem" in r:
                                i.then_inc(r["dsem"], 16)
                            elif r.get("ms"):
                                i.then_inc(self.esem[e], 1)
                getattr(block, BLK[e])(body)


def build_program(dbg=None, nlayers=DEPTH):
    nc = bass.Bass("TRN2", target_bir_lowering=False)
    stack = contextlib.ExitStack()

    def din(name, shape):
        return nc.dram_tensor(name, list(shape), F32, kind="ExternalInput").ap()

    def dscr(name, shape, dt):
        return nc.dram_tensor(name, list(shape), dt, kind="Internal").ap()

    xT = din("xT", [D, TOK])
    condT = din("condT", [128, 32])
    ada_w = din("ada_w", [DEPTH, D, 6 * D])
    ada_bT = din("ada_bT", [DEPTH, 128, 96])
    ln_gT = din("ln_gT", [DEPTH, 2, 128, 16])
    ln_bT = din("ln_bT", [DEPTH, 2, 128, 16])
    ret_w_qkv = din("ret_w_qkv", [2, D, 8192])
    ret_w_g = din("ret_w_g", [2, D, 8192])
    ret_w_o = din("ret_w_o", [2, 4096, D])
    dlog = din("dlog", [2, 128, 16])
    w_dq = din("w_dq", [2, D, 512])
    w_dkv = din("w_dkv", [2, D, 640])
    g_qT = din("g_qT", [2, 128, 4])
    g_kvT = din("g_kvT", [2, 128, 4])
    w_uq = din("w_uq", [2, 512, 16 * 256])
    w_ukv_k = din("w_ukv_k", [2, 512, 2048])
    w_ukv_v = din("w_ukv_v", [2, 512, 2048])
    mla_w_o = din("mla_w_o", [2, D, D])
    ffn_w_in = din("ffn_w_in", [DEPTH, D, 2 * FH])
    ffn_w_out = din("ffn_w_out", [DEPTH, FH, D])
    rtab = din("rtab", [4, 128, NLAT])
    mtab = din("mtab", [2, 64, NLAT])
    ident_in = din("ident", [128, 128])
    outT = nc.dram_tensor("outT", [D, NLAT], F32, kind="ExternalOutput").ap()

    H = dscr("H", [D, TOK], F32)
    U = dscr("U", [D, TOK], BF16)
    QT = dscr("QT", [D, TOK], BF16)
    KT = dscr("KT", [D, TOK], BF16)
    V = dscr("V", [TOK, 4096], BF16)
    G = dscr("G", [TOK, 8192], BF16)
    YF = dscr("YF", [TOK, 4096], F32)
    YT = dscr("YT", [4096, TOK], BF16)
    HID = dscr("HID", [FH, TOK], BF16)
    CQN = dscr("CQN", [512, TOK], BF16)
    CKVN = dscr("CKVN", [512, TOK], BF16)
    KRT = dscr("KRT", [64, TOK], BF16)
    QN = dscr("QN", [D, TOK], BF16)
    QR = dscr("QR", [16 * 64, TOK], BF16)
    KN = dscr("KN", [D, TOK], BF16)
    VA = dscr("VA", [TOK, D], BF16)
    OT = dscr("OT", [D, TOK], BF16)
    dbg_out = {}
    if dbg:
        for name in dbg:
            src = {"H": H, "U": U, "QT": QT, "KT": KT, "V": V, "G": G, "YT": YT, "HID": HID, "YF": YF,
                   "OT": OT, "QN": QN, "KN": KN, "VA": VA, "QR": QR, "KRT": KRT, "CQN": CQN, "CKVN": CKVN}[name]
            dbg_out[name] = (src, nc.dram_tensor("dbg_" + name, list(src.shape), src.dtype, kind="ExternalOutput").ap())

    kb = KB(nc, stack)

    ones_b, ones32 = kb.alloc32(128, "ones32", True)
    id_b, ident = kb.alloc16(128, "ident", True)
    mod_b, MOD = kb.alloc32(DEPTH * 96 * 2, "MOD", True)
    kb.end_persist()

    def modidx(l, which, c, w):
        return ((l * 6 + which) * 16 + c) * 2 + w

    def modcol(l, which, w):
        s = modidx(l, which, 0, w)
        return MOD[:, s:s + 31:2]

    kb.op("pool", lambda e: e.memset(ones32, 1.0), writes=[ones_b])
    kb.dma("pool", ident, ident_in, writes=[id_b])
    hb = Buf("hcopy")
    kb.bufs.append(hb)
    kb.dma("sp", H, xT, sb=hb)
    cb, cnd = kb.alloc32(32, "cond")
    csb, cnds = kb.alloc16(32, "conds")
    kb.dma("sp", cnd, condT, writes=[cb])
    kb.op("act", lambda e: e.activation(out=cnds, in_=cnd, func=AF.Silu), reads=[cb], writes=[csb])
    wslots = [kb.alloc16(16 * 512, "ws%d" % i) for i in range(3)]
    abb, adab = kb.alloc32(96, "adab")
    wi = 0
    for l in range(DEPTH):
        pb, ps = kb.ps()
        for blk in range(24):
            sb_, sl = wslots[wi % 3]
            wi += 1
            kb.dma("pool", sl.rearrange("p (k n) -> p k n", n=512),
                   ada_w[l][:, blk * 512:(blk + 1) * 512].rearrange("(k p) n -> p k n", p=128), writes=[sb_])
            for mm in range(4):
                j = blk * 4 + mm
                for kc in range(16):
                    kb.op("pe", lambda e, ps=ps, sl=sl, j=j, kc=kc, mm=mm: e.matmul(
                        ps[:, j * 2:j * 2 + 2], sl[:, kc * 512 + mm * 128:kc * 512 + mm * 128 + 128],
                        cnds[:, kc * 2:kc * 2 + 2], start=(kc == 0), stop=(kc == 15)),
                        reads=[sb_, csb], writes=[pb])
        kb.dma("sp", adab, ada_bT[l], writes=[abb])
        for w in range(2):
            s = modidx(l, 0, 0, w)
            kb.op("dve", lambda e, ps=ps, s=s, w=w: e.tensor_tensor(
                out=MOD[:, s:s + 191:2], in0=ps[:, w:w + 191:2], in1=adab, op=ALU.add),
                reads=[pb, abb], writes=[mod_b])
    kb.barrier()

    def ln_tables(l, which_ln, has_next):
        gb, gt = kb.alloc32(16, "lng")
        bb, bt = kb.alloc32(16, "lnb")
        kb.dma("sp", gt, ln_gT[l, which_ln], writes=[gb])
        kb.dma("sp", bt, ln_bT[l, which_ln], writes=[bb])
        tb_, tt_ = kb.alloc32(16 * 8, "lntab")
        res = {"G": gt, "B": bt, "bufs": [gb, bb, tb_, mod_b]}
        gate_which = 2 if which_ln == 0 else 5
        for w in range(2):
            ga = tt_[:, w * 16:(w + 1) * 16]
            kb.op("dve", lambda e, ga=ga, w=w: e.tensor_scalar(
                out=ga, in0=modcol(l, gate_which, w), scalar1=1.0 / ALPHA, scalar2=None, op0=ALU.mult),
                reads=[mod_b], writes=[tb_])
            res["GA%d" % w] = ga
            if has_next:
                if which_ln == 0:
                    nl, nsc, nsh = l, 4, 3
                else:
                    nl, nsc, nsh = l + 1, 1, 0
                a1 = tt_[:, 32 + w * 16:32 + (w + 1) * 16]
                ug = tt_[:, 64 + w * 16:64 + (w + 1) * 16]
                ub = tt_[:, 96 + w * 16:96 + (w + 1) * 16]
                kb.op("dve", lambda e, a1=a1, w=w: e.tensor_scalar(
                    out=a1, in0=modcol(nl, nsc, w), scalar1=1.0, scalar2=None, op0=ALU.add),
                    reads=[mod_b], writes=[tb_])
                kb.op("dve", lambda e, a1=a1, ug=ug: e.tensor_tensor(out=ug, in0=a1, in1=gt, op=ALU.mult),
                      reads=[tb_, gb], writes=[tb_])
                kb.op("dve", lambda e, a1=a1, ub=ub: e.tensor_tensor(out=ub, in0=a1, in1=bt, op=ALU.mult),
                      reads=[tb_, bb], writes=[tb_])
                kb.op("dve", lambda e, ub=ub, w=w: e.tensor_tensor(
                    out=ub, in0=ub, in1=modcol(nl, nsh, w), op=ALU.add), reads=[tb_, mod_b], writes=[tb_])
                res["UG%d" % w] = ug
                res["UB%d" % w] = ub
        return res

    class WStream:
        def __init__(self, nslots=3, slot_cols=16 * 512):
            self.slots = [kb.alloc16(slot_cols, "ws%d" % i) for i in range(nslots)]
            self.i = 0

        def load(self, pieces, kc, n):
            b, sl = self.slots[self.i % len(self.slots)]
            self.i += 1
            v = sl[:, :kc * n].rearrange("p (k n) -> p k n", n=n)
            first = True
            for (src, off, ncol) in pieces:
                kb.dma("pool", v[:, :, off:off + ncol], src.rearrange("(k p) n -> p k n", p=128), writes=[b])
            return b, sl

    def load_x(slot, src, t0, T, kc):
        b, ap = slot
        kb.dma("sp", ap[:, :kc * 512].rearrange("p (k t) -> p k t", t=512)[:, :, :T],
               src[:, t0:t0 + T].rearrange("(k p) t -> p k t", p=128), writes=[b])

    def phase_mod0():
        slots = [kb.alloc32(16 * 512, "hin%d" % i) for i in range(2)]
        outs = [kb.alloc16(16 * 512, "uo%d" % i) for i in range(2)]
        a1b, a1 = kb.alloc32(32, "a1")
        for w in range(2):
            kb.op("dve", lambda e, w=w: e.tensor_scalar(out=a1[:, w * 16:(w + 1) * 16], in0=modcol(0, 1, w),
                                                        scalar1=1.0, scalar2=None, op0=ALU.add),
                  reads=[mod_b], writes=[a1b])
        for ti, (t0, T) in enumerate(TILES):
            w = 0 if t0 < NLAT else 1
            hb_, hin = slots[ti % 2]
            ob_, uo = outs[ti % 2]
            kb.dma("sp", hin.rearrange("p (k t) -> p k t", t=512)[:, :, :T],
                   xT[:, t0:t0 + T].rearrange("(k p) t -> p k t", p=128), writes=[hb_])
            for c in range(16):
                s = modidx(0, 0, c, w)
                kb.op("dve", lambda e, hin=hin, uo=uo, c=c, w=w, s=s, T=T: e.tensor_scalar(
                    out=uo[:, c * 512:c * 512 + T], in0=hin[:, c * 512:c * 512 + T],
                    scalar1=a1[:, w * 16 + c:w * 16 + c + 1], scalar2=MOD[:, s:s + 1], op0=ALU.mult, op1=ALU.add),
                    reads=[hb_, a1b, mod_b], writes=[ob_])
            kb.dma("sp", U[:, t0:t0 + T].rearrange("(k p) t -> p k t", p=128),
                   uo.rearrange("p (k t) -> p k t", t=512)[:, :, :T], reads=[ob_])
        kb.barrier()

    def phase_out_ln(l, which_ln, Xsrc, KC, Wsrc, tiles, has_next, final=False):
        tabs = ln_tables(l, which_ln, has_next)
        tb = tabs["bufs"]
        nb = 256 if KC <= 16 else 128
        ws = WStream(3, KC * nb)
        nx = 2 if KC <= 16 else 1
        xs = [kb.alloc16(KC * 512, "x%d" % i) for i in range(nx)]
        zb, Z = kb.alloc32(16 * 512, "Z")
        sq = [kb.alloc32(512, "sq%d" % i) for i in range(2)]
        mb_, MEAN = kb.alloc32(512, "mean")
        m2b, M2 = kb.alloc32(512, "m2")
        rb_, RSTD = kb.alloc32(512, "rstd")
        t1 = [kb.alloc32(512, "t1%d" % i) for i in range(2)]
        ho = [kb.alloc32(512, "ho%d" % i) for i in range(2)]
        uo = [kb.alloc16(512, "uo%d" % i) for i in range(2)]
        load_x(xs[0], Xsrc, tiles[0][0], tiles[0][1], KC)
        for ti, (t0, T) in enumerate(tiles):
            w = 0 if t0 < NLAT else 1
            xb, X = xs[ti % nx]
            if nx == 2 and ti + 1 < len(tiles):
                load_x(xs[(ti + 1) % 2], Xsrc, tiles[ti + 1][0], tiles[ti + 1][1], KC)
            kb.dma("sp", Z.rearrange("p (k t) -> p k t", t=512)[:, :, :T],
                   H[:, t0:t0 + T].rearrange("(k p) t -> p k t", p=128), writes=[zb])
            for blk in range(D // nb):
                wb, W = ws.load([(Wsrc[:, blk * nb:(blk + 1) * nb], 0, nb)], KC, nb)
                for mm in range(nb // 128):
                    c = blk * (nb // 128) + mm
                    pb, ps = kb.ps()
                    for kc in range(KC):
                        kb.op("pe", lambda e, ps=ps, W=W, X=X, kc=kc, mm=mm, T=T: e.matmul(
                            ps[:, :T], W[:, kc * nb + mm * 128:kc * nb + mm * 128 + 128],
                            X[:, kc * 512:kc * 512 + T], start=(kc == 0), stop=(kc == KC - 1)),
                            reads=[wb, xb], writes=[pb])
                    ga = tabs["GA%d" % w]
                    kb.op("dve", lambda e, ps=ps, c=c, T=T, ga=ga: e.scalar_tensor_tensor(
                        out=Z[:, c * 512:c * 512 + T], in0=ps[:, :T], scalar=ga[:, c:c + 1],
                        in1=Z[:, c * 512:c * 512 + T], op0=ALU.mult, op1=ALU.add),
                        reads=[pb] + tb, writes=[zb])
            if nx == 1 and ti + 1 < len(tiles):
                load_x(xs[0], Xsrc, tiles[ti + 1][0], tiles[ti + 1][1], KC)
            pb1, ps1 = kb.ps()
            pb2, ps2 = kb.ps()
            for c in range(16):
                kb.op("pe", lambda e, ps1=ps1, c=c, T=T: e.matmul(
                    ps1[:, :T], ones32, Z[:, c * 512:c * 512 + T], start=(c == 0), stop=(c == 15)),
                    reads=[ones_b, zb], writes=[pb1])
                qb, q = sq[c % 2]
                kb.op("act", lambda e, q=q, c=c, T=T: e.activation(
                    out=q[:, :T], in_=Z[:, c * 512:c * 512 + T], func=AF.Square), reads=[zb], writes=[qb])
                kb.op("pe", lambda e, ps2=ps2, q=q, c=c, T=T: e.matmul(
                    ps2[:, :T], ones32, q[:, :T], start=(c == 0), stop=(c == 15)),
                    reads=[ones_b, qb], writes=[pb2])
            kb.op("dve", lambda e, ps1=ps1, T=T: e.tensor_scalar(
                out=MEAN[:, :T], in0=ps1[:, :T], scalar1=1.0 / D, scalar2=None, op0=ALU.mult),
                reads=[pb1], writes=[mb_])
            kb.op("dve", lambda e, T=T: e.tensor_tensor(out=M2[:, :T], in0=MEAN[:, :T], in1=MEAN[:, :T], op=ALU.mult),
                  reads=[mb_], writes=[m2b])
            kb.op("dve", lambda e, ps2=ps2, T=T: e.scalar_tensor_tensor(
                out=M2[:, :T], in0=ps2[:, :T], scalar=1.0 / D, in1=M2[:, :T], op0=ALU.mult, op1=ALU.subtract),
                reads=[pb2, m2b], writes=[m2b])
            kb.op("dve", lambda e, T=T: e.tensor_scalar(
                out=M2[:, :T], in0=M2[:, :T], scalar1=LN_EPS_EFF, scalar2=None, op0=ALU.add),
                reads=[m2b], writes=[m2b])
            kb.op("act", lambda e, T=T: e.activation(out=RSTD[:, :T], in_=M2[:, :T], func=AF.Sqrt),
                  reads=[m2b], writes=[rb_])
            kb.op("dve", lambda e, T=T: e.reciprocal(out=RSTD[:, :T], in_=RSTD[:, :T]), reads=[rb_], writes=[rb_])
            for c in range(16):
                tb1, T1 = t1[c % 2]
                hb1, HO = ho[c % 2]
                ub1, UO = uo[c % 2]
                kb.op("dve", lambda e, T1=T1, c=c, T=T: e.tensor_tensor(
                    out=T1[:, :T], in0=Z[:, c * 512:c * 512 + T], in1=MEAN[:, :T], op=ALU.subtract),
                    reads=[zb, mb_], writes=[tb1])
                kb.op("dve", lambda e, T1=T1, T=T: e.tensor_tensor(
                    out=T1[:, :T], in0=T1[:, :T], in1=RSTD[:, :T], op=ALU.mult), reads=[tb1, rb_], writes=[tb1])
                kb.op("act", lambda e, T1=T1, HO=HO, c=c, T=T: e.activation(
                    out=HO[:, :T], in_=T1[:, :T], func=AF.Identity, scale=tabs["G"][:, c:c + 1],
                    bias=tabs["B"][:, c:c + 1]), reads=[tb1] + tb, writes=[hb1])
                if final:
                    kb.dma("sp", outT[c * 128:(c + 1) * 128, t0:t0 + T], HO[:, :T], reads=[hb1])
                else:
                    kb.dma("sp", H[c * 128:(c + 1) * 128, t0:t0 + T], HO[:, :T], reads=[hb1])
                if has_next:
                    ug, ub = tabs["UG%d" % w], tabs["UB%d" % w]
                    kb.op("act", lambda e, T1=T1, UO=UO, c=c, T=T, ug=ug, ub=ub: e.activation(
                        out=UO[:, :T], in_=T1[:, :T], func=AF.Identity, scale=ug[:, c:c + 1],
                        bias=ub[:, c:c + 1]), reads=[tb1] + tb, writes=[ub1])
                    kb.dma("sp", U[c * 128:(c + 1) * 128, t0:t0 + T], UO[:, :T], reads=[ub1])
        kb.barrier()

    def phase_ffn_in(l, tiles):
        ws = WStream(3, 16 * 512)
        xs = [kb.alloc16(16 * 512, "x%d" % i) for i in range(2)]
        sa = [kb.alloc32(512, "sa%d" % i) for i in range(2)]
        hd = [kb.alloc16(512, "hd%d" % i) for i in range(3)]
        Wsrc = ffn_w_in[l]
        load_x(xs[0], U, tiles[0][0], tiles[0][1], 16)
        hi = 0
        for ti, (t0, T) in enumerate(tiles):
            xb, X = xs[ti % 2]
            if ti + 1 < len(tiles):
                load_x(xs[(ti + 1) % 2], U, tiles[ti + 1][0], tiles[ti + 1][1], 16)
            for blk in range(22):
                wb, W = ws.load([(Wsrc[:, blk * 256:(blk + 1) * 256], 0, 256),
                                 (Wsrc[:, FH + blk * 256:FH + (blk + 1) * 256], 256, 256)], 16, 512)
                for jj in range(2):
                    j = blk * 2 + jj
                    pa, psa = kb.ps()
                    pb, psb = kb.ps()
                    for (pp, pps, off) in ((pa, psa, jj * 128), (pb, psb, 256 + jj * 128)):
                        for kc in range(16):
                            kb.op("pe", lambda e, pps=pps, W=W, X=X, kc=kc, off=off, T=T: e.matmul(
                                pps[:, :T], W[:, kc * 512 + off:kc * 512 + off + 128],
                                X[:, kc * 512:kc * 512 + T], start=(kc == 0), stop=(kc == 15)),
                                reads=[wb, xb], writes=[pp])
                    sb_, SA = sa[j % 2]
                    hb_, HD = hd[hi % 3]
                    hi += 1
                    kb.op("act", lambda e, SA=SA, psa=psa, T=T: e.activation(out=SA[:, :T], in_=psa[:, :T], func=AF.Silu),
                          reads=[pa], writes=[sb_])
                    kb.op("dve", lambda e, SA=SA, HD=HD, psb=psb, T=T: e.tensor_tensor(
                        out=HD[:, :T], in0=psb[:, :T], in1=SA[:, :T], op=ALU.mult), reads=[pb, sb_], writes=[hb_])
                    kb.dma("sp", HID[j * 128:(j + 1) * 128, t0:t0 + T], HD[:, :T], reads=[hb_])
        kb.barrier()

    def phase_ret_proj(j):
        ws = WStream(3, 16 * 512)
        xs = [kb.alloc16(16 * 512, "x%d" % i) for i in range(2)]
        tabb, TAB = kb.alloc32(4 * 512, "rtab")
        tm = [kb.alloc32(512, "tm%d" % i) for i in range(4)]
        o16 = [kb.alloc16(512, "o%d" % i) for i in range(4)]
        Wq = ret_w_qkv[j]
        Wg = ret_w_g[j]
        load_x(xs[0], U, TILES[0][0], TILES[0][1], 16)
        oi = 0
        for ti, (t0, T) in enumerate(TILES):
            lat = t0 < NLAT
            xb, X = xs[ti % 2]
            if ti + 1 < len(TILES):
                load_x(xs[(ti + 1) % 2], U, TILES[ti + 1][0], TILES[ti + 1][1], 16)
            if lat:
                kb.dma("sp", TAB.rearrange("p (a t) -> p a t", t=512),
                       rtab[:, :, t0:t0 + T].rearrange("a p t -> p a t"), writes=[tabb])
            for hk in range(16):
                isk = hk >= 8
                dst = KT if isk else QT
                h = hk % 8
                wb, W = ws.load([(Wq[:, hk * 256:(hk + 1) * 256], 0, 256)], 16, 256)
                p1, ps1 = kb.ps()
                p2, ps2 = kb.ps()
                for (pp, pps, off) in ((p1, ps1, 0), (p2, ps2, 128)):
                    for kc in range(16):
                        kb.op("pe", lambda e, pps=pps, W=W, X=X, kc=kc, off=off, T=T: e.matmul(
                            pps[:, :T], W[:, kc * 256 + off:kc * 256 + off + 128],
                            X[:, kc * 512:kc * 512 + T], start=(kc == 0), stop=(kc == 15)),
                            reads=[wb, xb], writes=[pp])
                ob1, O1 = o16[oi % 4]
                ob2, O2 = o16[(oi + 1) % 4]
                oi += 2
                if lat:
                    cs = TAB[:, (2 if isk else 0) * 512:(2 if isk else 0) * 512 + T]
                    sn = TAB[:, (3 if isk else 1) * 512:(3 if isk else 1) * 512 + T]
                    (b1, A1), (b2, A2), (b3, A3), (b4, A4) = tm
                    kb.op("dve", lambda e, A1=A1, ps1=ps1, cs=cs, T=T: e.tensor_tensor(out=A1[:, :T], in0=ps1[:, :T], in1=cs, op=ALU.mult),
                          reads=[p1, tabb], writes=[b1])
                    kb.op("dve", lambda e, A2=A2, ps2=ps2, sn=sn, T=T: e.tensor_tensor(out=A2[:, :T], in0=ps2[:, :T], in1=sn, op=ALU.mult),
                          reads=[p2, tabb], writes=[b2])
                    kb.op("dve", lambda e, A1=A1, A2=A2, O1=O1, T=T: e.tensor_tensor(out=O1[:, :T], in0=A1[:, :T], in1=A2[:, :T], op=ALU.subtract),
                          reads=[b1, b2], writes=[ob1])
                    kb.op("dve", lambda e, A3=A3, ps1=ps1, sn=sn, T=T: e.tensor_tensor(out=A3[:, :T], in0=ps1[:, :T], in1=sn, op=ALU.mult),
                          reads=[p1, tabb], writes=[b3])
                    kb.op("dve", lambda e, A4=A4, ps2=ps2, cs=cs, T=T: e.tensor_tensor(out=A4[:, :T], in0=ps2[:, :T], in1=cs, op=ALU.mult),
                          reads=[p2, tabb], writes=[b4])
                    kb.op("dve", lambda e, A3=A3, A4=A4, O2=O2, T=T: e.tensor_tensor(out=O2[:, :T], in0=A3[:, :T], in1=A4[:, :T], op=ALU.add),
                          reads=[b3, b4], writes=[ob2])
                else:
                    sc = 1.0 / 16.0 if isk else 1.0
                    kb.op("act", lambda e, O1=O1, ps1=ps1, T=T, sc=sc: e.activation(out=O1[:, :T], in_=ps1[:, :T], func=AF.Copy, scale=sc),
                          reads=[p1], writes=[ob1])
                    kb.op("act", lambda e, O2=O2, ps2=ps2, T=T, sc=sc: e.activation(out=O2[:, :T], in_=ps2[:, :T], func=AF.Copy, scale=sc),
                          reads=[p2], writes=[ob2])
                kb.dma("sp", dst[h * 256:h * 256 + 128, t0:t0 + T], O1[:, :T], reads=[ob1])
                kb.dma("sp", dst[h * 256 + 128:h * 256 + 256, t0:t0 + T], O2[:, :T], reads=[ob2])
            for cb_ in range(24):
                if cb_ < 8:
                    src = Wq[:, 4096 + cb_ * 512:4096 + (cb_ + 1) * 512]
                else:
                    src = Wg[:, (cb_ - 8) * 512:(cb_ - 7) * 512]
                wb, W = ws.load([(src, 0, 512)], 16, 512)
                for tq in range(T // 128):
                    pp, pps = kb.ps()
                    for kc in range(16):
                        kb.op("pe", lambda e, pps=pps, W=W, X=X, kc=kc, tq=tq: e.matmul(
                            pps, X[:, kc * 512 + tq * 128:kc * 512 + tq * 128 + 128],
                            W[:, kc * 512:(kc + 1) * 512], start=(kc == 0), stop=(kc == 15)),
                            reads=[wb, xb], writes=[pp])
                    ob, O = o16[oi % 4]
                    oi += 1
                    fn = AF.Copy if cb_ < 8 else AF.Silu
                    kb.op("act", lambda e, O=O, pps=pps, fn=fn: e.activation(out=O, in_=pps, func=fn),
                          reads=[pp], writes=[ob])
                    r0 = t0 + tq * 128
                    if cb_ < 8:
                        kb.dma("sp", V[r0:r0 + 128, cb_ * 512:(cb_ + 1) * 512], O, reads=[ob])
                    else:
                        kb.dma("sp", G[r0:r0 + 128, (cb_ - 8) * 512:(cb_ - 7) * 512], O, reads=[ob])
        kb.barrier()

    def phase_ret_scan(j):
        dlb, DL = kb.alloc32(16, "dl")
        kb.dma("sp", DL, dlog[j], writes=[dlb])
        kb.op("act", lambda e: e.activation(out=DL, in_=DL, func=AF.Exp, scale=-1.0), reads=[dlb], writes=[dlb])
        kb.op("act", lambda e: e.activation(out=DL, in_=DL, func=AF.Ln, bias=1.0), reads=[dlb], writes=[dlb])
        ixb, IXI = kb.alloc(4 * 128, "ixi")
        IXI = IXI.bitcast(I32)
        dfb, DIFF = kb.alloc32(128, "diff")
        kb.op("pool", lambda e: e.iota(IXI, pattern=[[1, 128]], base=0, channel_multiplier=-1), writes=[ixb])
        kb.op("dve", lambda e: e.tensor_copy(out=DIFF, in_=IXI), reads=[ixb], writes=[dfb])
        pcb, PC = kb.alloc32(4, "pc")
        kb.op("dve", lambda e: e.tensor_scalar(out=PC[:, 3:4], in0=DIFF[:, 0:1], scalar1=-1.0, scalar2=None, op0=ALU.mult),
              reads=[dfb], writes=[pcb])
        kb.op("dve", lambda e: e.tensor_scalar(out=PC[:, 0:1], in0=PC[:, 3:4], scalar1=1.0, scalar2=None, op0=ALU.add),
              reads=[pcb], writes=[pcb])
        kb.op("dve", lambda e: e.tensor_scalar(out=PC[:, 1:2], in0=DIFF[:, 0:1], scalar1=127.0, scalar2=None, op0=ALU.add),
              reads=[dfb], writes=[pcb])
        kb.op("dve", lambda e: e.tensor_scalar(out=PC[:, 2:3], in0=DIFF[:, 0:1], scalar1=128.0, scalar2=None, op0=ALU.add),
              reads=[dfb], writes=[pcb])
        agb, ARG = kb.alloc32(48, "arg")
        decb, DEC = kb.alloc32(48, "dec")
        rqb, RQD = kb.alloc32(16, "rqd")
        for d in range(2):
            qc = PC[:, 0:1] if d == 0 else PC[:, 2:3]
            kc_ = PC[:, 1:2] if d == 0 else PC[:, 3:4]
            kb.op("dve", lambda e, d=d, qc=qc: e.tensor_scalar(out=ARG[:, d * 8:d * 8 + 8], in0=DL[:, d * 8:d * 8 + 8],
                                                               scalar1=qc, scalar2=None, op0=ALU.mult),
                  reads=[dlb, pcb], writes=[agb])
            kb.op("dve", lambda e, d=d, kc_=kc_: e.tensor_scalar(out=ARG[:, 16 + d * 8:16 + d * 8 + 8], in0=DL[:, d * 8:d * 8 + 8],
                                                                 scalar1=kc_, scalar2=None, op0=ALU.mult),
                  reads=[dlb, pcb], writes=[agb])
        kb.op("dve", lambda e: e.tensor_scalar(out=ARG[:, 32:48], in0=DL, scalar1=128.0, scalar2=None, op0=ALU.mult),
              reads=[dlb], writes=[agb])
        kb.op("act", lambda e: e.activation(out=DEC, in_=ARG, func=AF.Exp, scale=-1.0), reads=[agb], writes=[decb])
        kb.op("act", lambda e: e.activation(out=RQD, in_=ARG[:, 0:16], func=AF.Exp, scale=1.0), reads=[agb], writes=[rqb])
        inb, IND = kb.alloc32(256, "ind")
        kb.op("dve", lambda e: e.tensor_scalar(out=IND[:, 0:128], in0=DIFF, scalar1=0.0, scalar2=None, op0=ALU.is_ge),
              reads=[dfb], writes=[inb])
        kb.op("dve", lambda e: e.tensor_scalar(out=IND[:, 128:256], in0=DIFF, scalar1=0.0, scalar2=None, op0=ALU.is_le),
              reads=[dfb], writes=[inb])

        Sb, S = kb.alloc32(8 * 2 * 512, "S")
        Sbb, Sbf = kb.alloc16(8 * 2 * 512, "Sbf")
        ld = []
        for i in range(2):
            ld.append(dict(q=kb.alloc16(2048, "q%d" % i), k=kb.alloc16(2048, "k%d" % i),
                           v=kb.alloc16(4096, "v%d" % i), g=kb.alloc16(4096, "g%d" % i)))
        kdb, KD = kb.alloc16(2048, "kd")
        pts = [kb.alloc16(128, "pt%d" % i) for i in range(2)]
        o1b, O1 = kb.alloc32(8 * 512, "O1")
        o2 = [kb.alloc32(512, "o2%d" % i) for i in range(2)]
        ydb, YD = kb.alloc32(4096, "YD")
        yfb, YFc = kb.alloc32(4096, "YFc")
        ybb, Yb = kb.alloc16(4096, "Yb")
        yts = [kb.alloc16(1024, "yt%d" % i) for i in range(2)]
        stb, ST = kb.alloc32(8 * 6, "st")
        mvb, MV = kb.alloc32(8 * 2, "mv")
        rsb, RS = kb.alloc32(8, "rs")
        nbb, NB = kb.alloc32(8, "nb")
        li = 0
        orders = [[32, 33] + list(range(32)), [33, 32] + list(range(31, -1, -1))]
        for d in range(2):
            for n, ck in enumerate(orders[d]):
                first = (n == 0)
                tok0 = ck * 128
                cur = ld[li % 2]
                li += 1
                (qb, Q), (kb_, K), (vb, Vc), (gb, Gc) = cur["q"], cur["k"], cur["v"], cur["g"]
                kb.dma("sp", Q.rearrange("p (c t) -> p c t", t=128), QT[:, tok0:tok0 + 128].rearrange("(c p) t -> p c t", p=128), writes=[qb])
                kb.dma("sp", K.rearrange("p (c t) -> p c t", t=128), KT[:, tok0:tok0 + 128].rearrange("(c p) t -> p c t", p=128), writes=[kb_])
                kb.dma("sp", Vc, V[tok0:tok0 + 128, :], writes=[vb])
                kb.dma("sp", Gc, G[tok0:tok0 + 128, d * 4096:(d + 1) * 4096], writes=[gb])
                if d == 1:
                    kb.dma("sp", YFc, YF[tok0:tok0 + 128, :], writes=[yfb])
                for half in range(2):
                    pb, ps = kb.ps()
                    psT = ps.bitcast(BF16)
                    for c8 in range(8):
                        c = half * 8 + c8
                        kb.op("pe", lambda e, psT=psT, K=K, c=c, c8=c8: e.transpose(
                            psT[:, c8 * 128:(c8 + 1) * 128], K[:, c * 128:(c + 1) * 128], ident),
                            reads=[kb_, id_b], writes=[pb])
                    for h4 in range(4):
                        h = half * 4 + h4
                        kb.op("act", lambda e, psT=psT, h=h, h4=h4, d=d: e.activation(
                            out=KD[:, h * 256:(h + 1) * 256], in_=psT[:, h4 * 256:(h4 + 1) * 256], func=AF.Copy,
                            scale=DEC[:, 16 + d * 8 + h:16 + d * 8 + h + 1]), reads=[pb, decb], writes=[kdb])
                for h in range(8):
                    pbs, pss = kb.ps()
                    for dc in range(2):
                        c = 2 * h + dc
                        kb.op("pe", lambda e, pss=pss, K=K, Q=Q, c=c, dc=dc: e.matmul(
                            pss[:, :128], K[:, c * 128:(c + 1) * 128], Q[:, c * 128:(c + 1) * 128],
                            start=(dc == 0), stop=(dc == 1)), reads=[kb_, qb], writes=[pbs])
                    ptb, PT = pts[h % 2]
                    kb.op("dve", lambda e, PT=PT, pss=pss, d=d, h=h: e.scalar_tensor_tensor(
                        out=PT, in0=pss[:, :128], scalar=RQD[:, d * 8 + h:d * 8 + h + 1],
                        in1=IND[:, d * 128:(d + 1) * 128], op0=ALU.mult, op1=ALU.mult),
                        reads=[pbs, rqb, inb], writes=[ptb])
                    pbo, pso = kb.ps()
                    kb.op("pe", lambda e, pso=pso, PT=PT, Vc=Vc, h=h, first=first: e.matmul(
                        pso, PT, Vc[:, h * 512:(h + 1) * 512], start=True, stop=first), reads=[ptb, vb], writes=[pbo])
                    if not first:
                        for dc in range(2):
                            c = 2 * h + dc
                            kb.op("pe", lambda e, pso=pso, Q=Q, c=c, dc=dc, h=h: e.matmul(
                                pso, Q[:, c * 128:(c + 1) * 128], Sbf[:, (h * 2 + dc) * 512:(h * 2 + dc + 1) * 512],
                                start=False, stop=(dc == 1)), reads=[qb, Sbb], writes=[pbo])
                    kb.op("act", lambda e, pso=pso, h=h, d=d: e.activation(
                        out=O1[:, h * 512:(h + 1) * 512], in_=pso, func=AF.Copy, scale=DEC[:, d * 8 + h:d * 8 + h + 1]),
                        reads=[pbo, decb], writes=[o1b])
                    for dc in range(2):
                        pbu, psu = kb.ps()
                        kb.op("pe", lambda e, psu=psu, h=h, dc=dc, Vc=Vc: e.matmul(
                            psu, KD[:, h * 256 + dc * 128:h * 256 + dc * 128 + 128], Vc[:, h * 512:(h + 1) * 512],
                            start=True, stop=True), reads=[kdb, vb], writes=[pbu])
                        sl = S[:, (h * 2 + dc) * 512:(h * 2 + dc + 1) * 512]
                        if first:
                            kb.op("dve", lambda e, sl=sl, psu=psu: e.tensor_copy(out=sl, in_=psu), reads=[pbu], writes=[Sb])
                        else:
                            kb.op("dve", lambda e, sl=sl, psu=psu, d=d, h=h: e.scalar_tensor_tensor(
                                out=sl, in0=sl, scalar=DEC[:, 32 + d * 8 + h:32 + d * 8 + h + 1], in1=psu,
                                op0=ALU.mult, op1=ALU.add), reads=[pbu, Sb, decb], writes=[Sb])
                        kb.op("act", lambda e, sl=sl, h=h, dc=dc: e.activation(
                            out=Sbf[:, (h * 2 + dc) * 512:(h * 2 + dc + 1) * 512], in_=sl, func=AF.Copy),
                            reads=[Sb], writes=[Sbb])
                for h in range(8):
                    kb.op("dve", lambda e, h=h: e.bn_stats(out=ST[:, h * 6:(h + 1) * 6], in_=O1[:, h * 512:(h + 1) * 512]),
                          reads=[o1b], writes=[stb])
                for h in range(8):
                    kb.op("dve", lambda e, h=h: e.bn_aggr(out=MV[:, h * 2:(h + 1) * 2], in_=ST[:, h * 6:(h + 1) * 6]),
                          reads=[stb], writes=[mvb])
                kb.op("dve", lambda e: e.tensor_scalar(out=RS, in0=MV[:, 1:16:2], scalar1=GN_EPS, scalar2=None, op0=ALU.add),
                      reads=[mvb], writes=[rsb])
                kb.op("act", lambda e: e.activation(out=RS, in_=RS, func=AF.Sqrt), reads=[rsb], writes=[rsb])
                kb.op("dve", lambda e: e.reciprocal(out=RS, in_=RS), reads=[rsb], writes=[rsb])
                kb.op("dve", lambda e: e.scalar_tensor_tensor(out=NB, in0=MV[:, 0:16:2], scalar=-1.0, in1=RS,
                                                              op0=ALU.mult, op1=ALU.mult), reads=[mvb, rsb], writes=[nbb])
                for h in range(8):
                    ob, O2 = o2[h % 2]
                    kb.op("act", lambda e, O2=O2, h=h: e.activation(
                        out=O2, in_=O1[:, h * 512:(h + 1) * 512], func=AF.Identity, scale=RS[:, h:h + 1], bias=NB[:, h:h + 1]),
                        reads=[o1b, rsb, nbb], writes=[ob])
                    kb.op("dve", lambda e, O2=O2, h=h, Gc=Gc: e.tensor_tensor(
                        out=YD[:, h * 512:(h + 1) * 512], in0=O2, in1=Gc[:, h * 512:(h + 1) * 512], op=ALU.mult),
                        reads=[ob, gb], writes=[ydb])
                if d == 0:
                    kb.dma("sp", YF[tok0:tok0 + 128, :], YD, reads=[ydb])
                else:
                    kb.op("dve", lambda e: e.tensor_tensor(out=Yb, in0=YD, in1=YFc, op=ALU.add), reads=[ydb, yfb], writes=[ybb])
                    for g4 in range(4):
                        pb, ps = kb.ps()
                        psT = ps.bitcast(BF16)
                        for c8 in range(8):
                            c = g4 * 8 + c8
                            kb.op("pe", lambda e, psT=psT, c=c, c8=c8: e.transpose(
                                psT[:, c8 * 128:(c8 + 1) * 128], Yb[:, c * 128:(c + 1) * 128], ident),
                                reads=[ybb, id_b], writes=[pb])
                        tb_, YTt = yts[g4 % 2]
                        kb.op("act", lambda e, YTt=YTt, psT=psT: e.activation(out=YTt, in_=psT, func=AF.Copy),
                              reads=[pb], writes=[tb_])
                        kb.dma("sp", YT[g4 * 1024:(g4 + 1) * 1024, tok0:tok0 + 128].rearrange("(c p) t -> p c t", p=128),
                               YTt.rearrange("p (c t) -> p c t", t=128), reads=[tb_])
        kb.barrier()

    def phase_mla_down(j):
        ws = WStream(3, 16 * 256)
        xs = [kb.alloc16(16 * 512, "x%d" % i) for i in range(2)]
        gqb, GQ = kb.alloc32(8, "gq")
        kb.dma("sp", GQ[:, 0:4], g_qT[j], writes=[gqb])
        kb.dma("sp", GQ[:, 4:8], g_kvT[j], writes=[gqb])
        mtb, MT = kb.alloc32(2 * 512, "mtab")
        cb_, C = kb.alloc32(4 * 512, "C")
        sq = [kb.alloc32(512, "sq%d" % i) for i in range(2)]
        rb_, R = kb.alloc32(512, "R")
        on = [kb.alloc16(512, "on%d" % i) for i in range(3)]
        tm = [kb.alloc32(512, "tm%d" % i) for i in range(2)]
        load_x(xs[0], U, TILES[0][0], TILES[0][1], 16)
        oi = 0
        for ti, (t0, T) in enumerate(TILES):
            lat = t0 < NLAT
            xb, X = xs[ti % 2]
            if ti + 1 < len(TILES):
                load_x(xs[(ti + 1) % 2], U, TILES[ti + 1][0], TILES[ti + 1][1], 16)
            if lat:
                kb.dma("sp", MT.rearrange("p (a t) -> p a t", t=512)[0:64], mtab[:, :, t0:t0 + T].rearrange("a p t -> p a t"),
                       writes=[mtb])
            for which, (Wsrc, dst, goff) in enumerate(((w_dq[j], CQN, 0), (w_dkv[j], CKVN, 4))):
                for blk in range(2):
                    wb, W = ws.load([(Wsrc[:, blk * 256:(blk + 1) * 256], 0, 256)], 16, 256)
                    for mm in range(2):
                        c = blk * 2 + mm
                        pb, ps = kb.ps()
                        for kc in range(16):
                            kb.op("pe", lambda e, ps=ps, W=W, X=X, kc=kc, mm=mm, T=T: e.matmul(
                                ps[:, :T], W[:, kc * 256 + mm * 128:kc * 256 + mm * 128 + 128],
                                X[:, kc * 512:kc * 512 + T], start=(kc == 0), stop=(kc == 15)), reads=[wb, xb], writes=[pb])
                        kb.op("act", lambda e, ps=ps, c=c, T=T: e.activation(out=C[:, c * 512:c * 512 + T], in_=ps[:, :T], func=AF.Copy),
                              reads=[pb], writes=[cb_])
                pbs, pss = kb.ps()
                for c in range(4):
                    qb, q = sq[c % 2]
                    kb.op("act", lambda e, q=q, c=c, T=T: e.activation(out=q[:, :T], in_=C[:, c * 512:c * 512 + T], func=AF.Square),
                          reads=[cb_], writes=[qb])
                    kb.op("pe", lambda e, pss=pss, q=q, c=c, T=T: e.matmul(pss[:, :T], ones32, q[:, :T], start=(c == 0), stop=(c == 3)),
                          reads=[ones_b, qb], writes=[pbs])
                kb.op("dve", lambda e, pss=pss, T=T: e.tensor_scalar(out=R[:, :T], in0=pss[:, :T], scalar1=1.0 / 512, scalar2=RMS_EPS,
                                                                     op0=ALU.mult, op1=ALU.add), reads=[pbs], writes=[rb_])
                kb.op("act", lambda e, T=T: e.activation(out=R[:, :T], in_=R[:, :T], func=AF.Sqrt), reads=[rb_], writes=[rb_])
                kb.op("dve", lambda e, T=T: e.reciprocal(out=R[:, :T], in_=R[:, :T]), reads=[rb_], writes=[rb_])
                for c in range(4):
                    ob, O = on[oi % 3]
                    oi += 1
                    kb.op("dve", lambda e, O=O, c=c, T=T, goff=goff: e.scalar_tensor_tensor(
                        out=O[:, :T], in0=C[:, c * 512:c * 512 + T], scalar=GQ[:, goff + c:goff + c + 1], in1=R[:, :T],
                        op0=ALU.mult, op1=ALU.mult), reads=[cb_, gqb, rb_], writes=[ob])
                    kb.dma("sp", dst[c * 128:(c + 1) * 128, t0:t0 + T], O[:, :T], reads=[ob])
            wb, W = ws.load([(w_dkv[j][:, 512:640], 0, 128)], 16, 128)
            pa, psa = kb.ps()
            pb, psb = kb.ps()
            for (pp, pps, off) in ((pa, psa, 0), (pb, psb, 64)):
                for kc in range(16):
                    kb.op("pe", lambda e, pps=pps, W=W, X=X, kc=kc, off=off, T=T: e.matmul(
                        pps[0:64, :T], W[:, kc * 128 + off:kc * 128 + off + 64], X[:, kc * 512:kc * 512 + T],
                        start=(kc == 0), stop=(kc == 15)), reads=[wb, xb], writes=[pp])
            ob, O = on[oi % 3]
            oi += 1
            if lat:
                (b1, A1), (b2, A2) = tm
                kb.op("dve", lambda e, A1=A1, psa=psa, T=T: e.tensor_tensor(out=A1[0:64, :T], in0=psa[0:64, :T], in1=MT[0:64, 0:T], op=ALU.mult),
                      reads=[pa, mtb], writes=[b1])
                kb.op("dve", lambda e, A2=A2, psb=psb, T=T: e.tensor_tensor(out=A2[0:64, :T], in0=psb[0:64, :T], in1=MT[0:64, 512:512 + T], op=ALU.mult),
                      reads=[pb, mtb], writes=[b2])
                kb.op("dve", lambda e, A1=A1, A2=A2, O=O, T=T: e.tensor_tensor(out=O[0:64, :T], in0=A1[0:64, :T], in1=A2[0:64, :T], op=ALU.add),
                      reads=[b1, b2], writes=[ob])
            else:
                kb.op("act", lambda e, O=O, psa=psa, T=T: e.activation(out=O[0:64, :T], in_=psa[0:64, :T], func=AF.Copy),
                      reads=[pa], writes=[ob])
            kb.dma("sp", KRT[:, t0:t0 + T], O[0:64, :T], reads=[ob])
        kb.barrier()

    def phase_mla_up(j, qtiles):
        ws = WStream(3, 4 * 512)
        xq = [kb.alloc16(4 * 512, "xq%d" % i) for i in range(2)]
        xk = [kb.alloc16(4 * 512, "xk%d" % i) for i in range(2)]
        mtb, MT = kb.alloc32(2 * 512, "mtab")
        on = [kb.alloc16(512, "on%d" % i) for i in range(4)]
        tm = [kb.alloc32(512, "tm%d" % i) for i in range(2)]
        oi = 0
        for ti, (t0, T) in enumerate(TILES):
            lat = t0 < NLAT
            doq = (t0, T) in qtiles
            xqb, XQ = xq[ti % 2]
            xkb, XK = xk[ti % 2]
            if doq:
                load_x((xqb, XQ), CQN, t0, T, 4)
            load_x((xkb, XK), CKVN, t0, T, 4)
            if lat:
                kb.dma("sp", MT.rearrange("p (a t) -> p a t", t=512)[0:64], mtab[:, :, t0:t0 + T].rearrange("a p t -> p a t"),
                       writes=[mtb])
            if doq:
                for hb2 in range(8):
                    wb, W = ws.load([(w_uq[j][:, hb2 * 512:(hb2 + 1) * 512], 0, 512)], 4, 512)
                    for hh in range(2):
                        h = hb2 * 2 + hh
                        pn, psn = kb.ps()
                        pa, psa = kb.ps()
                        pb, psb = kb.ps()
                        for (pp, pps, off, M) in ((pn, psn, hh * 256, 128), (pa, psa, hh * 256 + 128, 64), (pb, psb, hh * 256 + 192, 64)):
                            for kc in range(4):
                                kb.op("pe", lambda e, pps=pps, W=W, XQ=XQ, kc=kc, off=off, M=M, T=T: e.matmul(
                                    pps[0:M, :T], W[:, kc * 512 + off:kc * 512 + off + M], XQ[:, kc * 512:kc * 512 + T],
                                    start=(kc == 0), stop=(kc == 3)), reads=[wb, xqb], writes=[pp])
                        ob, O = on[oi % 4]
                        oi += 1
                        kb.op("act", lambda e, O=O, psn=psn, T=T: e.activation(out=O[:, :T], in_=psn[:, :T], func=AF.Copy),
                              reads=[pn], writes=[ob])
                        kb.dma("sp", QN[h * 128:(h + 1) * 128, t0:t0 + T], O[:, :T], reads=[ob])
                        ob, O = on[oi % 4]
                        oi += 1
                        if lat:
                            (b1, A1), (b2, A2) = tm
                            kb.op("dve", lambda e, A1=A1, psa=psa, T=T: e.tensor_tensor(out=A1[0:64, :T], in0=psa[0:64, :T], in1=MT[0:64, 0:T], op=ALU.mult),
                                  reads=[pa, mtb], writes=[b1])
                            kb.op("dve", lambda e, A2=A2, psb=psb, T=T: e.tensor_tensor(out=A2[0:64, :T], in0=psb[0:64, :T], in1=MT[0:64, 512:512 + T], op=ALU.mult),
                                  reads=[pb, mtb], writes=[b2])
                            kb.op("dve", lambda e, A1=A1, A2=A2, O=O, T=T: e.tensor_tensor(out=O[0:64, :T], in0=A1[0:64, :T], in1=A2[0:64, :T], op=ALU.add),
                                  reads=[b1, b2], writes=[ob])
                        else:
                            kb.op("act", lambda e, O=O, psa=psa, T=T: e.activation(out=O[0:64, :T], in_=psa[0:64, :T], func=AF.Copy),
                                  reads=[pa], writes=[ob])
                        kb.dma("sp", QR[h * 64:(h + 1) * 64, t0:t0 + T], O[0:64, :T], reads=[ob])
            for blk in range(4):
                wb, W = ws.load([(w_ukv_k[j][:, blk * 512:(blk + 1) * 512], 0, 512)], 4, 512)
                for mm in range(4):
                    h = blk * 4 + mm
                    pp, pps = kb.ps()
                    for kc in range(4):
                        kb.op("pe", lambda e, pps=pps, W=W, XK=XK, kc=kc, mm=mm, T=T: e.matmul(
                            pps[:, :T], W[:, kc * 512 + mm * 128:kc * 512 + mm * 128 + 128], XK[:, kc * 512:kc * 512 + T],
                            start=(kc == 0), stop=(kc == 3)), reads=[wb, xkb], writes=[pp])
                    ob, O = on[oi % 4]
                    oi += 1
                    kb.op("act", lambda e, O=O, pps=pps, T=T: e.activation(out=O[:, :T], in_=pps[:, :T], func=AF.Copy),
                          reads=[pp], writes=[ob])
                    kb.dma("sp", KN[h * 128:(h + 1) * 128, t0:t0 + T], O[:, :T], reads=[ob])
            for blk in range(4):
                wb, W = ws.load([(w_ukv_v[j][:, blk * 512:(blk + 1) * 512], 0, 512)], 4, 512)
                for tq in range(T // 128):
                    pp, pps = kb.ps()
                    for kc in range(4):
                        kb.op("pe", lambda e, pps=pps, W=W, XK=XK, kc=kc, tq=tq: e.matmul(
                            pps, XK[:, kc * 512 + tq * 128:kc * 512 + tq * 128 + 128], W[:, kc * 512:(kc + 1) * 512],
                            start=(kc == 0), stop=(kc == 3)), reads=[wb, xkb], writes=[pp])
                    ob, O = on[oi % 4]
                    oi += 1
                    kb.op("act", lambda e, O=O, pps=pps: e.activation(out=O, in_=pps, func=AF.Copy), reads=[pp], writes=[ob])
                    r0 = t0 + tq * 128
                    kb.dma("sp", VA[r0:r0 + 128, blk * 512:(blk + 1) * 512], O, reads=[ob])
        kb.barrier()

    def phase_mla_attn(qtiles):
        krb, KR = kb.alloc16(TOK, "KR")
        kb.dma("sp", KR[0:64, :], KRT, writes=[krb])
        kn = [kb.alloc16(TOK, "kn%d" % i) for i in range(2)]
        vh = [kb.alloc16(34 * 130, "vh%d" % i) for i in range(2)]
        for (vb, VH) in vh:
            kb.op("pool", lambda e, VH=VH: e.memset(VH, 1.0), writes=[vb])
        qn = [kb.alloc16(512, "qn%d" % i) for i in range(2)]
        qr = [kb.alloc16(512, "qr%d" % i) for i in range(2)]
        pt = [kb.alloc16(512, "pt%d" % i) for i in range(3)]
        rcb, RC = kb.alloc32(4, "rc")
        otm = [kb.alloc16(128, "otm%d" % i) for i in range(2)]
        ott = [kb.alloc16(512, "ott%d" % i) for i in range(2)]
        qi_ = 0
        pi = 0
        for h in range(16):
            knb, KNh = kn[h % 2]
            vb, VH = vh[h % 2]
            kb.dma("sp", KNh, KN[h * 128:(h + 1) * 128, :], writes=[knb])
            kb.dma("sp", VH.rearrange("p (c d) -> p c d", d=130)[:, :, 0:128],
                   VA[:, h * 128:(h + 1) * 128].rearrange("(c p) d -> p c d", p=128), writes=[vb])
            for (t0, T) in qtiles:
                keys = list(range(34)) if t0 < NLAT else [32, 33]
                qnb, QNt = qn[qi_ % 2]
                qrb, QRt = qr[qi_ % 2]
                qi_ += 1
                kb.dma("sp", QNt[:, :T], QN[h * 128:(h + 1) * 128, t0:t0 + T], writes=[qnb])
                kb.dma("sp", QRt[0:64, :T], QR[h * 64:(h + 1) * 64, t0:t0 + T], writes=[qrb])
                nq = T // 128
                pob = [(kb.pbufs[i], kb.psum[:, i, :]) for i in range((nq + 1) // 2)]
                for idx, kc in enumerate(keys):
                    si = 3 + (pi % 5)
                    pbs, pss = kb.pbufs[si], kb.psum[:, si, :]
                    kb.op("pe", lambda e, pss=pss, KNh=KNh, QNt=QNt, kc=kc, T=T: e.matmul(
                        pss[:, :T], KNh[:, kc * 128:(kc + 1) * 128], QNt[:, :T], start=True, stop=False),
                        reads=[knb, qnb], writes=[pbs])
                    kb.op("pe", lambda e, pss=pss, QRt=QRt, kc=kc, T=T: e.matmul(
                        pss[:, :T], KR[0:64, kc * 128:(kc + 1) * 128], QRt[0:64, :T], start=False, stop=True),
                        reads=[krb, qrb], writes=[pbs])
                    ptb, PT = pt[pi % 3]
                    pi += 1
                    kb.op("act", lambda e, PT=PT, pss=pss, T=T: e.activation(out=PT[:, :T], in_=pss[:, :T], func=AF.Exp, scale=MLA_SCALE),
                          reads=[pbs], writes=[ptb])
                    for qi in range(nq):
                        pbo, pso = pob[qi // 2]
                        kb.op("pe", lambda e, pso=pso, PT=PT, VH=VH, qi=qi, kc=kc, idx=idx, keys=keys: e.matmul(
                            pso[:, (qi % 2) * 129:(qi % 2) * 129 + 129], PT[:, qi * 128:(qi + 1) * 128],
                            VH[:, kc * 130:kc * 130 + 129], start=(idx == 0), stop=(idx == len(keys) - 1)),
                            reads=[ptb, vb], writes=[pbo])
                otb, OTt = ott[qi_ % 2]
                pbt, pst = kb.pbufs[2], kb.psum[:, 2, :]
                psT = pst.bitcast(BF16)
                for qi in range(nq):
                    pbo, pso = pob[qi // 2]
                    o0 = (qi % 2) * 129
                    kb.op("dve", lambda e, pso=pso, o0=o0, qi=qi: e.reciprocal(out=RC[:, qi:qi + 1], in_=pso[:, o0 + 128:o0 + 129]),
                          reads=[pbo], writes=[rcb])
                    ob, OM = otm[qi % 2]
                    kb.op("act", lambda e, OM=OM, pso=pso, o0=o0, qi=qi: e.activation(
                        out=OM, in_=pso[:, o0:o0 + 128], func=AF.Copy, scale=RC[:, qi:qi + 1]), reads=[pbo, rcb], writes=[ob])
                    kb.op("pe", lambda e, psT=psT, OM=OM, qi=qi: e.transpose(psT[:, qi * 128:(qi + 1) * 128], OM, ident),
                          reads=[ob, id_b], writes=[pbt])
                kb.op("dve", lambda e, OTt=OTt, psT=psT, T=T: e.tensor_copy(out=OTt[:, :T], in_=psT[:, :T]), reads=[pbt], writes=[otb])
                kb.dma("sp", OT[h * 128:(h + 1) * 128, t0:t0 + T], OTt[:, :T], reads=[otb])
        kb.barrier()

    phase_mod0()
    for l in range(nlayers):
        j = l // 2
        ctx_out = l < DEPTH - 1
        tiles = TILES if ctx_out else TILES[:8]
        if l % 2 == 0:
            phase_ret_proj(j)
            phase_ret_scan(j)
            phase_out_ln(l, 0, YT, 32, ret_w_o[j], tiles, True)
        else:
            phase_mla_down(j)
            phase_mla_up(j, tiles)
            phase_mla_attn(tiles)
            phase_out_ln(l, 0, OT, 16, mla_w_o[j], tiles, True)
        phase_ffn_in(l, tiles)
        last = (l == DEPTH - 1)
        phase_out_ln(l, 1, HID, 44, ffn_w_out[l], tiles, not last, final=last)
    if nlayers < DEPTH:
        db_ = Buf("dump")
        kb.bufs.append(db_)
        kb.dma("sp", outT, H[:, 0:NLAT], sb=db_)
    for name, (src, dst) in dbg_out.items():
        b = Buf("dbg" + name)
        kb.bufs.append(b)
        kb.dma("sp", dst, src, sb=b)
    fin = []
    seen = set()
    for b in kb.bufs:
        if b.sem is not None and id(b.sem) not in seen:
            seen.add(id(b.sem))
            fin.append(("d", b.sem, kb.sem_cnt[id(b.sem)]))
    kb.ins["sp"].append(dict(fn=None, deps=fin))
    kb.emit()
    stack.close()
    return nc


def _prep_inputs(x, c, ctx, c_ctx, ada_w, ada_b, ln_g, ln_b, ret_w_qkv, ret_w_g, ret_decay_logit, ret_w_o,
                 mla_w_dq, mla_g_q, mla_w_uq, mla_w_dkv, mla_g_kv, mla_w_ukv, mla_w_o, ffn_w_in, ffn_w_out):
    f = np.float32
    A = lambda a: np.ascontiguousarray(np.asarray(a, dtype=f))
    t = np.arange(NLAT, dtype=f)
    ret_inv = (f(10000.0) ** (-np.linspace(0.0, 1.0, 128, dtype=f))).astype(f)
    ang = (t[:, None] * ret_inv[None, :]).astype(f)
    cs, sn = np.cos(ang).astype(f).T, np.sin(ang).astype(f).T
    rtab = A(np.stack([cs, sn, cs * f(1.0 / 16), sn * f(1.0 / 16)]))
    rows = (np.arange(NLAT) // 64).astype(f)
    cols = (np.arange(NLAT) % 64).astype(f)
    ax_inv = (f(10000.0) ** (-np.arange(8, dtype=f) * f(2.0) / f(16))).astype(f)
    ax_inv = (f(10000.0) ** (-np.arange(16, dtype=f) * f(2.0) / f(32))).astype(f)
    ra = (rows[:, None] * ax_inv[None, :]).astype(f)
    ca = (cols[:, None] * ax_inv[None, :]).astype(f)
    rc, rs_, cc, cs_ = np.cos(ra).T, np.sin(ra).T, np.cos(ca).T, np.sin(ca).T
    mtab = A(np.stack([np.concatenate([rc, cc, rc, cc], 0), np.concatenate([-rs_, -cs_, rs_, cs_], 0)]))
    ident = np.eye(128, dtype=f)
    fm16 = lambda v: A(np.asarray(v, dtype=f).reshape(16, 128).T)
    common = dict(
        ada_w=A(ada_w),
        ada_bT=A(np.asarray(ada_b, dtype=f).reshape(DEPTH, 96, 128).transpose(0, 2, 1)),
        ln_gT=A(np.asarray(ln_g, dtype=f).reshape(DEPTH, 2, 16, 128).transpose(0, 1, 3, 2)),
        ln_bT=A(np.asarray(ln_b, dtype=f).reshape(DEPTH, 2, 16, 128).transpose(0, 1, 3, 2)),
        ret_w_qkv=A(ret_w_qkv), ret_w_g=A(ret_w_g), ret_w_o=A(ret_w_o),
        dlog=A(np.broadcast_to(np.asarray(ret_decay_logit, dtype=f).reshape(2, 1, 16), (2, 128, 16))),
        w_dq=A(mla_w_dq), mla_w_o=A(mla_w_o), ffn_w_in=A(ffn_w_in), ffn_w_out=A(ffn_w_out),
        g_qT=A(np.asarray(mla_g_q, dtype=f).reshape(2, 4, 128).transpose(0, 2, 1)),
        g_kvT=A(np.asarray(mla_g_kv, dtype=f).reshape(2, 4, 128).transpose(0, 2, 1)),
        rtab=rtab, mtab=mtab, ident=ident,
    )
    pa = np.concatenate([np.arange(0, 16), np.arange(32, 48), np.arange(16, 32), np.arange(48, 64)])
    pb = np.concatenate([np.arange(16, 32), np.arange(48, 64), np.arange(0, 16), np.arange(32, 48)])
    wdkv = np.asarray(mla_w_dkv, dtype=f)
    common["w_dkv"] = A(np.concatenate([wdkv[:, :, :512], wdkv[:, :, 512 + pa], wdkv[:, :, 512 + pb]], axis=2))
    wuq = np.asarray(mla_w_uq, dtype=f).reshape(2, 512, 16, 192)
    common["w_uq"] = A(np.concatenate([wuq[..., :128], wuq[..., 128 + pa], wuq[..., 128 + pb]], axis=3).reshape(2, 512, 16 * 256))
    wukv = np.asarray(mla_w_ukv, dtype=f).reshape(2, 512, 16, 256)
    common["w_ukv_k"] = A(wukv[..., :128].reshape(2, 512, 2048))
    common["w_ukv_v"] = A(wukv[..., 128:].reshape(2, 512, 2048))
    maps = []
    x = np.asarray(x, dtype=f)
    ctx = np.asarray(ctx, dtype=f)
    c = np.asarray(c, dtype=f)
    c_ctx = np.asarray(c_ctx, dtype=f)
    for core in range(8):
        b = core % 4
        m = dict(common)
        m["xT"] = A(np.concatenate([x[b].T, ctx[b].T], axis=1))
        cond = np.stack([fm16(c[b]), fm16(c_ctx)], axis=2)
        m["condT"] = A(cond.reshape(128, 32))
        maps.append(m)
    return maps


def kernel(**inputs):
    maps = _prep_inputs(**inputs)
    nc = build_program()
    res = run_bass_kernel_spmd(nc, maps, core_ids=list(range(8)))
    out = np.stack([np.ascontiguousarray(res.results[b]["outT"].T) for b in range(4)], axis=0)
    return out.astype(np.float32)
```

```python
import contextlib
import numpy as np
import concourse.bass as bass
import concourse.mybir as mybir
from concourse.bass_utils import run_bass_kernel_spmd

F32 = mybir.dt.float32
BF16 = mybir.dt.bfloat16
I32 = mybir.dt.int32
AF = mybir.ActivationFunctionType
ALU = mybir.AluOpType

D = 2048
NLAT = 4096
NCTX = 256
TOK = NLAT + NCTX
DEPTH = 4
FH = 5632
ALPHA = (2 * DEPTH) ** 0.25
LN_EPS_EFF = 1e-5 / (ALPHA * ALPHA)
GN_EPS = 1e-6
RMS_EPS = 1e-6
MLA_SCALE = (128 + 64) ** -0.5
TILES = [(i * 512, 512) for i in range(8)] + [(NLAT, NCTX)]
ENG = ["pe", "act", "dve", "pool", "sp"]
BLK = {"pe": "tensor", "act": "scalar", "dve": "vector", "pool": "gpsimd", "sp": "sync"}
ARENA_BYTES = 200 * 1024


class Buf:
    __slots__ = ("name", "lw", "rd", "sem", "cnt", "persist")

    def __init__(self, name, persist=False):
        self.name = name
        self.lw = None
        self.rd = []
        self.sem = None
        self.cnt = 0
        self.persist = persist


class KB:
    def __init__(self, nc, stack):
        self.nc = nc
        self.stack = stack
        self.ins = {e: [] for e in ENG}
        self.esem = {e: stack.enter_context(nc.semaphore("es_" + e)) for e in ENG}
        self.bufs = []
        self.free_sems = []
        self.sem_cnt = {}
        self.nsem = 0
        self.arena = stack.enter_context(nc.sbuf_tensor("arena", [128, ARENA_BYTES // 2], BF16))
        self.psum = stack.enter_context(nc.psum_tensor("psum", [128, 8, 512], F32))
        self.pbufs = [Buf("ps%d" % i, True) for i in range(8)]
        self.bufs += self.pbufs
        self.pidx = 0
        self.base = 0
        self.ptr = 0

    def alloc(self, nbytes, name="t", persist=False):
        req = nbytes
        nbytes = (nbytes + 63) // 64 * 64
        off = self.ptr
        self.ptr += nbytes
        assert self.ptr <= ARENA_BYTES, ("arena overflow", name, self.ptr)
        b = Buf(name, persist)
        self.bufs.append(b)
        return b, self.arena[:, off // 2:(off + req) // 2]

    def alloc32(self, ncols, name="t", persist=False):
        b, ap = self.alloc(ncols * 4, name, persist)
        return b, ap.bitcast(F32)

    def alloc16(self, ncols, name="t", persist=False):
        return self.alloc(ncols * 2, name, persist)

    def ps(self):
        i = self.pidx
        self.pidx = (self.pidx + 1) % 8
        return self.pbufs[i], self.psum[:, i, :]

    def end_persist(self):
        self.base = self.ptr

    def _deps(self, eng, reads, writes):
        deps = []
        for b in reads:
            if b.lw is not None:
                deps.append(b.lw)
        for b in writes:
            if b.lw is not None:
                deps.append(b.lw)
            deps.extend(b.rd)
        if eng == "pe":
            deps = [d for d in deps if not (d[0] == "e" and d[1] == "pe")]
        return deps

    def op(self, eng, fn, reads=(), writes=()):
        idx = len(self.ins[eng])
        self.ins[eng].append(dict(fn=fn, deps=self._deps(eng, reads, writes)))
        ev = ("e", eng, idx)
        for b in reads:
            b.rd.append(ev)
        for b in writes:
            b.lw = ev
            b.rd = []

    def dma(self, q, out, in_, reads=(), writes=(), sb=None):
        own = sb or (writes[0] if writes else reads[0])
        if own.sem is None:
            if self.free_sems:
                own.sem = self.free_sems.pop()
            else:
                own.sem = self.stack.enter_context(self.nc.semaphore("ds%d" % self.nsem))
                self.nsem += 1
                self.sem_cnt[id(own.sem)] = 0
        self.sem_cnt[id(own.sem)] += 16
        val = self.sem_cnt[id(own.sem)]
        self.ins[q].append(dict(fn=lambda e: e.dma_start(out=out, in_=in_), deps=self._deps(q, reads, writes),
                                dsem=own.sem))
        ev = ("d", own.sem, val)
        for b in reads:
            b.rd.append(ev)
        for b in writes:
            b.lw = ev
            b.rd = []
        if own not in reads and own not in writes:
            own.rd.append(ev)

    def barrier(self):
        evs = []
        for e in ENG:
            for i in range(len(self.ins[e]) - 1, -1, -1):
                r = self.ins[e][i]
                if r["fn"] is not None and "dsem" not in r:
                    evs.append(("e", e, i))
                    break
        seen = set()
        for b in self.bufs:
            if b.sem is not None and id(b.sem) not in seen:
                seen.add(id(b.sem))
                evs.append(("d", b.sem, self.sem_cnt[id(b.sem)]))
        for e in ENG:
            self.ins[e].append(dict(fn=None, deps=list(evs)))
        keep = []
        for b in self.bufs:
            b.lw = None
            b.rd = []
            if b.sem is not None:
                self.free_sems.append(b.sem)
                b.sem = None
            if b.persist:
                keep.append(b)
        self.bufs = keep
        self.ptr = self.base

    def emit(self):
        ins = self.ins
        for e in ENG:
            for r in ins[e]:
                for d in r["deps"]:
                    if d[0] == "e":
                        ins[d[1]][d[2]]["ms"] = True
        for e in ENG:
            c = 0
            for r in ins[e]:
                if r.get("ms"):
                    c += 1
                    r["msv"] = c
        with self.nc.Block() as block:
            for e in ENG:
                def body(eng, e=e):
                    known = {}
                    for r in ins[e]:
                        for d in r["deps"]:
                            if d[0] == "e":
                                sem, val = self.esem[d[1]], ins[d[1]][d[2]]["msv"]
                            else:
                                sem, val = d[1], d[2]
                            if known.get(id(sem), 0) >= val:
                                continue
                            eng.wait_ge(sem, val)
                            known[id(sem)] = val
                        if r["fn"] is not None:
                            i = r["fn"](eng)
                            if "dsem" in r:
                                i.then_inc(r["dsem"], 16)
                            elif r.get("ms"):
                                i.then_inc(self.esem[e], 1)
                getattr(block, BLK[e])(body)


def build_program(dbg=None, nlayers=DEPTH):
    nc = bass.Bass("TRN2", target_bir_lowering=False)
    stack = contextlib.ExitStack()

    def din(name, shape):
        return nc.dram_tensor(name, list(shape), F32, kind="ExternalInput").ap()

    def dscr(name, shape, dt):
        return nc.dram_tensor(name, list(shape), dt, kind="Internal").ap()

    xT = din("xT", [D, TOK])
    condT = din("condT", [128, 32])
    ada_w = din("ada_w", [DEPTH, D, 6 * D])
    ada_bT = din("ada_bT", [DEPTH, 128, 96])
    ln_gT = din("ln_gT", [DEPTH, 2, 128, 16])
    ln_bT = din("ln_bT", [DEPTH, 2, 128, 16])
    ret_w_qkv = din("ret_w_qkv", [2, D, 8192])
    ret_w_g = din("ret_w_g", [2, D, 8192])
    ret_w_o = din("ret_w_o", [2, 4096, D])
    dlog = din("dlog", [2, 128, 16])
    w_dq = din("w_dq", [2, D, 512])
    w_dkv = din("w_dkv", [2, D, 640])
    g_qT = din("g_qT", [2, 128, 4])
    g_kvT = din("g_kvT", [2, 128, 4])
    w_uq = din("w_uq", [2, 512, 16 * 256])
    w_ukv_k = din("w_ukv_k", [2, 512, 2048])
    w_ukv_v = din("w_ukv_v", [2, 512, 2048])
    mla_w_o = din("mla_w_o", [2, D, D])
    ffn_w_in = din("ffn_w_in", [DEPTH, D, 2 * FH])
    ffn_w_out = din("ffn_w_out", [DEPTH, FH, D])
    rtab = din("rtab", [4, 128, NLAT])
    mtab = din("mtab", [2, 64, NLAT])
    ident_in = din("ident", [128, 128])
    outT = nc.dram_tensor("outT", [D, NLAT], F32, kind="ExternalOutput").ap()

    H = dscr("H", [D, TOK], F32)
    U = dscr("U", [D, TOK], BF16)
    QT = dscr("QT", [D, TOK], BF16)
    KT = dscr("KT", [D, TOK], BF16)
    V = dscr("V", [TOK, 4096], BF16)
    G = dscr("G", [TOK, 8192], BF16)
    YF = dscr("YF", [TOK, 4096], F32)
    YT = dscr("YT", [4096, TOK], BF16)
    HID = dscr("HID", [FH, TOK], BF16)
    CQN = dscr("CQN", [512, TOK], BF16)
    CKVN = dscr("CKVN", [512, TOK], BF16)
    KRT = dscr("KRT", [64, TOK], BF16)
    QN = dscr("QN", [D, TOK], BF16)
    QR = dscr("QR", [16 * 64, TOK], BF16)
    KN = dscr("KN", [D, TOK], BF16)
    VA = dscr("VA", [TOK, D], BF16)
    OT = dscr("OT", [D, TOK], BF16)
    dbg_out = {}
    if dbg:
        for name in dbg:
            src = {"H": H, "U": U, "QT": QT, "KT": KT, "V": V, "G": G, "YT": YT, "HID": HID, "YF": YF,
                   "OT": OT, "QN": QN, "KN": KN, "VA": VA, "QR": QR, "KRT": KRT, "CQN": CQN, "CKVN": CKVN}[name]
            dbg_out[name] = (src, nc.dram_tensor("dbg_" + name, list(src.shape), src.dtype, kind="ExternalOutput").ap())

    kb = KB(nc, stack)

    ones_b, ones32 = kb.alloc32(128, "ones32", True)
    id_b, ident = kb.alloc16(128, "ident", True)
    mod_b, MOD = kb.alloc32(DEPTH * 96 * 2, "MOD", True)
    kb.end_persist()

    def modidx(l, which, c, w):
        return ((l * 6 + which) * 16 + c) * 2 + w

    def modcol(l, which, w):
        s = modidx(l, which, 0, w)
        return MOD[:, s:s + 31:2]

    kb.op("pool", lambda e: e.memset(ones32, 1.0), writes=[ones_b])
    kb.dma("pool", ident, ident_in, writes=[id_b])
    hb = Buf("hcopy")
    kb.bufs.append(hb)
    kb.dma("sp", H, xT, sb=hb)
    cb, cnd = kb.alloc32(32, "cond")
    csb, cnds = kb.alloc16(32, "conds")
    kb.dma("sp", cnd, condT, writes=[cb])
    kb.op("act", lambda e: e.activation(out=cnds, in_=cnd, func=AF.Silu), reads=[cb], writes=[csb])
    wslots = [kb.alloc16(16 * 512, "ws%d" % i) for i in range(3)]
    abb, adab = kb.alloc32(96, "adab")
    wi = 0
    for l in range(DEPTH):
        pb, ps = kb.ps()
        for blk in range(24):
            sb_, sl = wslots[wi % 3]
            wi += 1
            kb.dma("pool", sl.rearrange("p (k n) -> p k n", n=512),
                   ada_w[l][:, blk * 512:(blk + 1) * 512].rearrange("(k p) n -> p k n", p=128), writes=[sb_])
            for mm in range(4):
                j = blk * 4 + mm
                for kc in range(16):
                    kb.op("pe", lambda e, ps=ps, sl=sl, j=j, kc=kc, mm=mm: e.matmul(
                        ps[:, j * 2:j * 2 + 2], sl[:, kc * 512 + mm * 128:kc * 512 + mm * 128 + 128],
                        cnds[:, kc * 2:kc * 2 + 2], start=(kc == 0), stop=(kc == 15)),
                        reads=[sb_, csb], writes=[pb])
        kb.dma("sp", adab, ada_bT[l], writes=[abb])
        for w in range(2):
            s = modidx(l, 0, 0, w)
            kb.op("dve", lambda e, ps=ps, s=s, w=w: e.tensor_tensor(
                out=MOD[:, s:s + 191:2], in0=ps[:, w:w + 191:2], in1=adab, op=ALU.add),
                reads=[pb, abb], writes=[mod_b])
    kb.barrier()

    def ln_tables(l, which_ln, has_next):
        gb, gt = kb.alloc32(16, "lng")
        bb, bt = kb.alloc32(16, "lnb")
        kb.dma("sp", gt, ln_gT[l, which_ln], writes=[gb])
        kb.dma("sp", bt, ln_bT[l, which_ln], writes=[bb])
        tb_, tt_ = kb.alloc32(16 * 8, "lntab")
        res = {"G": gt, "B": bt, "bufs": [gb, bb, tb_, mod_b]}
        gate_which = 2 if which_ln == 0 else 5
        for w in range(2):
            ga = tt_[:, w * 16:(w + 1) * 16]
            kb.op("dve", lambda e, ga=ga, w=w: e.tensor_scalar(
                out=ga, in0=modcol(l, gate_which, w), scalar1=1.0 / ALPHA, scalar2=None, op0=ALU.mult),
                reads=[mod_b], writes=[tb_])
            res["GA%d" % w] = ga
            if has_next:
                if which_ln == 0:
                    nl, nsc, nsh = l, 4, 3
                else:
                    nl, nsc, nsh = l + 1, 1, 0
                a1 = tt_[:, 32 + w * 16:32 + (w + 1) * 16]
                ug = tt_[:, 64 + w * 16:64 + (w + 1) * 16]
                ub = tt_[:, 96 + w * 16:96 + (w + 1) * 16]
                kb.op("dve", lambda e, a1=a1, w=w: e.tensor_scalar(
                    out=a1, in0=modcol(nl, nsc, w), scalar1=1.0, scalar2=None, op0=ALU.add),
                    reads=[mod_b], writes=[tb_])
                kb.op("dve", lambda e, a1=a1, ug=ug: e.tensor_tensor(out=ug, in0=a1, in1=gt, op=ALU.mult),
                      reads=[tb_, gb], writes=[tb_])
                kb.op("dve", lambda e, a1=a1, ub=ub: e.tensor_tensor(out=ub, in0=a1, in1=bt, op=ALU.mult),
                      reads=[tb_, bb], writes=[tb_])
                kb.op("dve", lambda e, ub=ub, w=w: e.tensor_tensor(
                    out=ub, in0=ub, in1=modcol(nl, nsh, w), op=ALU.add), reads=[tb_, mod_b], writes=[tb_])
                res["UG%d" % w] = ug
                res["UB%d" % w] = ub
        return res

    class WStream:
        def __init__(self, nslots=3, slot_cols=16 * 512):
            self.slots = [kb.alloc16(slot_cols, "ws%d" % i) for i in range(nslots)]
            self.i = 0

        def load(self, pieces, kc, n):
            b, sl = self.slots[self.i % len(self.slots)]
            self.i += 1
            v = sl[:, :kc * n].rearrange("p (k n) -> p k n", n=n)
            first = True
            for (src, off, ncol) in pieces:
                kb.dma("pool", v[:, :, off:off + ncol], src.rearrange("(k p) n -> p k n", p=128), writes=[b])
            return b, sl

    def load_x(slot, src, t0, T, kc):
        b, ap = slot
        kb.dma("sp", ap[:, :kc * 512].rearrange("p (k t) -> p k t", t=512)[:, :, :T],
               src[:, t0:t0 + T].rearrange("(k p) t -> p k t", p=128), writes=[b])

    def phase_mod0():
        slots = [kb.alloc32(16 * 512, "hin%d" % i) for i in range(2)]
        outs = [kb.alloc16(16 * 512, "uo%d" % i) for i in range(2)]
        a1b, a1 = kb.alloc32(32, "a1")
        for w in range(2):
            kb.op("dve", lambda e, w=w: e.tensor_scalar(out=a1[:, w * 16:(w + 1) * 16], in0=modcol(0, 1, w),
                                                        scalar1=1.0, scalar2=None, op0=ALU.add),
                  reads=[mod_b], writes=[a1b])
        for ti, (t0, T) in enumerate(TILES):
            w = 0 if t0 < NLAT else 1
            hb_, hin = slots[ti % 2]
            ob_, uo = outs[ti % 2]
            kb.dma("sp", hin.rearrange("p (k t) -> p k t", t=512)[:, :, :T],
                   xT[:, t0:t0 + T].rearrange("(k p) t -> p k t", p=128), writes=[hb_])
            for c in range(16):
                s = modidx(0, 0, c, w)
                kb.op("dve", lambda e, hin=hin, uo=uo, c=c, w=w, s=s, T=T: e.tensor_scalar(
                    out=uo[:, c * 512:c * 512 + T], in0=hin[:, c * 512:c * 512 + T],
                    scalar1=a1[:, w * 16 + c:w * 16 + c + 1], scalar2=MOD[:, s:s + 1], op0=ALU.mult, op1=ALU.add),
                    reads=[hb_, a1b, mod_b], writes=[ob_])
            kb.dma("sp", U[:, t0:t0 + T].rearrange("(k p) t -> p k t", p=128),
                   uo.rearrange("p (k t) -> p k t", t=512)[:, :, :T], reads=[ob_])
        kb.barrier()

    def phase_out_ln(l, which_ln, Xsrc, KC, Wsrc, tiles, has_next, final=False):
        tabs = ln_tables(l, which_ln, has_next)
        tb = tabs["bufs"]
        nb = 256 if KC <= 16 else 128
        ws = WStream(3, KC * nb)
        nx = 2 if KC <= 16 else 1
        xs = [kb.alloc16(KC * 512, "x%d" % i) for i in range(nx)]
        zb, Z = kb.alloc32(16 * 512, "Z")
        sq = [kb.alloc32(512, "sq%d" % i) for i in range(2)]
        mb_, MEAN = kb.alloc32(512, "mean")
        m2b, M2 = kb.alloc32(512, "m2")
        rb_, RSTD = kb.alloc32(512, "rstd")
        t1 = [kb.alloc32(512, "t1%d" % i) for i in range(2)]
        ho = [kb.alloc32(512, "ho%d" % i) for i in range(2)]
        uo = [kb.alloc16(512, "uo%d" % i) for i in range(2)]
        load_x(xs[0], Xsrc, tiles[0][0], tiles[0][1], KC)
        for ti, (t0, T) in enumerate(tiles):
            w = 0 if t0 < NLAT else 1
            xb, X = xs[ti % nx]
            if nx == 2 and ti + 1 < len(tiles):
                load_x(xs[(ti + 1) % 2], Xsrc, tiles[ti + 1][0], tiles[ti + 1][1], KC)
            kb.dma("sp", Z.rearrange("p (k t) -> p k t", t=512)[:, :, :T],
                   H[:, t0:t0 + T].rearrange("(k p) t -> p k t", p=128), writes=[zb])
            for blk in range(D // nb):
                wb, W = ws.load([(Wsrc[:, blk * nb:(blk + 1) * nb], 0, nb)], KC, nb)
                for mm in range(nb // 128):
                    c = blk * (nb // 128) + mm
                    pb, ps = kb.ps()
                    for kc in range(KC):
                        kb.op("pe", lambda e, ps=ps, W=W, X=X, kc=kc, mm=mm, T=T: e.matmul(
                            ps[:, :T], W[:, kc * nb + mm * 128:kc * nb + mm * 128 + 128],
                            X[:, kc * 512:kc * 512 + T], start=(kc == 0), stop=(kc == KC - 1)),
                            reads=[wb, xb], writes=[pb])
                    ga = tabs["GA%d" % w]
                    kb.op("dve", lambda e, ps=ps, c=c, T=T, ga=ga: e.scalar_tensor_tensor(
                        out=Z[:, c * 512:c * 512 + T], in0=ps[:, :T], scalar=ga[:, c:c + 1],
                        in1=Z[:, c * 512:c * 512 + T], op0=ALU.mult, op1=ALU.add),
                        reads=[pb] + tb, writes=[zb])
            if nx == 1 and ti + 1 < len(tiles):
                load_x(xs[0], Xsrc, tiles[ti + 1][0], tiles[ti + 1][1], KC)
            pb1, ps1 = kb.ps()
            pb2, ps2 = kb.ps()
            for c in range(16):
                kb.op("pe", lambda e, ps1=ps1, c=c, T=T: e.matmul(
                    ps1[:, :T], ones32, Z[:, c * 512:c * 512 + T], start=(c == 0), stop=(c == 15)),
                    reads=[ones_b, zb], writes=[pb1])
                qb, q = sq[c % 2]
                kb.op("act", lambda e, q=q, c=c, T=T: e.activation(
                    out=q[:, :T], in_=Z[:, c * 512:c * 512 + T], func=AF.Square), reads=[zb], writes=[qb])
                kb.op("pe", lambda e, ps2=ps2, q=q, c=c, T=T: e.matmul(
                    ps2[:, :T], ones32, q[:, :T], start=(c == 0), stop=(c == 15)),
                    reads=[ones_b, qb], writes=[pb2])
            kb.op("dve", lambda e, ps1=ps1, T=T: e.tensor_scalar(
                out=MEAN[:, :T], in0=ps1[:, :T], scalar1=1.0 / D, scalar2=None, op0=ALU.mult),
                reads=[pb1], writes=[mb_])
            kb.op("dve", lambda e, T=T: e.tensor_tensor(out=M2[:, :T], in0=MEAN[:, :T], in1=MEAN[:, :T], op=ALU.mult),
                  reads=[mb_], writes=[m2b])
            kb.op("dve", lambda e, ps2=ps2, T=T: e.scalar_tensor_tensor(
                out=M2[:, :T], in0=ps2[:, :T], scalar=1.0 / D, in1=M2[:, :T], op0=ALU.mult, op1=ALU.subtract),
                reads=[pb2, m2b], writes=[m2b])
            kb.op("dve", lambda e, T=T: e.tensor_scalar(
                out=M2[:, :T], in0=M2[:, :T], scalar1=LN_EPS_EFF, scalar2=None, op0=ALU.add),
                reads=[m2b], writes=[m2b])
            kb.op("act", lambda e, T=T: e.activation(out=RSTD[:, :T], in_=M2[:, :T], func=AF.Sqrt),
                  reads=[m2b], writes=[rb_])
            kb.op("dve", lambda e, T=T: e.reciprocal(out=RSTD[:, :T], in_=RSTD[:, :T]), reads=[rb_], writes=[rb_])
            for c in range(16):
                tb1, T1 = t1[c % 2]
                hb1, HO = ho[c % 2]
                ub1, UO = uo[c % 2]
                kb.op("dve", lambda e, T1=T1, c=c, T=T: e.tensor_tensor(
                    out=T1[:, :T], in0=Z[:, c * 512:c * 512 + T], in1=MEAN[:, :T], op=ALU.subtract),
                    reads=[zb, mb_], writes=[tb1])
                kb.op("dve", lambda e, T1=T1, T=T: e.tensor_tensor(
                    out=T1[:, :T], in0=T1[:, :T], in1=RSTD[:, :T], op=ALU.mult), reads=[tb1, rb_], writes=[tb1])
                kb.op("act", lambda e, T1=T1, HO=HO, c=c, T=T: e.activation(
                    out=HO[:, :T], in_=T1[:, :T], func=AF.Identity, scale=tabs["G"][:, c:c + 1],
                    bias=tabs["B"][:, c:c + 1]), reads=[tb1] + tb, writes=[hb1])
                if final:
                    kb.dma("sp", outT[c * 128:(c + 1) * 128, t0:t0 + T], HO[:, :T], reads=[hb1])
                else:
                    kb.dma("sp", H[c * 128:(c + 1) * 128, t0:t0 + T], HO[:, :T], reads=[hb1])
                if has_next:
                    ug, ub = tabs["UG%d" % w], tabs["UB%d" % w]
                    kb.op("act", lambda e, T1=T1, UO=UO, c=c, T=T, ug=ug, ub=ub: e.activation(
                        out=UO[:, :T], in_=T1[:, :T], func=AF.Identity, scale=ug[:, c:c + 1],
                        bias=ub[:, c:c + 1]), reads=[tb1] + tb, writes=[ub1])
                    kb.dma("sp", U[c * 128:(c + 1) * 128, t0:t0 + T], UO[:, :T], reads=[ub1])
        kb.barrier()

    def phase_ffn_in(l, tiles):
        ws = WStream(3, 16 * 512)
        xs = [kb.alloc16(16 * 512, "x%d" % i) for i in range(4)]
        sa = [kb.alloc32(512, "sa%d" % i) for i in range(2)]
        hd = [kb.alloc16(512, "hd%d" % i) for i in range(3)]
        Wsrc = ffn_w_in[l]
        groups = [tiles[i:i + 2] for i in range(0, len(tiles), 2)]
        for k, (t0, T) in enumerate(groups[0]):
            load_x(xs[k], U, t0, T, 16)
        hi = 0
        si = 0
        for gi, grp in enumerate(groups):
            if gi + 1 < len(groups):
                for k, (t0, T) in enumerate(groups[gi + 1]):
                    load_x(xs[((gi + 1) % 2) * 2 + k], U, t0, T, 16)
            for blk in range(22):
                wb, W = ws.load([(Wsrc[:, blk * 256:(blk + 1) * 256], 0, 256),
                                 (Wsrc[:, FH + blk * 256:FH + (blk + 1) * 256], 256, 256)], 16, 512)
                for k, (t0, T) in enumerate(grp):
                    xb, X = xs[(gi % 2) * 2 + k]
                    for jj in range(2):
                        j = blk * 2 + jj
                        pa, psa = kb.ps()
                        pb, psb = kb.ps()
                        for (pp, pps, off) in ((pa, psa, jj * 128), (pb, psb, 256 + jj * 128)):
                            for kc in range(16):
                                kb.op("pe", lambda e, pps=pps, W=W, X=X, kc=kc, off=off, T=T: e.matmul(
                                    pps[:, :T], W[:, kc * 512 + off:kc * 512 + off + 128],
                                    X[:, kc * 512:kc * 512 + T], start=(kc == 0), stop=(kc == 15)),
                                    reads=[wb, xb], writes=[pp])
                        sb_, SA = sa[si % 2]
                        si += 1
                        hb_, HD = hd[hi % 3]
                        hi += 1
                        kb.op("act", lambda e, SA=SA, psa=psa, T=T: e.activation(out=SA[:, :T], in_=psa[:, :T], func=AF.Silu),
                              reads=[pa], writes=[sb_])
                        kb.op("dve", lambda e, SA=SA, HD=HD, psb=psb, T=T: e.tensor_tensor(
                            out=HD[:, :T], in0=psb[:, :T], in1=SA[:, :T], op=ALU.mult), reads=[pb, sb_], writes=[hb_])
                        kb.dma("sp", HID[j * 128:(j + 1) * 128, t0:t0 + T], HD[:, :T], reads=[hb_])
        kb.barrier()

    def phase_ret_proj(j):
        ws = WStream(3, 16 * 512)
        xs = [kb.alloc16(16 * 512, "x%d" % i) for i in range(4)]
        tabs_ = [kb.alloc32(4 * 512, "rtab%d" % i) for i in range(2)]
        tm = [kb.alloc32(512, "tm%d" % i) for i in range(4)]
        o16 = [kb.alloc16(512, "o%d" % i) for i in range(4)]
        Wq = ret_w_qkv[j]
        Wg = ret_w_g[j]
        groups = [TILES[i:i + 2] for i in range(0, len(TILES), 2)]
        for k, (t0, T) in enumerate(groups[0]):
            load_x(xs[k], U, t0, T, 16)
        oi = 0
        for gi, grp in enumerate(groups):
            if gi + 1 < len(groups):
                for k, (t0, T) in enumerate(groups[gi + 1]):
                    load_x(xs[((gi + 1) % 2) * 2 + k], U, t0, T, 16)
            for k, (t0, T) in enumerate(grp):
                if t0 < NLAT:
                    tabb, TAB = tabs_[k]
                    kb.dma("sp", TAB.rearrange("p (a t) -> p a t", t=512),
                           rtab[:, :, t0:t0 + T].rearrange("a p t -> p a t"), writes=[tabb])
            for hk in range(16):
                isk = hk >= 8
                dst = KT if isk else QT
                h = hk % 8
                wb, W = ws.load([(Wq[:, hk * 256:(hk + 1) * 256], 0, 256)], 16, 256)
                for k, (t0, T) in enumerate(grp):
                    lat = t0 < NLAT
                    xb, X = xs[(gi % 2) * 2 + k]
                    tabb, TAB = tabs_[k]
                    p1, ps1 = kb.ps()
                    p2, ps2 = kb.ps()
                    for (pp, pps, off) in ((p1, ps1, 0), (p2, ps2, 128)):
                        for kc in range(16):
                            kb.op("pe", lambda e, pps=pps, W=W, X=X, kc=kc, off=off, T=T: e.matmul(
                                pps[:, :T], W[:, kc * 256 + off:kc * 256 + off + 128],
                                X[:, kc * 512:kc * 512 + T], start=(kc == 0), stop=(kc == 15)),
                                reads=[wb, xb], writes=[pp])
                    ob1, O1 = o16[oi % 4]
                    ob2, O2 = o16[(oi + 1) % 4]
                    oi += 2
                    if lat:
                        cs = TAB[:, (2 if isk else 0) * 512:(2 if isk else 0) * 512 + T]
                        sn = TAB[:, (3 if isk else 1) * 512:(3 if isk else 1) * 512 + T]
                        (b1, A1), (b2, A2), (b3, A3), (b4, A4) = tm
                        kb.op("dve", lambda e, A1=A1, ps1=ps1, cs=cs, T=T: e.tensor_tensor(out=A1[:, :T], in0=ps1[:, :T], in1=cs, op=ALU.mult),
                              reads=[p1, tabb], writes=[b1])
                        kb.op("dve", lambda e, A2=A2, ps2=ps2, sn=sn, T=T: e.tensor_tensor(out=A2[:, :T], in0=ps2[:, :T], in1=sn, op=ALU.mult),
                              reads=[p2, tabb], writes=[b2])
                        kb.op("dve", lambda e, A1=A1, A2=A2, O1=O1, T=T: e.tensor_tensor(out=O1[:, :T], in0=A1[:, :T], in1=A2[:, :T], op=ALU.subtract),
                              reads=[b1, b2], writes=[ob1])
                        kb.op("dve", lambda e, A3=A3, ps1=ps1, sn=sn, T=T: e.tensor_tensor(out=A3[:, :T], in0=ps1[:, :T], in1=sn, op=ALU.mult),
                              reads=[p1, tabb], writes=[b3])
                        kb.op("dve", lambda e, A4=A4, ps2=ps2, cs=cs, T=T: e.tensor_tensor(out=A4[:, :T], in0=ps2[:, :T], in1=cs, op=ALU.mult),
                              reads=[p2, tabb], writes=[b4])
                        kb.op("dve", lambda e, A3=A3, A4=A4, O2=O2, T=T: e.tensor_tensor(out=O2[:, :T], in0=A3[:, :T], in1=A4[:, :T], op=ALU.add),
                              reads=[b3, b4], writes=[ob2])
                    else:
                        sc = 1.0 / 16.0 if isk else 1.0
                        kb.op("act", lambda e, O1=O1, ps1=ps1, T=T, sc=sc: e.activation(out=O1[:, :T], in_=ps1[:, :T], func=AF.Copy, scale=sc),
                              reads=[p1], writes=[ob1])
                        kb.op("act", lambda e, O2=O2, ps2=ps2, T=T, sc=sc: e.activation(out=O2[:, :T], in_=ps2[:, :T], func=AF.Copy, scale=sc),
                              reads=[p2], writes=[ob2])
                    kb.dma("sp", dst[h * 256:h * 256 + 128, t0:t0 + T], O1[:, :T], reads=[ob1])
                    kb.dma("sp", dst[h * 256 + 128:h * 256 + 256, t0:t0 + T], O2[:, :T], reads=[ob2])
            for cb_ in range(24):
                if cb_ < 8:
                    src = Wq[:, 4096 + cb_ * 512:4096 + (cb_ + 1) * 512]
                else:
                    src = Wg[:, (cb_ - 8) * 512:(cb_ - 7) * 512]
                wb, W = ws.load([(src, 0, 512)], 16, 512)
                for k, (t0, T) in enumerate(grp):
                    xb, X = xs[(gi % 2) * 2 + k]
                    for tq in range(T // 128):
                        pp, pps = kb.ps()
                        for kc in range(16):
                            kb.op("pe", lambda e, pps=pps, W=W, X=X, kc=kc, tq=tq: e.matmul(
                                pps, X[:, kc * 512 + tq * 128:kc * 512 + tq * 128 + 128],
                                W[:, kc * 512:(kc + 1) * 512], start=(kc == 0), stop=(kc == 15)),
                                reads=[wb, xb], writes=[pp])
                        ob, O = o16[oi % 4]
                        oi += 1
                        fn = AF.Copy if cb_ < 8 else AF.Silu
                        kb.op("act", lambda e, O=O, pps=pps, fn=fn: e.activation(out=O, in_=pps, func=fn),
                              reads=[pp], writes=[ob])
                        r0 = t0 + tq * 128
                        if cb_ < 8:
                            kb.dma("sp", V[r0:r0 + 128, cb_ * 512:(cb_ + 1) * 512], O, reads=[ob])
                        else:
                            kb.dma("sp", G[r0:r0 + 128, (cb_ - 8) * 512:(cb_ - 7) * 512], O, reads=[ob])
        kb.barrier()

    def phase_ret_scan(j):
        dlb, DL = kb.alloc32(16, "dl")
        kb.dma("sp", DL, dlog[j], writes=[dlb])
        kb.op("act", lambda e: e.activation(out=DL, in_=DL, func=AF.Exp, scale=-1.0), reads=[dlb], writes=[dlb])
        kb.op("act", lambda e: e.activation(out=DL, in_=DL, func=AF.Ln, bias=1.0), reads=[dlb], writes=[dlb])
        ixb, IXI = kb.alloc(4 * 128, "ixi")
        IXI = IXI.bitcast(I32)
        dfb, DIFF = kb.alloc32(128, "diff")
        kb.op("pool", lambda e: e.iota(IXI, pattern=[[1, 128]], base=0, channel_multiplier=-1), writes=[ixb])
        kb.op("dve", lambda e: e.tensor_copy(out=DIFF, in_=IXI), reads=[ixb], writes=[dfb])
        pcb, PC = kb.alloc32(4, "pc")
        kb.op("dve", lambda e: e.tensor_scalar(out=PC[:, 3:4], in0=DIFF[:, 0:1], scalar1=-1.0, scalar2=None, op0=ALU.mult),
              reads=[dfb], writes=[pcb])
        kb.op("dve", lambda e: e.tensor_scalar(out=PC[:, 0:1], in0=PC[:, 3:4], scalar1=1.0, scalar2=None, op0=ALU.add),
              reads=[pcb], writes=[pcb])
        kb.op("dve", lambda e: e.tensor_scalar(out=PC[:, 1:2], in0=DIFF[:, 0:1], scalar1=127.0, scalar2=None, op0=ALU.add),
              reads=[dfb], writes=[pcb])
        kb.op("dve", lambda e: e.tensor_scalar(out=PC[:, 2:3], in0=DIFF[:, 0:1], scalar1=128.0, scalar2=None, op0=ALU.add),
              reads=[dfb], writes=[pcb])
        agb, ARG = kb.alloc32(48, "arg")
        decb, DEC = kb.alloc32(48, "dec")
        rqb, RQD = kb.alloc32(16, "rqd")
        for d in range(2):
            qc = PC[:, 0:1] if d == 0 else PC[:, 2:3]
            kc_ = PC[:, 1:2] if d == 0 else PC[:, 3:4]
            kb.op("dve", lambda e, d=d, qc=qc: e.tensor_scalar(out=ARG[:, d * 8:d * 8 + 8], in0=DL[:, d * 8:d * 8 + 8],
                                                               scalar1=qc, scalar2=None, op0=ALU.mult),
                  reads=[dlb, pcb], writes=[agb])
            kb.op("dve", lambda e, d=d, kc_=kc_: e.tensor_scalar(out=ARG[:, 16 + d * 8:16 + d * 8 + 8], in0=DL[:, d * 8:d * 8 + 8],
                                                                 scalar1=kc_, scalar2=None, op0=ALU.mult),
                  reads=[dlb, pcb], writes=[agb])
        kb.op("dve", lambda e: e.tensor_scalar(out=ARG[:, 32:48], in0=DL, scalar1=128.0, scalar2=None, op0=ALU.mult),
              reads=[dlb], writes=[agb])
        kb.op("act", lambda e: e.activation(out=DEC, in_=ARG, func=AF.Exp, scale=-1.0), reads=[agb], writes=[decb])
        kb.op("act", lambda e: e.activation(out=RQD, in_=ARG[:, 0:16], func=AF.Exp, scale=1.0), reads=[agb], writes=[rqb])
        inb, IND = kb.alloc32(256, "ind")
        kb.op("dve", lambda e: e.tensor_scalar(out=IND[:, 0:128], in0=DIFF, scalar1=0.0, scalar2=None, op0=ALU.is_ge),
              reads=[dfb], writes=[inb])
        kb.op("dve", lambda e: e.tensor_scalar(out=IND[:, 128:256], in0=DIFF, scalar1=0.0, scalar2=None, op0=ALU.is_le),
              reads=[dfb], writes=[inb])

        Sb, S = kb.alloc32(8 * 2 * 512, "S")
        Sbb, Sbf = kb.alloc16(8 * 2 * 512, "Sbf")
        ld = []
        for i in range(2):
            ld.append(dict(q=kb.alloc16(2048, "q%d" % i), k=kb.alloc16(2048, "k%d" % i),
                           v=kb.alloc16(4096, "v%d" % i), g=kb.alloc16(4096, "g%d" % i)))
        kdb, KD = kb.alloc16(2048, "kd")
        pts = [kb.alloc16(128, "pt%d" % i) for i in range(2)]
        o1b, O1 = kb.alloc32(8 * 512, "O1")
        o2 = [kb.alloc32(512, "o2%d" % i) for i in range(2)]
        ydb, YD = kb.alloc32(4096, "YD")
        yfb, YFc = kb.alloc32(4096, "YFc")
        ybb, Yb = kb.alloc16(4096, "Yb")
        yts = [kb.alloc16(1024, "yt%d" % i) for i in range(2)]
        stb, ST = kb.alloc32(8 * 6, "st")
        mvb, MV = kb.alloc32(8 * 2, "mv")
        rsb, RS = kb.alloc32(8, "rs")
        nbb, NB = kb.alloc32(8, "nb")
        li = 0
        orders = [[32, 33] + list(range(32)), [33, 32] + list(range(31, -1, -1))]
        for d in range(2):
            for n, ck in enumerate(orders[d]):
                first = (n == 0)
                tok0 = ck * 128
                cur = ld[li % 2]
                li += 1
                (qb, Q), (kb_, K), (vb, Vc), (gb, Gc) = cur["q"], cur["k"], cur["v"], cur["g"]
                kb.dma("sp", Q.rearrange("p (c t) -> p c t", t=128), QT[:, tok0:tok0 + 128].rearrange("(c p) t -> p c t", p=128), writes=[qb])
                kb.dma("sp", K.rearrange("p (c t) -> p c t", t=128), KT[:, tok0:tok0 + 128].rearrange("(c p) t -> p c t", p=128), writes=[kb_])
                kb.dma("sp", Vc, V[tok0:tok0 + 128, :], writes=[vb])
                kb.dma("sp", Gc, G[tok0:tok0 + 128, d * 4096:(d + 1) * 4096], writes=[gb])
                if d == 1:
                    kb.dma("sp", YFc, YF[tok0:tok0 + 128, :], writes=[yfb])
                for half in range(2):
                    pb, ps = kb.ps()
                    psT = ps.bitcast(BF16)
                    for c8 in range(8):
                        c = half * 8 + c8
                        kb.op("pe", lambda e, psT=psT, K=K, c=c, c8=c8: e.transpose(
                            psT[:, c8 * 128:(c8 + 1) * 128], K[:, c * 128:(c + 1) * 128], ident),
                            reads=[kb_, id_b], writes=[pb])
                    for h4 in range(4):
                        h = half * 4 + h4
                        kb.op("act", lambda e, psT=psT, h=h, h4=h4, d=d: e.activation(
                            out=KD[:, h * 256:(h + 1) * 256], in_=psT[:, h4 * 256:(h4 + 1) * 256], func=AF.Copy,
                            scale=DEC[:, 16 + d * 8 + h:16 + d * 8 + h + 1]), reads=[pb, decb], writes=[kdb])
                for h in range(8):
                    pbs, pss = kb.ps()
                    for dc in range(2):
                        c = 2 * h + dc
                        kb.op("pe", lambda e, pss=pss, K=K, Q=Q, c=c, dc=dc: e.matmul(
                            pss[:, :128], K[:, c * 128:(c + 1) * 128], Q[:, c * 128:(c + 1) * 128],
                            start=(dc == 0), stop=(dc == 1)), reads=[kb_, qb], writes=[pbs])
                    ptb, PT = pts[h % 2]
                    kb.op("dve", lambda e, PT=PT, pss=pss, d=d, h=h: e.scalar_tensor_tensor(
                        out=PT, in0=pss[:, :128], scalar=RQD[:, d * 8 + h:d * 8 + h + 1],
                        in1=IND[:, d * 128:(d + 1) * 128], op0=ALU.mult, op1=ALU.mult),
                        reads=[pbs, rqb, inb], writes=[ptb])
                    pbo, pso = kb.ps()
                    kb.op("pe", lambda e, pso=pso, PT=PT, Vc=Vc, h=h, first=first: e.matmul(
                        pso, PT, Vc[:, h * 512:(h + 1) * 512], start=True, stop=first), reads=[ptb, vb], writes=[pbo])
                    if not first:
                        for dc in range(2):
                            c = 2 * h + dc
                            kb.op("pe", lambda e, pso=pso, Q=Q, c=c, dc=dc, h=h: e.matmul(
                                pso, Q[:, c * 128:(c + 1) * 128], Sbf[:, (h * 2 + dc) * 512:(h * 2 + dc + 1) * 512],
                                start=False, stop=(dc == 1)), reads=[qb, Sbb], writes=[pbo])
                    kb.op("act", lambda e, pso=pso, h=h, d=d: e.activation(
                        out=O1[:, h * 512:(h + 1) * 512], in_=pso, func=AF.Copy, scale=DEC[:, d * 8 + h:d * 8 + h + 1]),
                        reads=[pbo, decb], writes=[o1b])
                    for dc in range(2):
                        pbu, psu = kb.ps()
                        kb.op("pe", lambda e, psu=psu, h=h, dc=dc, Vc=Vc: e.matmul(
                            psu, KD[:, h * 256 + dc * 128:h * 256 + dc * 128 + 128], Vc[:, h * 512:(h + 1) * 512],
                            start=True, stop=True), reads=[kdb, vb], writes=[pbu])
                        sl = S[:, (h * 2 + dc) * 512:(h * 2 + dc + 1) * 512]
                        if first:
                            kb.op("dve", lambda e, sl=sl, psu=psu: e.tensor_copy(out=sl, in_=psu), reads=[pbu], writes=[Sb])
                        else:
                            kb.op("dve", lambda e, sl=sl, psu=psu, d=d, h=h: e.scalar_tensor_tensor(
                                out=sl, in0=sl, scalar=DEC[:, 32 + d * 8 + h:32 + d * 8 + h + 1], in1=psu,
                                op0=ALU.mult, op1=ALU.add), reads=[pbu, Sb, decb], writes=[Sb])
                        kb.op("act", lambda e, sl=sl, h=h, dc=dc: e.activation(
                            out=Sbf[:, (h * 2 + dc) * 512:(h * 2 + dc + 1) * 512], in_=sl, func=AF.Copy),
                            reads=[Sb], writes=[Sbb])
                for h in range(8):
                    kb.op("dve", lambda e, h=h: e.bn_stats(out=ST[:, h * 6:(h + 1) * 6], in_=O1[:, h * 512:(h + 1) * 512]),
                          reads=[o1b], writes=[stb])
                for h in range(8):
                    kb.op("dve", lambda e, h=h: e.bn_aggr(out=MV[:, h * 2:(h + 1) * 2], in_=ST[:, h * 6:(h + 1) * 6]),
                          reads=[stb], writes=[mvb])
                kb.op("dve", lambda e: e.tensor_scalar(out=RS, in0=MV[:, 1:16:2], scalar1=GN_EPS, scalar2=None, op0=ALU.add),
                      reads=[mvb], writes=[rsb])
                kb.op("act", lambda e: e.activation(out=RS, in_=RS, func=AF.Sqrt), reads=[rsb], writes=[rsb])
                kb.op("dve", lambda e: e.reciprocal(out=RS, in_=RS), reads=[rsb], writes=[rsb])
                kb.op("dve", lambda e: e.scalar_tensor_tensor(out=NB, in0=MV[:, 0:16:2], scalar=-1.0, in1=RS,
                                                              op0=ALU.mult, op1=ALU.mult), reads=[mvb, rsb], writes=[nbb])
                for h in range(8):
                    ob, O2 = o2[h % 2]
                    kb.op("act", lambda e, O2=O2, h=h: e.activation(
                        out=O2, in_=O1[:, h * 512:(h + 1) * 512], func=AF.Identity, scale=RS[:, h:h + 1], bias=NB[:, h:h + 1]),
                        reads=[o1b, rsb, nbb], writes=[ob])
                    kb.op("dve", lambda e, O2=O2, h=h, Gc=Gc: e.tensor_tensor(
                        out=YD[:, h * 512:(h + 1) * 512], in0=O2, in1=Gc[:, h * 512:(h + 1) * 512], op=ALU.mult),
                        reads=[ob, gb], writes=[ydb])
                if d == 0:
                    kb.dma("sp", YF[tok0:tok0 + 128, :], YD, reads=[ydb])
                else:
                    kb.op("dve", lambda e: e.tensor_tensor(out=Yb, in0=YD, in1=YFc, op=ALU.add), reads=[ydb, yfb], writes=[ybb])
                    for g4 in range(4):
                        pb, ps = kb.ps()
                        psT = ps.bitcast(BF16)
                        for c8 in range(8):
                            c = g4 * 8 + c8
                            kb.op("pe", lambda e, psT=psT, c=c, c8=c8: e.transpose(
                                psT[:, c8 * 128:(c8 + 1) * 128], Yb[:, c * 128:(c + 1) * 128], ident),
                                reads=[ybb, id_b], writes=[pb])
                        tb_, YTt = yts[g4 % 2]
                        kb.op("act", lambda e, YTt=YTt, psT=psT: e.activation(out=YTt, in_=psT, func=AF.Copy),
                              reads=[pb], writes=[tb_])
                        kb.dma("sp", YT[g4 * 1024:(g4 + 1) * 1024, tok0:tok0 + 128].rearrange("(c p) t -> p c t", p=128),
                               YTt.rearrange("p (c t) -> p c t", t=128), reads=[tb_])
        kb.barrier()

    def phase_mla_down(j):
        ws = WStream(3, 16 * 256)
        xs = [kb.alloc16(16 * 512, "x%d" % i) for i in range(2)]
        gqb, GQ = kb.alloc32(8, "gq")
        kb.dma("sp", GQ[:, 0:4], g_qT[j], writes=[gqb])
        kb.dma("sp", GQ[:, 4:8], g_kvT[j], writes=[gqb])
        mtb, MT = kb.alloc32(2 * 512, "mtab")
        cb_, C = kb.alloc32(4 * 512, "C")
        sq = [kb.alloc32(512, "sq%d" % i) for i in range(2)]
        rb_, R = kb.alloc32(512, "R")
        on = [kb.alloc16(512, "on%d" % i) for i in range(3)]
        tm = [kb.alloc32(512, "tm%d" % i) for i in range(2)]
        load_x(xs[0], U, TILES[0][0], TILES[0][1], 16)
        oi = 0
        for ti, (t0, T) in enumerate(TILES):
            lat = t0 < NLAT
            xb, X = xs[ti % 2]
            if ti + 1 < len(TILES):
                load_x(xs[(ti + 1) % 2], U, TILES[ti + 1][0], TILES[ti + 1][1], 16)
            if lat:
                kb.dma("sp", MT.rearrange("p (a t) -> p a t", t=512)[0:64], mtab[:, :, t0:t0 + T].rearrange("a p t -> p a t"),
                       writes=[mtb])
            for which, (Wsrc, dst, goff) in enumerate(((w_dq[j], CQN, 0), (w_dkv[j], CKVN, 4))):
                for blk in range(2):
                    wb, W = ws.load([(Wsrc[:, blk * 256:(blk + 1) * 256], 0, 256)], 16, 256)
                    for mm in range(2):
                        c = blk * 2 + mm
                        pb, ps = kb.ps()
                        for kc in range(16):
                            kb.op("pe", lambda e, ps=ps, W=W, X=X, kc=kc, mm=mm, T=T: e.matmul(
                                ps[:, :T], W[:, kc * 256 + mm * 128:kc * 256 + mm * 128 + 128],
                                X[:, kc * 512:kc * 512 + T], start=(kc == 0), stop=(kc == 15)), reads=[wb, xb], writes=[pb])
                        kb.op("act", lambda e, ps=ps, c=c, T=T: e.activation(out=C[:, c * 512:c * 512 + T], in_=ps[:, :T], func=AF.Copy),
                              reads=[pb], writes=[cb_])
                pbs, pss = kb.ps()
                for c in range(4):
                    qb, q = sq[c % 2]
                    kb.op("act", lambda e, q=q, c=c, T=T: e.activation(out=q[:, :T], in_=C[:, c * 512:c * 512 + T], func=AF.Square),
                          reads=[cb_], writes=[qb])
                    kb.op("pe", lambda e, pss=pss, q=q, c=c, T=T: e.matmul(pss[:, :T], ones32, q[:, :T], start=(c == 0), stop=(c == 3)),
                          reads=[ones_b, qb], writes=[pbs])
                kb.op("dve", lambda e, pss=pss, T=T: e.tensor_scalar(out=R[:, :T], in0=pss[:, :T], scalar1=1.0 / 512, scalar2=RMS_EPS,
                                                                     op0=ALU.mult, op1=ALU.add), reads=[pbs], writes=[rb_])
                kb.op("act", lambda e, T=T: e.activation(out=R[:, :T], in_=R[:, :T], func=AF.Sqrt), reads=[rb_], writes=[rb_])
                kb.op("dve", lambda e, T=T: e.reciprocal(out=R[:, :T], in_=R[:, :T]), reads=[rb_], writes=[rb_])
                for c in range(4):
                    ob, O = on[oi % 3]
                    oi += 1
                    kb.op("dve", lambda e, O=O, c=c, T=T, goff=goff: e.scalar_tensor_tensor(
                        out=O[:, :T], in0=C[:, c * 512:c * 512 + T], scalar=GQ[:, goff + c:goff + c + 1], in1=R[:, :T],
                        op0=ALU.mult, op1=ALU.mult), reads=[cb_, gqb, rb_], writes=[ob])
                    kb.dma("sp", dst[c * 128:(c + 1) * 128, t0:t0 + T], O[:, :T], reads=[ob])
            wb, W = ws.load([(w_dkv[j][:, 512:640], 0, 128)], 16, 128)
            pa, psa = kb.ps()
            pb, psb = kb.ps()
            for (pp, pps, off) in ((pa, psa, 0), (pb, psb, 64)):
                for kc in range(16):
                    kb.op("pe", lambda e, pps=pps, W=W, X=X, kc=kc, off=off, T=T: e.matmul(
                        pps[0:64, :T], W[:, kc * 128 + off:kc * 128 + off + 64], X[:, kc * 512:kc * 512 + T],
                        start=(kc == 0), stop=(kc == 15)), reads=[wb, xb], writes=[pp])
            ob, O = on[oi % 3]
            oi += 1
            if lat:
                (b1, A1), (b2, A2) = tm
                kb.op("dve", lambda e, A1=A1, psa=psa, T=T: e.tensor_tensor(out=A1[0:64, :T], in0=psa[0:64, :T], in1=MT[0:64, 0:T], op=ALU.mult),
                      reads=[pa, mtb], writes=[b1])
                kb.op("dve", lambda e, A2=A2, psb=psb, T=T: e.tensor_tensor(out=A2[0:64, :T], in0=psb[0:64, :T], in1=MT[0:64, 512:512 + T], op=ALU.mult),
                      reads=[pb, mtb], writes=[b2])
                kb.op("dve", lambda e, A1=A1, A2=A2, O=O, T=T: e.tensor_tensor(out=O[0:64, :T], in0=A1[0:64, :T], in1=A2[0:64, :T], op=ALU.add),
                      reads=[b1, b2], writes=[ob])
            else:
                kb.op("act", lambda e, O=O, psa=psa, T=T: e.activation(out=O[0:64, :T], in_=psa[0:64, :T], func=AF.Copy),
                      reads=[pa], writes=[ob])
            kb.dma("sp", KRT[:, t0:t0 + T], O[0:64, :T], reads=[ob])
        kb.barrier()

    def phase_mla_up(j, qtiles):
        ws = WStream(3, 4 * 512)
        xq = [kb.alloc16(4 * 512, "xq%d" % i) for i in range(2)]
        xk = [kb.alloc16(4 * 512, "xk%d" % i) for i in range(2)]
        mtb, MT = kb.alloc32(2 * 512, "mtab")
        on = [kb.alloc16(512, "on%d" % i) for i in range(4)]
        tm = [kb.alloc32(512, "tm%d" % i) for i in range(2)]
        oi = 0
        for ti, (t0, T) in enumerate(TILES):
            lat = t0 < NLAT
            doq = (t0, T) in qtiles
            xqb, XQ = xq[ti % 2]
            xkb, XK = xk[ti % 2]
            if doq:
                load_x((xqb, XQ), CQN, t0, T, 4)
            load_x((xkb, XK), CKVN, t0, T, 4)
            if lat:
                kb.dma("sp", MT.rearrange("p (a t) -> p a t", t=512)[0:64], mtab[:, :, t0:t0 + T].rearrange("a p t -> p a t"),
                       writes=[mtb])
            if doq:
                for hb2 in range(8):
                    wb, W = ws.load([(w_uq[j][:, hb2 * 512:(hb2 + 1) * 512], 0, 512)], 4, 512)
                    for hh in range(2):
                        h = hb2 * 2 + hh
                        pn, psn = kb.ps()
                        pa, psa = kb.ps()
                        pb, psb = kb.ps()
                        for (pp, pps, off, M) in ((pn, psn, hh * 256, 128), (pa, psa, hh * 256 + 128, 64), (pb, psb, hh * 256 + 192, 64)):
                            for kc in range(4):
                                kb.op("pe", lambda e, pps=pps, W=W, XQ=XQ, kc=kc, off=off, M=M, T=T: e.matmul(
                                    pps[0:M, :T], W[:, kc * 512 + off:kc * 512 + off + M], XQ[:, kc * 512:kc * 512 + T],
                                    start=(kc == 0), stop=(kc == 3)), reads=[wb, xqb], writes=[pp])
                        ob, O = on[oi % 4]
                        oi += 1
                        kb.op("act", lambda e, O=O, psn=psn, T=T: e.activation(out=O[:, :T], in_=psn[:, :T], func=AF.Copy),
                              reads=[pn], writes=[ob])
                        kb.dma("sp", QN[h * 128:(h + 1) * 128, t0:t0 + T], O[:, :T], reads=[ob])
                        ob, O = on[oi % 4]
                        oi += 1
                        if lat:
                            (b1, A1), (b2, A2) = tm
                            kb.op("dve", lambda e, A1=A1, psa=psa, T=T: e.tensor_tensor(out=A1[0:64, :T], in0=psa[0:64, :T], in1=MT[0:64, 0:T], op=ALU.mult),
                                  reads=[pa, mtb], writes=[b1])
                            kb.op("dve", lambda e, A2=A2, psb=psb, T=T: e.tensor_tensor(out=A2[0:64, :T], in0=psb[0:64, :T], in1=MT[0:64, 512:512 + T], op=ALU.mult),
                                  reads=[pb, mtb], writes=[b2])
                            kb.op("dve", lambda e, A1=A1, A2=A2, O=O, T=T: e.tensor_tensor(out=O[0:64, :T], in0=A1[0:64, :T], in1=A2[0:64, :T], op=ALU.add),
                                  reads=[b1, b2], writes=[ob])
                        else:
                            kb.op("act", lambda e, O=O, psa=psa, T=T: e.activation(out=O[0:64, :T], in_=psa[0:64, :T], func=AF.Copy),
                                  reads=[pa], writes=[ob])
                        kb.dma("sp", QR[h * 64:(h + 1) * 64, t0:t0 + T], O[0:64, :T], reads=[ob])
            for blk in range(4):
                wb, W = ws.load([(w_ukv_k[j][:, blk * 512:(blk + 1) * 512], 0, 512)], 4, 512)
                for mm in range(4):
                    h = blk * 4 + mm
                    pp, pps = kb.ps()
                    for kc in range(4):
                        kb.op("pe", lambda e, pps=pps, W=W, XK=XK, kc=kc, mm=mm, T=T: e.matmul(
                            pps[:, :T], W[:, kc * 512 + mm * 128:kc * 512 + mm * 128 + 128], XK[:, kc * 512:kc * 512 + T],
                            start=(kc == 0), stop=(kc == 3)), reads=[wb, xkb], writes=[pp])
                    ob, O = on[oi % 4]
                    oi += 1
                    kb.op("act", lambda e, O=O, pps=pps, T=T: e.activation(out=O[:, :T], in_=pps[:, :T], func=AF.Copy),
                          reads=[pp], writes=[ob])
                    kb.dma("sp", KN[h * 128:(h + 1) * 128, t0:t0 + T], O[:, :T], reads=[ob])
            for blk in range(4):
                wb, W = ws.load([(w_ukv_v[j][:, blk * 512:(blk + 1) * 512], 0, 512)], 4, 512)
                for tq in range(T // 128):
                    pp, pps = kb.ps()
                    for kc in range(4):
                        kb.op("pe", lambda e, pps=pps, W=W, XK=XK, kc=kc, tq=tq: e.matmul(
                            pps, XK[:, kc * 512 + tq * 128:kc * 512 + tq * 128 + 128], W[:, kc * 512:(kc + 1) * 512],
                            start=(kc == 0), stop=(kc == 3)), reads=[wb, xkb], writes=[pp])
                    ob, O = on[oi % 4]
                    oi += 1
                    kb.op("act", lambda e, O=O, pps=pps: e.activation(out=O, in_=pps, func=AF.Copy), reads=[pp], writes=[ob])
                    r0 = t0 + tq * 128
                    kb.dma("sp", VA[r0:r0 + 128, blk * 512:(blk + 1) * 512], O, reads=[ob])
        kb.barrier()

    def phase_mla_attn(qtiles):
        krb, KR = kb.alloc16(TOK, "KR")
        kb.dma("sp", KR[0:64, :], KRT, writes=[krb])
        kn = [kb.alloc16(TOK, "kn%d" % i) for i in range(2)]
        vh = [kb.alloc16(34 * 130, "vh%d" % i) for i in range(2)]
        for (vb, VH) in vh:
            kb.op("pool", lambda e, VH=VH: e.memset(VH, 1.0), writes=[vb])
        qn = [kb.alloc16(512, "qn%d" % i) for i in range(2)]
        qr = [kb.alloc16(512, "qr%d" % i) for i in range(2)]
        pt = [kb.alloc16(512, "pt%d" % i) for i in range(3)]
        rcb, RC = kb.alloc32(4, "rc")
        otm = [kb.alloc16(128, "otm%d" % i) for i in range(2)]
        ott = [kb.alloc16(512, "ott%d" % i) for i in range(2)]
        qi_ = 0
        pi = 0
        for h in range(16):
            knb, KNh = kn[h % 2]
            vb, VH = vh[h % 2]
            kb.dma("sp", KNh, KN[h * 128:(h + 1) * 128, :], writes=[knb])
            kb.dma("sp", VH.rearrange("p (c d) -> p c d", d=130)[:, :, 0:128],
                   VA[:, h * 128:(h + 1) * 128].rearrange("(c p) d -> p c d", p=128), writes=[vb])
            for (t0, T) in qtiles:
                keys = list(range(34)) if t0 < NLAT else [32, 33]
                qnb, QNt = qn[qi_ % 2]
                qrb, QRt = qr[qi_ % 2]
                qi_ += 1
                kb.dma("sp", QNt[:, :T], QN[h * 128:(h + 1) * 128, t0:t0 + T], writes=[qnb])
                kb.dma("sp", QRt[0:64, :T], QR[h * 64:(h + 1) * 64, t0:t0 + T], writes=[qrb])
                nq = T // 128
                pob = [(kb.pbufs[i], kb.psum[:, i, :]) for i in range((nq + 1) // 2)]
                for idx, kc in enumerate(keys):
                    si = 3 + (pi % 5)
                    pbs, pss = kb.pbufs[si], kb.psum[:, si, :]
                    kb.op("pe", lambda e, pss=pss, KNh=KNh, QNt=QNt, kc=kc, T=T: e.matmul(
                        pss[:, :T], KNh[:, kc * 128:(kc + 1) * 128], QNt[:, :T], start=True, stop=False),
                        reads=[knb, qnb], writes=[pbs])
                    kb.op("pe", lambda e, pss=pss, QRt=QRt, kc=kc, T=T: e.matmul(
                        pss[:, :T], KR[0:64, kc * 128:(kc + 1) * 128], QRt[0:64, :T], start=False, stop=True),
                        reads=[krb, qrb], writes=[pbs])
                    ptb, PT = pt[pi % 3]
                    pi += 1
                    kb.op("act", lambda e, PT=PT, pss=pss, T=T: e.activation(out=PT[:, :T], in_=pss[:, :T], func=AF.Exp, scale=MLA_SCALE),
                          reads=[pbs], writes=[ptb])
                    for qi in range(nq):
                        pbo, pso = pob[qi // 2]
                        kb.op("pe", lambda e, pso=pso, PT=PT, VH=VH, qi=qi, kc=kc, idx=idx, keys=keys: e.matmul(
                            pso[:, (qi % 2) * 129:(qi % 2) * 129 + 129], PT[:, qi * 128:(qi + 1) * 128],
                            VH[:, kc * 130:kc * 130 + 129], start=(idx == 0), stop=(idx == len(keys) - 1)),
                            reads=[ptb, vb], writes=[pbo])
                otb, OTt = ott[qi_ % 2]
                pbt, pst = kb.pbufs[2], kb.psum[:, 2, :]
                psT = pst.bitcast(BF16)
                for qi in range(nq):
                    pbo, pso = pob[qi // 2]
                    o0 = (qi % 2) * 129
                    kb.op("dve", lambda e, pso=pso, o0=o0, qi=qi: e.reciprocal(out=RC[:, qi:qi + 1], in_=pso[:, o0 + 128:o0 + 129]),
                          reads=[pbo], writes=[rcb])
                    ob, OM = otm[qi % 2]
                    kb.op("act", lambda e, OM=OM, pso=pso, o0=o0, qi=qi: e.activation(
                        out=OM, in_=pso[:, o0:o0 + 128], func=AF.Copy, scale=RC[:, qi:qi + 1]), reads=[pbo, rcb], writes=[ob])
                    kb.op("pe", lambda e, psT=psT, OM=OM, qi=qi: e.transpose(psT[:, qi * 128:(qi + 1) * 128], OM, ident),
                          reads=[ob, id_b], writes=[pbt])
                kb.op("dve", lambda e, OTt=OTt, psT=psT, T=T: e.tensor_copy(out=OTt[:, :T], in_=psT[:, :T]), reads=[pbt], writes=[otb])
                kb.dma("sp", OT[h * 128:(h + 1) * 128, t0:t0 + T], OTt[:, :T], reads=[otb])
        kb.barrier()

    phase_mod0()
    for l in range(nlayers):
        j = l // 2
        ctx_out = l < DEPTH - 1
        tiles = TILES if ctx_out else TILES[:8]
        if l % 2 == 0:
            phase_ret_proj(j)
            phase_ret_scan(j)
            phase_out_ln(l, 0, YT, 32, ret_w_o[j], tiles, True)
        else:
            phase_mla_down(j)
            phase_mla_up(j, tiles)
            phase_mla_attn(tiles)
            phase_out_ln(l, 0, OT, 16, mla_w_o[j], tiles, True)
        phase_ffn_in(l, tiles)
        last = (l == DEPTH - 1)
        phase_out_ln(l, 1, HID, 44, ffn_w_out[l], tiles, not last, final=last)
    if nlayers < DEPTH:
        db_ = Buf("dump")
        kb.bufs.append(db_)
        kb.dma("sp", outT, H[:, 0:NLAT], sb=db_)
    for name, (src, dst) in dbg_out.items():
        b = Buf("dbg" + name)
        kb.bufs.append(b)
        kb.dma("sp", dst, src, sb=b)
    fin = []
    seen = set()
    for b in kb.bufs:
        if b.sem is not None and id(b.sem) not in seen:
            seen.add(id(b.sem))
            fin.append(("d", b.sem, kb.sem_cnt[id(b.sem)]))
    kb.ins["sp"].append(dict(fn=None, deps=fin))
    kb.emit()
    stack.close()
    return nc


def _prep_inputs(x, c, ctx, c_ctx, ada_w, ada_b, ln_g, ln_b, ret_w_qkv, ret_w_g, ret_decay_logit, ret_w_o,
                 mla_w_dq, mla_g_q, mla_w_uq, mla_w_dkv, mla_g_kv, mla_w_ukv, mla_w_o, ffn_w_in, ffn_w_out):
    f = np.float32
    A = lambda a: np.ascontiguousarray(np.asarray(a, dtype=f))
    t = np.arange(NLAT, dtype=f)
    ret_inv = (f(10000.0) ** (-np.linspace(0.0, 1.0, 128, dtype=f))).astype(f)
    ang = (t[:, None] * ret_inv[None, :]).astype(f)
    cs, sn = np.cos(ang).astype(f).T, np.sin(ang).astype(f).T
    rtab = A(np.stack([cs, sn, cs * f(1.0 / 16), sn * f(1.0 / 16)]))
    rows = (np.arange(NLAT) // 64).astype(f)
    cols = (np.arange(NLAT) % 64).astype(f)
    ax_inv = (f(10000.0) ** (-np.arange(8, dtype=f) * f(2.0) / f(16))).astype(f)
    ax_inv = (f(10000.0) ** (-np.arange(16, dtype=f) * f(2.0) / f(32))).astype(f)
    ra = (rows[:, None] * ax_inv[None, :]).astype(f)
    ca = (cols[:, None] * ax_inv[None, :]).astype(f)
    rc, rs_, cc, cs_ = np.cos(ra).T, np.sin(ra).T, np.cos(ca).T, np.sin(ca).T
    mtab = A(np.stack([np.concatenate([rc, cc, rc, cc], 0), np.concatenate([-rs_, -cs_, rs_, cs_], 0)]))
    ident = np.eye(128, dtype=f)
    fm16 = lambda v: A(np.asarray(v, dtype=f).reshape(16, 128).T)
    common = dict(
        ada_w=A(ada_w),
        ada_bT=A(np.asarray(ada_b, dtype=f).reshape(DEPTH, 96, 128).transpose(0, 2, 1)),
        ln_gT=A(np.asarray(ln_g, dtype=f).reshape(DEPTH, 2, 16, 128).transpose(0, 1, 3, 2)),
        ln_bT=A(np.asarray(ln_b, dtype=f).reshape(DEPTH, 2, 16, 128).transpose(0, 1, 3, 2)),
        ret_w_qkv=A(ret_w_qkv), ret_w_g=A(ret_w_g), ret_w_o=A(ret_w_o),
        dlog=A(np.broadcast_to(np.asarray(ret_decay_logit, dtype=f).reshape(2, 1, 16), (2, 128, 16))),
        w_dq=A(mla_w_dq), mla_w_o=A(mla_w_o), ffn_w_in=A(ffn_w_in), ffn_w_out=A(ffn_w_out),
        g_qT=A(np.asarray(mla_g_q, dtype=f).reshape(2, 4, 128).transpose(0, 2, 1)),
        g_kvT=A(np.asarray(mla_g_kv, dtype=f).reshape(2, 4, 128).transpose(0, 2, 1)),
        rtab=rtab, mtab=mtab, ident=ident,
    )
    pa = np.concatenate([np.arange(0, 16), np.arange(32, 48), np.arange(16, 32), np.arange(48, 64)])
    pb = np.concatenate([np.arange(16, 32), np.arange(48, 64), np.arange(0, 16), np.arange(32, 48)])
    wdkv = np.asarray(mla_w_dkv, dtype=f)
    common["w_dkv"] = A(np.concatenate([wdkv[:, :, :512], wdkv[:, :, 512 + pa], wdkv[:, :, 512 + pb]], axis=2))
    wuq = np.asarray(mla_w_uq, dtype=f).reshape(2, 512, 16, 192)
    common["w_uq"] = A(np.concatenate([wuq[..., :128], wuq[..., 128 + pa], wuq[..., 128 + pb]], axis=3).reshape(2, 512, 16 * 256))
    wukv = np.asarray(mla_w_ukv, dtype=f).reshape(2, 512, 16, 256)
    common["w_ukv_k"] = A(wukv[..., :128].reshape(2, 512, 2048))
    common["w_ukv_v"] = A(wukv[..., 128:].reshape(2, 512, 2048))
    maps = []
    x = np.asarray(x, dtype=f)
    ctx = np.asarray(ctx, dtype=f)
    c = np.asarray(c, dtype=f)
    c_ctx = np.asarray(c_ctx, dtype=f)
    for core in range(8):
        b = core % 4
        m = dict(common)
        m["xT"] = A(np.concatenate([x[b].T, ctx[b].T], axis=1))
        cond = np.stack([fm16(c[b]), fm16(c_ctx)], axis=2)
        m["condT"] = A(cond.reshape(128, 32))
        maps.append(m)
    return maps


def kernel(**inputs):
    maps = _prep_inputs(**inputs)
    nc = build_program()
    res = run_bass_kernel_spmd(nc, maps, core_ids=list(range(8)))
    out = np.stack([np.ascontiguousarray(res.results[b]["outT"].T) for b in range(4)], axis=0)
    return out.astype(np.float32)
```

```python
import contextlib
import numpy as np
import concourse.bass as bass
import concourse.mybir as mybir
from concourse.bass_utils import run_bass_kernel_spmd

F32 = mybir.dt.float32
BF16 = mybir.dt.bfloat16
I32 = mybir.dt.int32
AF = mybir.ActivationFunctionType
ALU = mybir.AluOpType

D = 2048
NLAT = 4096
NCTX = 256
TOK = NLAT + NCTX
DEPTH = 4
FH = 5632
ALPHA = (2 * DEPTH) ** 0.25
LN_EPS_EFF = 1e-5 / (ALPHA * ALPHA)
GN_EPS = 1e-6
RMS_EPS = 1e-6
MLA_SCALE = (128 + 64) ** -0.5
TILES = [(i * 512, 512) for i in range(8)] + [(NLAT, NCTX)]
ENG = ["pe", "act", "dve", "pool", "sp"]
BLK = {"pe": "tensor", "act": "scalar", "dve": "vector", "pool": "gpsimd", "sp": "sync"}
ARENA_BYTES = 200 * 1024


class Buf:
    __slots__ = ("name", "lw", "rd", "sem", "cnt", "persist")

    def __init__(self, name, persist=False):
        self.name = name
        self.lw = None
        self.rd = []
        self.sem = None
        self.cnt = 0
        self.persist = persist


class KB:
    def __init__(self, nc, stack):
        self.nc = nc
        self.stack = stack
        self.ins = {e: [] for e in ENG}
        self.esem = {e: stack.enter_context(nc.semaphore("es_" + e)) for e in ENG}
        self.bufs = []
        self.free_sems = []
        self.sem_cnt = {}
        self.nsem = 0
        self.arena = stack.enter_context(nc.sbuf_tensor("arena", [128, ARENA_BYTES // 2], BF16))
        self.psum = stack.enter_context(nc.psum_tensor("psum", [128, 8, 512], F32))
        self.pbufs = [Buf("ps%d" % i, True) for i in range(8)]
        self.bufs += self.pbufs
        self.pidx = 0
        self.base = 0
        self.ptr = 0

    def alloc(self, nbytes, name="t", persist=False):
        req = nbytes
        nbytes = (nbytes + 63) // 64 * 64
        off = self.ptr
        self.ptr += nbytes
        assert self.ptr <= ARENA_BYTES, ("arena overflow", name, self.ptr)
        b = Buf(name, persist)
        self.bufs.append(b)
        return b, self.arena[:, off // 2:(off + req) // 2]

    def alloc32(self, ncols, name="t", persist=False):
        b, ap = self.alloc(ncols * 4, name, persist)
        return b, ap.bitcast(F32)

    def alloc16(self, ncols, name="t", persist=False):
        return self.alloc(ncols * 2, name, persist)

    def ps(self):
        i = self.pidx
        self.pidx = (self.pidx + 1) % 8
        return self.pbufs[i], self.psum[:, i, :]

    def end_persist(self):
        self.base = self.ptr

    def _deps(self, eng, reads, writes):
        deps = []
        for b in reads:
            if b.lw is not None:
                deps.append(b.lw)
        for b in writes:
            if b.lw is not None:
                deps.append(b.lw)
            deps.extend(b.rd)
        if eng == "pe":
            deps = [d for d in deps if not (d[0] == "e" and d[1] == "pe")]
        return deps

    def op(self, eng, fn, reads=(), writes=()):
        idx = len(self.ins[eng])
        self.ins[eng].append(dict(fn=fn, deps=self._deps(eng, reads, writes)))
        ev = ("e", eng, idx)
        for b in reads:
            b.rd.append(ev)
        for b in writes:
            b.lw = ev
            b.rd = []

    def dma(self, q, out, in_, reads=(), writes=(), sb=None):
        own = sb or (writes[0] if writes else reads[0])
        if own.sem is None:
            if self.free_sems:
                own.sem = self.free_sems.pop()
            else:
                own.sem = self.stack.enter_context(self.nc.semaphore("ds%d" % self.nsem))
                self.nsem += 1
                self.sem_cnt[id(own.sem)] = 0
        self.sem_cnt[id(own.sem)] += 16
        val = self.sem_cnt[id(own.sem)]
        self.ins[q].append(dict(fn=lambda e: e.dma_start(out=out, in_=in_), deps=self._deps(q, reads, writes),
                                dsem=own.sem))
        ev = ("d", own.sem, val)
        for b in reads:
            b.rd.append(ev)
        for b in writes:
            b.lw = ev
            b.rd = []
        if own not in reads and own not in writes:
            own.rd.append(ev)

    def barrier(self):
        evs = []
        for e in ENG:
            for i in range(len(self.ins[e]) - 1, -1, -1):
                r = self.ins[e][i]
                if r["fn"] is not None and "dsem" not in r:
                    evs.append(("e", e, i))
                    break
        seen = set()
        for b in self.bufs:
            if b.sem is not None and id(b.sem) not in seen:
                seen.add(id(b.sem))
                evs.append(("d", b.sem, self.sem_cnt[id(b.sem)]))
        for e in ENG:
            self.ins[e].append(dict(fn=None, deps=list(evs)))
        keep = []
        for b in self.bufs:
            b.lw = None
            b.rd = []
            if b.sem is not None:
                self.free_sems.append(b.sem)
                b.sem = None
            if b.persist:
                keep.append(b)
        self.bufs = keep
        self.ptr = self.base

    def emit(self):
        ins = self.ins
        for e in ENG:
            for r in ins[e]:
                for d in r["deps"]:
                    if d[0] == "e":
                        ins[d[1]][d[2]]["ms"] = True
        for e in ENG:
            c = 0
            for r in ins[e]:
                if r.get("ms"):
                    c += 1
                    r["msv"] = c
        with self.nc.Block() as block:
            for e in ENG:
                def body(eng, e=e):
                    known = {}
                    for r in ins[e]:
                        for d in r["deps"]:
                            if d[0] == "e":
                                sem, val = self.esem[d[1]], ins[d[1]][d[2]]["msv"]
                            else:
                                sem, val = d[1], d[2]
                            if known.get(id(sem), 0) >= val:
                                continue
                            eng.wait_ge(sem, val)
                            known[id(sem)] = val
                        if r["fn"] is not None:
                            i = r["fn"](eng)
                            if "dsem" in r:
                                i.then_inc(r["dsem"], 16)
                            elif r.get("ms"):
                                i.then_inc(self.esem[e], 1)
                getattr(block, BLK[e])(body)


def build_program(dbg=None, nlayers=DEPTH):
    nc = bass.Bass("TRN2", target_bir_lowering=False)
    stack = contextlib.ExitStack()

    def din(name, shape):
        return nc.dram_tensor(name, list(shape), F32, kind="ExternalInput").ap()

    def dscr(name, shape, dt):
        return nc.dram_tensor(name, list(shape), dt, kind="Internal").ap()

    xT = din("xT", [D, TOK])
    condT = din("condT", [128, 32])
    ada_w = din("ada_w", [DEPTH, D, 6 * D])
    ada_bT = din("ada_bT", [DEPTH, 128, 96])
    ln_gT = din("ln_gT", [DEPTH, 2, 128, 16])
    ln_bT = din("ln_bT", [DEPTH, 2, 128, 16])
    ret_w_qkv = din("ret_w_qkv", [2, D, 8192])
    ret_w_g = din("ret_w_g", [2, D, 8192])
    ret_w_o = din("ret_w_o", [2, 4096, D])
    dlog = din("dlog", [2, 128, 16])
    w_dq = din("w_dq", [2, D, 512])
    w_dkv = din("w_dkv", [2, D, 640])
    g_qT = din("g_qT", [2, 128, 4])
    g_kvT = din("g_kvT", [2, 128, 4])
    w_uq = din("w_uq", [2, 512, 16 * 256])
    w_ukv_k = din("w_ukv_k", [2, 512, 2048])
    w_ukv_v = din("w_ukv_v", [2, 512, 2048])
    mla_w_o = din("mla_w_o", [2, D, D])
    ffn_w_in = din("ffn_w_in", [DEPTH, D, 2 * FH])
    ffn_w_out = din("ffn_w_out", [DEPTH, FH, D])
    rtab = din("rtab", [4, 128, NLAT])
    mtab = din("mtab", [2, 64, NLAT])
    ident_in = din("ident", [128, 128])
    outT = nc.dram_tensor("outT", [D, NLAT], F32, kind="ExternalOutput").ap()

    H = dscr("H", [D, TOK], F32)
    U = dscr("U", [D, TOK], BF16)
    QT = dscr("QT", [D, TOK], BF16)
    KT = dscr("KT", [D, TOK], BF16)
    V = dscr("V", [TOK, 4096], BF16)
    G = dscr("G", [TOK, 8192], BF16)
    YF = dscr("YF", [TOK, 4096], F32)
    YT = dscr("YT", [4096, TOK], BF16)
    HID = dscr("HID", [FH, TOK], BF16)
    CQN = dscr("CQN", [512, TOK], BF16)
    CKVN = dscr("CKVN", [512, TOK], BF16)
    KRT = dscr("KRT", [64, TOK], BF16)
    QN = dscr("QN", [D, TOK], BF16)
    QR = dscr("QR", [16 * 64, TOK], BF16)
    KN = dscr("KN", [D, TOK], BF16)
    VA = dscr("VA", [TOK, D], BF16)
    OT = dscr("OT", [D, TOK], BF16)
    dbg_out = {}
    if dbg:
        for name in dbg:
            src = {"H": H, "U": U, "QT": QT, "KT": KT, "V": V, "G": G, "YT": YT, "HID": HID, "YF": YF,
                   "OT": OT, "QN": QN, "KN": KN, "VA": VA, "QR": QR, "KRT": KRT, "CQN": CQN, "CKVN": CKVN}[name]
            dbg_out[name] = (src, nc.dram_tensor("dbg_" + name, list(src.shape), src.dtype, kind="ExternalOutput").ap())

    kb = KB(nc, stack)

    ones_b, ones32 = kb.alloc32(128, "ones32", True)
    id_b, ident = kb.alloc16(128, "ident", True)
    mod_b, MOD = kb.alloc32(DEPTH * 96 * 2, "MOD", True)
    kb.end_persist()

    def modidx(l, which, c, w):
        return ((l * 6 + which) * 16 + c) * 2 + w

    def modcol(l, which, w):
        s = modidx(l, which, 0, w)
        return MOD[:, s:s + 31:2]

    kb.op("pool", lambda e: e.memset(ones32, 1.0), writes=[ones_b])
    kb.dma("pool", ident, ident_in, writes=[id_b])
    hb = Buf("hcopy")
    kb.bufs.append(hb)
    kb.dma("sp", H, xT, sb=hb)
    cb, cnd = kb.alloc32(32, "cond")
    csb, cnds = kb.alloc16(32, "conds")
    kb.dma("sp", cnd, condT, writes=[cb])
    kb.op("act", lambda e: e.activation(out=cnds, in_=cnd, func=AF.Silu), reads=[cb], writes=[csb])
    wslots = [kb.alloc16(16 * 512, "ws%d" % i) for i in range(3)]
    abb, adab = kb.alloc32(96, "adab")
    wi = 0
    for l in range(DEPTH):
        pb, ps = kb.ps()
        for blk in range(24):
            sb_, sl = wslots[wi % 3]
            wi += 1
            kb.dma("pool", sl.rearrange("p (k n) -> p k n", n=512),
                   ada_w[l][:, blk * 512:(blk + 1) * 512].rearrange("(k p) n -> p k n", p=128), writes=[sb_])
            for mm in range(4):
                j = blk * 4 + mm
                for kc in range(16):
                    kb.op("pe", lambda e, ps=ps, sl=sl, j=j, kc=kc, mm=mm: e.matmul(
                        ps[:, j * 2:j * 2 + 2], sl[:, kc * 512 + mm * 128:kc * 512 + mm * 128 + 128],
                        cnds[:, kc * 2:kc * 2 + 2], start=(kc == 0), stop=(kc == 15)),
                        reads=[sb_, csb], writes=[pb])
        kb.dma("sp", adab, ada_bT[l], writes=[abb])
        for w in range(2):
            s = modidx(l, 0, 0, w)
            kb.op("dve", lambda e, ps=ps, s=s, w=w: e.tensor_tensor(
                out=MOD[:, s:s + 191:2], in0=ps[:, w:w + 191:2], in1=adab, op=ALU.add),
                reads=[pb, abb], writes=[mod_b])
    kb.barrier()

    def ln_tables(l, which_ln, has_next):
        gb, gt = kb.alloc32(16, "lng")
        bb, bt = kb.alloc32(16, "lnb")
        kb.dma("sp", gt, ln_gT[l, which_ln], writes=[gb])
        kb.dma("sp", bt, ln_bT[l, which_ln], writes=[bb])
        tb_, tt_ = kb.alloc32(16 * 8, "lntab")
        res = {"G": gt, "B": bt, "bufs": [gb, bb, tb_, mod_b]}
        gate_which = 2 if which_ln == 0 else 5
        for w in range(2):
            ga = tt_[:, w * 16:(w + 1) * 16]
            kb.op("dve", lambda e, ga=ga, w=w: e.tensor_scalar(
                out=ga, in0=modcol(l, gate_which, w), scalar1=1.0 / ALPHA, scalar2=None, op0=ALU.mult),
                reads=[mod_b], writes=[tb_])
            res["GA%d" % w] = ga
            if has_next:
                if which_ln == 0:
                    nl, nsc, nsh = l, 4, 3
                else:
                    nl, nsc, nsh = l + 1, 1, 0
                a1 = tt_[:, 32 + w * 16:32 + (w + 1) * 16]
                ug = tt_[:, 64 + w * 16:64 + (w + 1) * 16]
                ub = tt_[:, 96 + w * 16:96 + (w + 1) * 16]
                kb.op("dve", lambda e, a1=a1, w=w: e.tensor_scalar(
                    out=a1, in0=modcol(nl, nsc, w), scalar1=1.0, scalar2=None, op0=ALU.add),
                    reads=[mod_b], writes=[tb_])
                kb.op("dve", lambda e, a1=a1, ug=ug: e.tensor_tensor(out=ug, in0=a1, in1=gt, op=ALU.mult),
                      reads=[tb_, gb], writes=[tb_])
                kb.op("dve", lambda e, a1=a1, ub=ub: e.tensor_tensor(out=ub, in0=a1, in1=bt, op=ALU.mult),
                      reads=[tb_, bb], writes=[tb_])
                kb.op("dve", lambda e, ub=ub, w=w: e.tensor_tensor(
                    out=ub, in0=ub, in1=modcol(nl, nsh, w), op=ALU.add), reads=[tb_, mod_b], writes=[tb_])
                res["UG%d" % w] = ug
                res["UB%d" % w] = ub
        return res

    class WStream:
        def __init__(self, nslots=3, slot_cols=16 * 512):
            self.slots = [kb.alloc16(slot_cols, "ws%d" % i) for i in range(nslots)]
            self.i = 0

        def load(self, pieces, kc, n):
            b, sl = self.slots[self.i % len(self.slots)]
            self.i += 1
            v = sl[:, :kc * n].rearrange("p (k n) -> p k n", n=n)
            first = True
            for (src, off, ncol) in pieces:
                kb.dma("pool", v[:, :, off:off + ncol], src.rearrange("(k p) n -> p k n", p=128), writes=[b])
            return b, sl

    def load_x(slot, src, t0, T, kc):
        b, ap = slot
        kb.dma("sp", ap[:, :kc * 512].rearrange("p (k t) -> p k t", t=512)[:, :, :T],
               src[:, t0:t0 + T].rearrange("(k p) t -> p k t", p=128), writes=[b])

    def phase_mod0():
        slots = [kb.alloc32(16 * 512, "hin%d" % i) for i in range(2)]
        outs = [kb.alloc16(16 * 512, "uo%d" % i) for i in range(2)]
        a1b, a1 = kb.alloc32(32, "a1")
        for w in range(2):
            kb.op("dve", lambda e, w=w: e.tensor_scalar(out=a1[:, w * 16:(w + 1) * 16], in0=modcol(0, 1, w),
                                                        scalar1=1.0, scalar2=None, op0=ALU.add),
                  reads=[mod_b], writes=[a1b])
        for ti, (t0, T) in enumerate(TILES):
            w = 0 if t0 < NLAT else 1
            hb_, hin = slots[ti % 2]
            ob_, uo = outs[ti % 2]
            kb.dma("sp", hin.rearrange("p (k t) -> p k t", t=512)[:, :, :T],
                   xT[:, t0:t0 + T].rearrange("(k p) t -> p k t", p=128), writes=[hb_])
            for c in range(16):
                s = modidx(0, 0, c, w)
                kb.op("dve", lambda e, hin=hin, uo=uo, c=c, w=w, s=s, T=T: e.tensor_scalar(
                    out=uo[:, c * 512:c * 512 + T], in0=hin[:, c * 512:c * 512 + T],
                    scalar1=a1[:, w * 16 + c:w * 16 + c + 1], scalar2=MOD[:, s:s + 1], op0=ALU.mult, op1=ALU.add),
                    reads=[hb_, a1b, mod_b], writes=[ob_])
            kb.dma("sp", U[:, t0:t0 + T].rearrange("(k p) t -> p k t", p=128),
                   uo.rearrange("p (k t) -> p k t", t=512)[:, :, :T], reads=[ob_])
        kb.barrier()

    def phase_out_ln(l, which_ln, Xsrc, KC, Wsrc, tiles, has_next, final=False):
        tabs = ln_tables(l, which_ln, has_next)
        tb = tabs["bufs"]
        nb = 256 if KC <= 16 else 128
        ws = WStream(3, KC * nb)
        nx = 2 if KC <= 16 else 1
        xs = [kb.alloc16(KC * 512, "x%d" % i) for i in range(nx)]
        zs = [kb.alloc32(16 * 512, "Z%d" % i) for i in range(2)]
        sq = [kb.alloc32(512, "sq%d" % i) for i in range(2)]
        mb_, MEAN = kb.alloc32(512, "mean")
        m2b, M2 = kb.alloc32(512, "m2")
        rb_, RSTD = kb.alloc32(512, "rstd")
        t1 = [kb.alloc32(512, "t1%d" % i) for i in range(2)]
        ho = [kb.alloc32(512, "ho%d" % i) for i in range(2)]
        uo = [kb.alloc16(512, "uo%d" % i) for i in range(2)]
        load_x(xs[0], Xsrc, tiles[0][0], tiles[0][1], KC)

        def load_z(ti):
            zb_, Z_ = zs[ti % 2]
            t0_, T_ = tiles[ti]
            kb.dma("sp", Z_.rearrange("p (k t) -> p k t", t=512)[:, :, :T_],
                   H[:, t0_:t0_ + T_].rearrange("(k p) t -> p k t", p=128), writes=[zb_])

        load_z(0)
        for ti, (t0, T) in enumerate(tiles):
            w = 0 if t0 < NLAT else 1
            xb, X = xs[ti % nx]
            zb, Z = zs[ti % 2]
            if nx == 2 and ti + 1 < len(tiles):
                load_x(xs[(ti + 1) % 2], Xsrc, tiles[ti + 1][0], tiles[ti + 1][1], KC)
            for blk in range(D // nb):
                wb, W = ws.load([(Wsrc[:, blk * nb:(blk + 1) * nb], 0, nb)], KC, nb)
                for mm in range(nb // 128):
                    c = blk * (nb // 128) + mm
                    pb, ps = kb.ps()
                    for kc in range(KC):
                        kb.op("pe", lambda e, Z=Z, ps=ps, W=W, X=X, kc=kc, mm=mm, T=T: e.matmul(
                            ps[:, :T], W[:, kc * nb + mm * 128:kc * nb + mm * 128 + 128],
                            X[:, kc * 512:kc * 512 + T], start=(kc == 0), stop=(kc == KC - 1)),
                            reads=[wb, xb], writes=[pb])
                    ga = tabs["GA%d" % w]
                    kb.op("dve", lambda e, Z=Z, ps=ps, c=c, T=T, ga=ga: e.scalar_tensor_tensor(
                        out=Z[:, c * 512:c * 512 + T], in0=ps[:, :T], scalar=ga[:, c:c + 1],
                        in1=Z[:, c * 512:c * 512 + T], op0=ALU.mult, op1=ALU.add),
                        reads=[pb] + tb, writes=[zb])
            if nx == 1 and ti + 1 < len(tiles):
                load_x(xs[0], Xsrc, tiles[ti + 1][0], tiles[ti + 1][1], KC)
            if ti + 1 < len(tiles):
                load_z(ti + 1)
            pb1, ps1 = kb.ps()
            pb2, ps2 = kb.ps()
            for c in range(16):
                kb.op("pe", lambda e, Z=Z, ps1=ps1, c=c, T=T: e.matmul(
                    ps1[:, :T], ones32, Z[:, c * 512:c * 512 + T], start=(c == 0), stop=(c == 15)),
                    reads=[ones_b, zb], writes=[pb1])
                qb, q = sq[c % 2]
                kb.op("act", lambda e, Z=Z, q=q, c=c, T=T: e.activation(
                    out=q[:, :T], in_=Z[:, c * 512:c * 512 + T], func=AF.Square), reads=[zb], writes=[qb])
                kb.op("pe", lambda e, Z=Z, ps2=ps2, q=q, c=c, T=T: e.matmul(
                    ps2[:, :T], ones32, q[:, :T], start=(c == 0), stop=(c == 15)),
                    reads=[ones_b, qb], writes=[pb2])
            kb.op("dve", lambda e, Z=Z, ps1=ps1, T=T: e.tensor_scalar(
                out=MEAN[:, :T], in0=ps1[:, :T], scalar1=1.0 / D, scalar2=None, op0=ALU.mult),
                reads=[pb1], writes=[mb_])
            kb.op("dve", lambda e, Z=Z, T=T: e.tensor_tensor(out=M2[:, :T], in0=MEAN[:, :T], in1=MEAN[:, :T], op=ALU.mult),
                  reads=[mb_], writes=[m2b])
            kb.op("dve", lambda e, Z=Z, ps2=ps2, T=T: e.scalar_tensor_tensor(
                out=M2[:, :T], in0=ps2[:, :T], scalar=1.0 / D, in1=M2[:, :T], op0=ALU.mult, op1=ALU.subtract),
                reads=[pb2, m2b], writes=[m2b])
            kb.op("dve", lambda e, Z=Z, T=T: e.tensor_scalar(
                out=M2[:, :T], in0=M2[:, :T], scalar1=LN_EPS_EFF, scalar2=None, op0=ALU.add),
                reads=[m2b], writes=[m2b])
            kb.op("act", lambda e, Z=Z, T=T: e.activation(out=RSTD[:, :T], in_=M2[:, :T], func=AF.Sqrt),
                  reads=[m2b], writes=[rb_])
            kb.op("dve", lambda e, Z=Z, T=T: e.reciprocal(out=RSTD[:, :T], in_=RSTD[:, :T]), reads=[rb_], writes=[rb_])
            for c in range(16):
                tb1, T1 = t1[c % 2]
                hb1, HO = ho[c % 2]
                ub1, UO = uo[c % 2]
                kb.op("dve", lambda e, Z=Z, T1=T1, c=c, T=T: e.tensor_tensor(
                    out=T1[:, :T], in0=Z[:, c * 512:c * 512 + T], in1=MEAN[:, :T], op=ALU.subtract),
                    reads=[zb, mb_], writes=[tb1])
                kb.op("dve", lambda e, Z=Z, T1=T1, T=T: e.tensor_tensor(
                    out=T1[:, :T], in0=T1[:, :T], in1=RSTD[:, :T], op=ALU.mult), reads=[tb1, rb_], writes=[tb1])
                kb.op("act", lambda e, Z=Z, T1=T1, HO=HO, c=c, T=T: e.activation(
                    out=HO[:, :T], in_=T1[:, :T], func=AF.Identity, scale=tabs["G"][:, c:c + 1],
                    bias=tabs["B"][:, c:c + 1]), reads=[tb1] + tb, writes=[hb1])
                if final:
                    kb.dma("sp", outT[c * 128:(c + 1) * 128, t0:t0 + T], HO[:, :T], reads=[hb1])
                else:
                    kb.dma("sp", H[c * 128:(c + 1) * 128, t0:t0 + T], HO[:, :T], reads=[hb1])
                if has_next:
                    ug, ub = tabs["UG%d" % w], tabs["UB%d" % w]
                    kb.op("act", lambda e, Z=Z, T1=T1, UO=UO, c=c, T=T, ug=ug, ub=ub: e.activation(
                        out=UO[:, :T], in_=T1[:, :T], func=AF.Identity, scale=ug[:, c:c + 1],
                        bias=ub[:, c:c + 1]), reads=[tb1] + tb, writes=[ub1])
                    kb.dma("sp", U[c * 128:(c + 1) * 128, t0:t0 + T], UO[:, :T], reads=[ub1])
        kb.barrier()

    def phase_ffn_in(l, tiles):
        ws = WStream(3, 16 * 512)
        xs = [kb.alloc16(16 * 512, "x%d" % i) for i in range(4)]
        sa = [kb.alloc32(512, "sa%d" % i) for i in range(2)]
        hd = [kb.alloc16(512, "hd%d" % i) for i in range(3)]
        Wsrc = ffn_w_in[l]
        groups = [tiles[i:i + 2] for i in range(0, len(tiles), 2)]
        for k, (t0, T) in enumerate(groups[0]):
            load_x(xs[k], U, t0, T, 16)
        hi = 0
        si = 0
        for gi, grp in enumerate(groups):
            if gi + 1 < len(groups):
                for k, (t0, T) in enumerate(groups[gi + 1]):
                    load_x(xs[((gi + 1) % 2) * 2 + k], U, t0, T, 16)
            for blk in range(22):
                wb, W = ws.load([(Wsrc[:, blk * 256:(blk + 1) * 256], 0, 256),
                                 (Wsrc[:, FH + blk * 256:FH + (blk + 1) * 256], 256, 256)], 16, 512)
                for k, (t0, T) in enumerate(grp):
                    xb, X = xs[(gi % 2) * 2 + k]
                    for jj in range(2):
                        j = blk * 2 + jj
                        pa, psa = kb.ps()
                        pb, psb = kb.ps()
                        for (pp, pps, off) in ((pa, psa, jj * 128), (pb, psb, 256 + jj * 128)):
                            for kc in range(16):
                                kb.op("pe", lambda e, pps=pps, W=W, X=X, kc=kc, off=off, T=T: e.matmul(
                                    pps[:, :T], W[:, kc * 512 + off:kc * 512 + off + 128],
                                    X[:, kc * 512:kc * 512 + T], start=(kc == 0), stop=(kc == 15)),
                                    reads=[wb, xb], writes=[pp])
                        sb_, SA = sa[si % 2]
                        si += 1
                        hb_, HD = hd[hi % 3]
                        hi += 1
                        kb.op("act", lambda e, SA=SA, psa=psa, T=T: e.activation(out=SA[:, :T], in_=psa[:, :T], func=AF.Silu),
                              reads=[pa], writes=[sb_])
                        kb.op("dve", lambda e, SA=SA, HD=HD, psb=psb, T=T: e.tensor_tensor(
                            out=HD[:, :T], in0=psb[:, :T], in1=SA[:, :T], op=ALU.mult), reads=[pb, sb_], writes=[hb_])
                        kb.dma("sp", HID[j * 128:(j + 1) * 128, t0:t0 + T], HD[:, :T], reads=[hb_])
        kb.barrier()

    def phase_ret_proj(j):
        ws = WStream(3, 16 * 512)
        xs = [kb.alloc16(16 * 512, "x%d" % i) for i in range(4)]
        tabs_ = [kb.alloc32(4 * 512, "rtab%d" % i) for i in range(2)]
        tm = [kb.alloc32(512, "tm%d" % i) for i in range(4)]
        o16 = [kb.alloc16(512, "o%d" % i) for i in range(4)]
        Wq = ret_w_qkv[j]
        Wg = ret_w_g[j]
        groups = [TILES[i:i + 2] for i in range(0, len(TILES), 2)]
        for k, (t0, T) in enumerate(groups[0]):
            load_x(xs[k], U, t0, T, 16)
        oi = 0
        for gi, grp in enumerate(groups):
            if gi + 1 < len(groups):
                for k, (t0, T) in enumerate(groups[gi + 1]):
                    load_x(xs[((gi + 1) % 2) * 2 + k], U, t0, T, 16)
            for k, (t0, T) in enumerate(grp):
                if t0 < NLAT:
                    tabb, TAB = tabs_[k]
                    kb.dma("sp", TAB.rearrange("p (a t) -> p a t", t=512),
                           rtab[:, :, t0:t0 + T].rearrange("a p t -> p a t"), writes=[tabb])
            for hk in range(16):
                isk = hk >= 8
                dst = KT if isk else QT
                h = hk % 8
                wb, W = ws.load([(Wq[:, hk * 256:(hk + 1) * 256], 0, 256)], 16, 256)
                for k, (t0, T) in enumerate(grp):
                    lat = t0 < NLAT
                    xb, X = xs[(gi % 2) * 2 + k]
                    tabb, TAB = tabs_[k]
                    p1, ps1 = kb.ps()
                    p2, ps2 = kb.ps()
                    for (pp, pps, off) in ((p1, ps1, 0), (p2, ps2, 128)):
                        for kc in range(16):
                            kb.op("pe", lambda e, pps=pps, W=W, X=X, kc=kc, off=off, T=T: e.matmul(
                                pps[:, :T], W[:, kc * 256 + off:kc * 256 + off + 128],
                                X[:, kc * 512:kc * 512 + T], start=(kc == 0), stop=(kc == 15)),
                                reads=[wb, xb], writes=[pp])
                    ob1, O1 = o16[oi % 4]
                    ob2, O2 = o16[(oi + 1) % 4]
                    oi += 2
                    if lat:
                        cs = TAB[:, (2 if isk else 0) * 512:(2 if isk else 0) * 512 + T]
                        sn = TAB[:, (3 if isk else 1) * 512:(3 if isk else 1) * 512 + T]
                        (b1, A1), (b2, A2), (b3, A3), (b4, A4) = tm
                        kb.op("dve", lambda e, A1=A1, ps1=ps1, cs=cs, T=T: e.tensor_tensor(out=A1[:, :T], in0=ps1[:, :T], in1=cs, op=ALU.mult),
                              reads=[p1, tabb], writes=[b1])
                        kb.op("dve", lambda e, A2=A2, ps2=ps2, sn=sn, T=T: e.tensor_tensor(out=A2[:, :T], in0=ps2[:, :T], in1=sn, op=ALU.mult),
                              reads=[p2, tabb], writes=[b2])
                        kb.op("dve", lambda e, A1=A1, A2=A2, O1=O1, T=T: e.tensor_tensor(out=O1[:, :T], in0=A1[:, :T], in1=A2[:, :T], op=ALU.subtract),
                              reads=[b1, b2], writes=[ob1])
                        kb.op("dve", lambda e, A3=A3, ps1=ps1, sn=sn, T=T: e.tensor_tensor(out=A3[:, :T], in0=ps1[:, :T], in1=sn, op=ALU.mult),
                              reads=[p1, tabb], writes=[b3])
                        kb.op("dve", lambda e, A4=A4, ps2=ps2, cs=cs, T=T: e.tensor_tensor(out=A4[:, :T], in0=ps2[:, :T], in1=cs, op=ALU.mult),
                              reads=[p2, tabb], writes=[b4])
                        kb.op("dve", lambda e, A3=A3, A4=A4, O2=O2, T=T: e.tensor_tensor(out=O2[:, :T], in0=A3[:, :T], in1=A4[:, :T], op=ALU.add),
                              reads=[b3, b4], writes=[ob2])
                    else:
                        sc = 1.0 / 16.0 if isk else 1.0
                        kb.op("act", lambda e, O1=O1, ps1=ps1, T=T, sc=sc: e.activation(out=O1[:, :T], in_=ps1[:, :T], func=AF.Copy, scale=sc),
                              reads=[p1], writes=[ob1])
                        kb.op("act", lambda e, O2=O2, ps2=ps2, T=T, sc=sc: e.activation(out=O2[:, :T], in_=ps2[:, :T], func=AF.Copy, scale=sc),
                              reads=[p2], writes=[ob2])
                    kb.dma("sp", dst[h * 256:h * 256 + 128, t0:t0 + T], O1[:, :T], reads=[ob1])
                    kb.dma("sp", dst[h * 256 + 128:h * 256 + 256, t0:t0 + T], O2[:, :T], reads=[ob2])
            for cb_ in range(24):
                if cb_ < 8:
                    src = Wq[:, 4096 + cb_ * 512:4096 + (cb_ + 1) * 512]
                else:
                    src = Wg[:, (cb_ - 8) * 512:(cb_ - 7) * 512]
                wb, W = ws.load([(src, 0, 512)], 16, 512)
                for k, (t0, T) in enumerate(grp):
                    xb, X = xs[(gi % 2) * 2 + k]
                    for tq in range(T // 128):
                        pp, pps = kb.ps()
                        for kc in range(16):
                            kb.op("pe", lambda e, pps=pps, W=W, X=X, kc=kc, tq=tq: e.matmul(
                                pps, X[:, kc * 512 + tq * 128:kc * 512 + tq * 128 + 128],
                                W[:, kc * 512:(kc + 1) * 512], start=(kc == 0), stop=(kc == 15)),
                                reads=[wb, xb], writes=[pp])
                        ob, O = o16[oi % 4]
                        oi += 1
                        fn = AF.Copy if cb_ < 8 else AF.Silu
                        kb.op("act", lambda e, O=O, pps=pps, fn=fn: e.activation(out=O, in_=pps, func=fn),
                              reads=[pp], writes=[ob])
                        r0 = t0 + tq * 128
                        if cb_ < 8:
                            kb.dma("sp", V[r0:r0 + 128, cb_ * 512:(cb_ + 1) * 512], O, reads=[ob])
                        else:
                            kb.dma("sp", G[r0:r0 + 128, (cb_ - 8) * 512:(cb_ - 7) * 512], O, reads=[ob])
        kb.barrier()

    def phase_ret_scan(j):
        dlb, DL = kb.alloc32(16, "dl")
        kb.dma("sp", DL, dlog[j], writes=[dlb])
        kb.op("act", lambda e: e.activation(out=DL, in_=DL, func=AF.Exp, scale=-1.0), reads=[dlb], writes=[dlb])
        kb.op("act", lambda e: e.activation(out=DL, in_=DL, func=AF.Ln, bias=1.0), reads=[dlb], writes=[dlb])
        ixb, IXI = kb.alloc(4 * 128, "ixi")
        IXI = IXI.bitcast(I32)
        dfb, DIFF = kb.alloc32(128, "diff")
        kb.op("pool", lambda e: e.iota(IXI, pattern=[[1, 128]], base=0, channel_multiplier=-1), writes=[ixb])
        kb.op("dve", lambda e: e.tensor_copy(out=DIFF, in_=IXI), reads=[ixb], writes=[dfb])
        pcb, PC = kb.alloc32(4, "pc")
        kb.op("dve", lambda e: e.tensor_scalar(out=PC[:, 3:4], in0=DIFF[:, 0:1], scalar1=-1.0, scalar2=None, op0=ALU.mult),
              reads=[dfb], writes=[pcb])
        kb.op("dve", lambda e: e.tensor_scalar(out=PC[:, 0:1], in0=PC[:, 3:4], scalar1=1.0, scalar2=None, op0=ALU.add),
              reads=[pcb], writes=[pcb])
        kb.op("dve", lambda e: e.tensor_scalar(out=PC[:, 1:2], in0=DIFF[:, 0:1], scalar1=127.0, scalar2=None, op0=ALU.add),
              reads=[dfb], writes=[pcb])
        kb.op("dve", lambda e: e.tensor_scalar(out=PC[:, 2:3], in0=DIFF[:, 0:1], scalar1=128.0, scalar2=None, op0=ALU.add),
              reads=[dfb], writes=[pcb])
        agb, ARG = kb.alloc32(48, "arg")
        decb, DEC = kb.alloc32(48, "dec")
        rqb, RQD = kb.alloc32(16, "rqd")
        for d in range(2):
            qc = PC[:, 0:1] if d == 0 else PC[:, 2:3]
            kc_ = PC[:, 1:2] if d == 0 else PC[:, 3:4]
            kb.op("dve", lambda e, d=d, qc=qc: e.tensor_scalar(out=ARG[:, d * 8:d * 8 + 8], in0=DL[:, d * 8:d * 8 + 8],
                                                               scalar1=qc, scalar2=None, op0=ALU.mult),
                  reads=[dlb, pcb], writes=[agb])
            kb.op("dve", lambda e, d=d, kc_=kc_: e.tensor_scalar(out=ARG[:, 16 + d * 8:16 + d * 8 + 8], in0=DL[:, d * 8:d * 8 + 8],
                                                                 scalar1=kc_, scalar2=None, op0=ALU.mult),
                  reads=[dlb, pcb], writes=[agb])
        kb.op("dve", lambda e: e.tensor_scalar(out=ARG[:, 32:48], in0=DL, scalar1=128.0, scalar2=None, op0=ALU.mult),
              reads=[dlb], writes=[agb])
        kb.op("act", lambda e: e.activation(out=DEC, in_=ARG, func=AF.Exp, scale=-1.0), reads=[agb], writes=[decb])
        kb.op("act", lambda e: e.activation(out=RQD, in_=ARG[:, 0:16], func=AF.Exp, scale=1.0), reads=[agb], writes=[rqb])
        inb, IND = kb.alloc32(256, "ind")
        kb.op("dve", lambda e: e.tensor_scalar(out=IND[:, 0:128], in0=DIFF, scalar1=0.0, scalar2=None, op0=ALU.is_ge),
              reads=[dfb], writes=[inb])
        kb.op("dve", lambda e: e.tensor_scalar(out=IND[:, 128:256], in0=DIFF, scalar1=0.0, scalar2=None, op0=ALU.is_le),
              reads=[dfb], writes=[inb])

        S_h = [kb.alloc32(2 * 512, "S%d" % h) for h in range(8)]
        Sbf_h = [kb.alloc16(2 * 512, "Sbf%d" % h) for h in range(8)]
        ld = []
        for i in range(2):
            ld.append(dict(q=kb.alloc16(2048, "q%d" % i), k=kb.alloc16(2048, "k%d" % i),
                           v=kb.alloc16(4096, "v%d" % i), g=kb.alloc16(4096, "g%d" % i)))
        KD_h = [kb.alloc16(256, "kd%d" % h) for h in range(8)]
        pts = [kb.alloc16(128, "pt%d" % i) for i in range(2)]
        O1_h = [kb.alloc32(512, "O1%d" % h) for h in range(8)]
        o2 = [kb.alloc32(512, "o2%d" % i) for i in range(2)]
        ydb, YD = kb.alloc32(4096, "YD")
        yfb, YFc = kb.alloc32(4096, "YFc")
        ybb, Yb = kb.alloc16(4096, "Yb")
        yts = [kb.alloc16(1024, "yt%d" % i) for i in range(2)]
        stb, ST = kb.alloc32(8 * 6, "st")
        mvb, MV = kb.alloc32(8 * 2, "mv")
        rsb, RS = kb.alloc32(8, "rs")
        nbb, NB = kb.alloc32(8, "nb")
        orders = [[32, 33] + list(range(32)), [33, 32] + list(range(31, -1, -1))]
        seq = [(d, n, ck) for d in range(2) for n, ck in enumerate(orders[d])]

        def issue_loads(i):
            d_, n_, ck_ = seq[i]
            cur_ = ld[i % 2]
            t_ = ck_ * 128
            (qb_, Q_), (kbb_, K_), (vb_, V_), (gb_, G_) = cur_["q"], cur_["k"], cur_["v"], cur_["g"]
            kb.dma("sp", Q_.rearrange("p (c t) -> p c t", t=128), QT[:, t_:t_ + 128].rearrange("(c p) t -> p c t", p=128), writes=[qb_])
            kb.dma("sp", K_.rearrange("p (c t) -> p c t", t=128), KT[:, t_:t_ + 128].rearrange("(c p) t -> p c t", p=128), writes=[kbb_])
            kb.dma("sp", V_, V[t_:t_ + 128, :], writes=[vb_])
            kb.dma("sp", G_, G[t_:t_ + 128, d_ * 4096:(d_ + 1) * 4096], writes=[gb_])

        issue_loads(0)
        for i, (d, n, ck) in enumerate(seq):
            first = (n == 0)
            tok0 = ck * 128
            cur = ld[i % 2]
            (qb, Q), (kb_, K), (vb, Vc), (gb, Gc) = cur["q"], cur["k"], cur["v"], cur["g"]
            if i + 1 < len(seq):
                issue_loads(i + 1)
            if d == 1:
                kb.dma("sp", YFc, YF[tok0:tok0 + 128, :], writes=[yfb])
            for half in range(2):
                pb, ps = kb.ps()
                psT = ps.bitcast(BF16)
                for c8 in range(8):
                    c = half * 8 + c8
                    kb.op("pe", lambda e, psT=psT, K=K, c=c, c8=c8: e.transpose(
                        psT[:, c8 * 128:(c8 + 1) * 128], K[:, c * 128:(c + 1) * 128], ident),
                        reads=[kb_, id_b], writes=[pb])
                for h4 in range(4):
                    h = half * 4 + h4
                    kdb, KD = KD_h[h]
                    kb.op("act", lambda e, psT=psT, h=h, h4=h4, d=d, KD=KD: e.activation(
                        out=KD, in_=psT[:, h4 * 256:(h4 + 1) * 256], func=AF.Copy,
                        scale=DEC[:, 16 + d * 8 + h:16 + d * 8 + h + 1]), reads=[pb, decb], writes=[kdb])
            for h in range(8):
                kdb, KD = KD_h[h]
                Sb, S = S_h[h]
                Sbb, Sbf = Sbf_h[h]
                o1b, O1 = O1_h[h]
                pbs, pss = kb.ps()
                for dc in range(2):
                    c = 2 * h + dc
                    kb.op("pe", lambda e, pss=pss, K=K, Q=Q, c=c, dc=dc: e.matmul(
                        pss[:, :128], K[:, c * 128:(c + 1) * 128], Q[:, c * 128:(c + 1) * 128],
                        start=(dc == 0), stop=(dc == 1)), reads=[kb_, qb], writes=[pbs])
                ptb, PT = pts[h % 2]
                kb.op("dve", lambda e, PT=PT, pss=pss, d=d, h=h: e.scalar_tensor_tensor(
                    out=PT, in0=pss[:, :128], scalar=RQD[:, d * 8 + h:d * 8 + h + 1],
                    in1=IND[:, d * 128:(d + 1) * 128], op0=ALU.mult, op1=ALU.mult),
                    reads=[pbs, rqb, inb], writes=[ptb])
                pbo, pso = kb.ps()
                kb.op("pe", lambda e, pso=pso, PT=PT, Vc=Vc, h=h, first=first: e.matmul(
                    pso, PT, Vc[:, h * 512:(h + 1) * 512], start=True, stop=first), reads=[ptb, vb], writes=[pbo])
                if not first:
                    for dc in range(2):
                        c = 2 * h + dc
                        kb.op("pe", lambda e, pso=pso, Q=Q, c=c, dc=dc, Sbf=Sbf: e.matmul(
                            pso, Q[:, c * 128:(c + 1) * 128], Sbf[:, dc * 512:(dc + 1) * 512],
                            start=False, stop=(dc == 1)), reads=[qb, Sbb], writes=[pbo])
                kb.op("act", lambda e, pso=pso, h=h, d=d, O1=O1: e.activation(
                    out=O1, in_=pso, func=AF.Copy, scale=DEC[:, d * 8 + h:d * 8 + h + 1]),
                    reads=[pbo, decb], writes=[o1b])
                for dc in range(2):
                    pbu, psu = kb.ps()
                    kb.op("pe", lambda e, psu=psu, h=h, dc=dc, Vc=Vc, KD=KD: e.matmul(
                        psu, KD[:, dc * 128:dc * 128 + 128], Vc[:, h * 512:(h + 1) * 512],
                        start=True, stop=True), reads=[kdb, vb], writes=[pbu])
                    sl = S[:, dc * 512:(dc + 1) * 512]
                    if first:
                        kb.op("dve", lambda e, sl=sl, psu=psu: e.tensor_copy(out=sl, in_=psu), reads=[pbu], writes=[Sb])
                    else:
                        kb.op("dve", lambda e, sl=sl, psu=psu, d=d, h=h: e.scalar_tensor_tensor(
                            out=sl, in0=sl, scalar=DEC[:, 32 + d * 8 + h:32 + d * 8 + h + 1], in1=psu,
                            op0=ALU.mult, op1=ALU.add), reads=[pbu, Sb, decb], writes=[Sb])
                    kb.op("act", lambda e, sl=sl, dc=dc, Sbf=Sbf: e.activation(
                        out=Sbf[:, dc * 512:(dc + 1) * 512], in_=sl, func=AF.Copy),
                        reads=[Sb], writes=[Sbb])
            for h in range(8):
                o1b, O1 = O1_h[h]
                kb.op("dve", lambda e, h=h, O1=O1: e.bn_stats(out=ST[:, h * 6:(h + 1) * 6], in_=O1),
                      reads=[o1b], writes=[stb])
            for h in range(8):
                kb.op("dve", lambda e, h=h: e.bn_aggr(out=MV[:, h * 2:(h + 1) * 2], in_=ST[:, h * 6:(h + 1) * 6]),
                      reads=[stb], writes=[mvb])
            kb.op("dve", lambda e: e.tensor_scalar(out=RS, in0=MV[:, 1:16:2], scalar1=GN_EPS, scalar2=None, op0=ALU.add),
                  reads=[mvb], writes=[rsb])
            kb.op("act", lambda e: e.activation(out=RS, in_=RS, func=AF.Sqrt), reads=[rsb], writes=[rsb])
            kb.op("dve", lambda e: e.reciprocal(out=RS, in_=RS), reads=[rsb], writes=[rsb])
            kb.op("dve", lambda e: e.scalar_tensor_tensor(out=NB, in0=MV[:, 0:16:2], scalar=-1.0, in1=RS,
                                                          op0=ALU.mult, op1=ALU.mult), reads=[mvb, rsb], writes=[nbb])
            for h in range(8):
                o1b, O1 = O1_h[h]
                ob, O2 = o2[h % 2]
                kb.op("act", lambda e, O2=O2, h=h, O1=O1: e.activation(
                    out=O2, in_=O1, func=AF.Identity, scale=RS[:, h:h + 1], bias=NB[:, h:h + 1]),
                    reads=[o1b, rsb, nbb], writes=[ob])
                kb.op("dve", lambda e, O2=O2, h=h, Gc=Gc: e.tensor_tensor(
                    out=YD[:, h * 512:(h + 1) * 512], in0=O2, in1=Gc[:, h * 512:(h + 1) * 512], op=ALU.mult),
                    reads=[ob, gb], writes=[ydb])
            if d == 0:
                kb.dma("sp", YF[tok0:tok0 + 128, :], YD, reads=[ydb])
            else:
                kb.op("dve", lambda e: e.tensor_tensor(out=Yb, in0=YD, in1=YFc, op=ALU.add), reads=[ydb, yfb], writes=[ybb])
                for g4 in range(4):
                    pb, ps = kb.ps()
                    psT = ps.bitcast(BF16)
                    for c8 in range(8):
                        c = g4 * 8 + c8
                        kb.op("pe", lambda e, psT=psT, c=c, c8=c8: e.transpose(
                            psT[:, c8 * 128:(c8 + 1) * 128], Yb[:, c * 128:(c + 1) * 128], ident),
                            reads=[ybb, id_b], writes=[pb])
                    tb_, YTt = yts[g4 % 2]
                    kb.op("act", lambda e, YTt=YTt, psT=psT: e.activation(out=YTt, in_=psT, func=AF.Copy),
                          reads=[pb], writes=[tb_])
                    kb.dma("sp", YT[g4 * 1024:(g4 + 1) * 1024, tok0:tok0 + 128].rearrange("(c p) t -> p c t", p=128),
                           YTt.rearrange("p (c t) -> p c t", t=128), reads=[tb_])
        kb.barrier()

    def phase_mla_down(j):
        ws = WStream(3, 16 * 256)
        xs = [kb.alloc16(16 * 512, "x%d" % i) for i in range(2)]
        gqb, GQ = kb.alloc32(8, "gq")
        kb.dma("sp", GQ[:, 0:4], g_qT[j], writes=[gqb])
        kb.dma("sp", GQ[:, 4:8], g_kvT[j], writes=[gqb])
        mtb, MT = kb.alloc32(2 * 512, "mtab")
        cb_, C = kb.alloc32(4 * 512, "C")
        sq = [kb.alloc32(512, "sq%d" % i) for i in range(2)]
        rb_, R = kb.alloc32(512, "R")
        on = [kb.alloc16(512, "on%d" % i) for i in range(3)]
        tm = [kb.alloc32(512, "tm%d" % i) for i in range(2)]
        load_x(xs[0], U, TILES[0][0], TILES[0][1], 16)
        oi = 0
        for ti, (t0, T) in enumerate(TILES):
            lat = t0 < NLAT
            xb, X = xs[ti % 2]
            if ti + 1 < len(TILES):
                load_x(xs[(ti + 1) % 2], U, TILES[ti + 1][0], TILES[ti + 1][1], 16)
            if lat:
                kb.dma("sp", MT.rearrange("p (a t) -> p a t", t=512)[0:64], mtab[:, :, t0:t0 + T].rearrange("a p t -> p a t"),
                       writes=[mtb])
            for which, (Wsrc, dst, goff) in enumerate(((w_dq[j], CQN, 0), (w_dkv[j], CKVN, 4))):
                for blk in range(2):
                    wb, W = ws.load([(Wsrc[:, blk * 256:(blk + 1) * 256], 0, 256)], 16, 256)
                    for mm in range(2):
                        c = blk * 2 + mm
                        pb, ps = kb.ps()
                        for kc in range(16):
                            kb.op("pe", lambda e, ps=ps, W=W, X=X, kc=kc, mm=mm, T=T: e.matmul(
                                ps[:, :T], W[:, kc * 256 + mm * 128:kc * 256 + mm * 128 + 128],
                                X[:, kc * 512:kc * 512 + T], start=(kc == 0), stop=(kc == 15)), reads=[wb, xb], writes=[pb])
                        kb.op("act", lambda e, ps=ps, c=c, T=T: e.activation(out=C[:, c * 512:c * 512 + T], in_=ps[:, :T], func=AF.Copy),
                              reads=[pb], writes=[cb_])
                pbs, pss = kb.ps()
                for c in range(4):
                    qb, q = sq[c % 2]
                    kb.op("act", lambda e, q=q, c=c, T=T: e.activation(out=q[:, :T], in_=C[:, c * 512:c * 512 + T], func=AF.Square),
                          reads=[cb_], writes=[qb])
                    kb.op("pe", lambda e, pss=pss, q=q, c=c, T=T: e.matmul(pss[:, :T], ones32, q[:, :T], start=(c == 0), stop=(c == 3)),
                          reads=[ones_b, qb], writes=[pbs])
                kb.op("dve", lambda e, pss=pss, T=T: e.tensor_scalar(out=R[:, :T], in0=pss[:, :T], scalar1=1.0 / 512, scalar2=RMS_EPS,
                                                                     op0=ALU.mult, op1=ALU.add), reads=[pbs], writes=[rb_])
                kb.op("act", lambda e, T=T: e.activation(out=R[:, :T], in_=R[:, :T], func=AF.Sqrt), reads=[rb_], writes=[rb_])
                kb.op("dve", lambda e, T=T: e.reciprocal(out=R[:, :T], in_=R[:, :T]), reads=[rb_], writes=[rb_])
                for c in range(4):
                    ob, O = on[oi % 3]
                    oi += 1
                    kb.op("dve", lambda e, O=O, c=c, T=T, goff=goff: e.scalar_tensor_tensor(
                        out=O[:, :T], in0=C[:, c * 512:c * 512 + T], scalar=GQ[:, goff + c:goff + c + 1], in1=R[:, :T],
                        op0=ALU.mult, op1=ALU.mult), reads=[cb_, gqb, rb_], writes=[ob])
                    kb.dma("sp", dst[c * 128:(c + 1) * 128, t0:t0 + T], O[:, :T], reads=[ob])
            wb, W = ws.load([(w_dkv[j][:, 512:640], 0, 128)], 16, 128)
            pa, psa = kb.ps()
            pb, psb = kb.ps()
            for (pp, pps, off) in ((pa, psa, 0), (pb, psb, 64)):
                for kc in range(16):
                    kb.op("pe", lambda e, pps=pps, W=W, X=X, kc=kc, off=off, T=T: e.matmul(
                        pps[0:64, :T], W[:, kc * 128 + off:kc * 128 + off + 64], X[:, kc * 512:kc * 512 + T],
                        start=(kc == 0), stop=(kc == 15)), reads=[wb, xb], writes=[pp])
            ob, O = on[oi % 3]
            oi += 1
            if lat:
                (b1, A1), (b2, A2) = tm
                kb.op("dve", lambda e, A1=A1, psa=psa, T=T: e.tensor_tensor(out=A1[0:64, :T], in0=psa[0:64, :T], in1=MT[0:64, 0:T], op=ALU.mult),
                      reads=[pa, mtb], writes=[b1])
                kb.op("dve", lambda e, A2=A2, psb=psb, T=T: e.tensor_tensor(out=A2[0:64, :T], in0=psb[0:64, :T], in1=MT[0:64, 512:512 + T], op=ALU.mult),
                      reads=[pb, mtb], writes=[b2])
                kb.op("dve", lambda e, A1=A1, A2=A2, O=O, T=T: e.tensor_tensor(out=O[0:64, :T], in0=A1[0:64, :T], in1=A2[0:64, :T], op=ALU.add),
                      reads=[b1, b2], writes=[ob])
            else:
                kb.op("act", lambda e, O=O, psa=psa, T=T: e.activation(out=O[0:64, :T], in_=psa[0:64, :T], func=AF.Copy),
                      reads=[pa], writes=[ob])
            kb.dma("sp", KRT[:, t0:t0 + T], O[0:64, :T], reads=[ob])
        kb.barrier()

    def phase_mla_up(j, qtiles):
        ws = WStream(3, 4 * 512)
        xq = [kb.alloc16(4 * 512, "xq%d" % i) for i in range(2)]
        xk = [kb.alloc16(4 * 512, "xk%d" % i) for i in range(2)]
        mtb, MT = kb.alloc32(2 * 512, "mtab")
        on = [kb.alloc16(512, "on%d" % i) for i in range(4)]
        tm = [kb.alloc32(512, "tm%d" % i) for i in range(2)]
        oi = 0
        for ti, (t0, T) in enumerate(TILES):
            lat = t0 < NLAT
            doq = (t0, T) in qtiles
            xqb, XQ = xq[ti % 2]
            xkb, XK = xk[ti % 2]
            if doq:
                load_x((xqb, XQ), CQN, t0, T, 4)
            load_x((xkb, XK), CKVN, t0, T, 4)
            if lat:
                kb.dma("sp", MT.rearrange("p (a t) -> p a t", t=512)[0:64], mtab[:, :, t0:t0 + T].rearrange("a p t -> p a t"),
                       writes=[mtb])
            if doq:
                for hb2 in range(8):
                    wb, W = ws.load([(w_uq[j][:, hb2 * 512:(hb2 + 1) * 512], 0, 512)], 4, 512)
                    for hh in range(2):
                        h = hb2 * 2 + hh
                        pn, psn = kb.ps()
                        pa, psa = kb.ps()
                        pb, psb = kb.ps()
                        for (pp, pps, off, M) in ((pn, psn, hh * 256, 128), (pa, psa, hh * 256 + 128, 64), (pb, psb, hh * 256 + 192, 64)):
                            for kc in range(4):
                                kb.op("pe", lambda e, pps=pps, W=W, XQ=XQ, kc=kc, off=off, M=M, T=T: e.matmul(
                                    pps[0:M, :T], W[:, kc * 512 + off:kc * 512 + off + M], XQ[:, kc * 512:kc * 512 + T],
                                    start=(kc == 0), stop=(kc == 3)), reads=[wb, xqb], writes=[pp])
                        ob, O = on[oi % 4]
                        oi += 1
                        kb.op("act", lambda e, O=O, psn=psn, T=T: e.activation(out=O[:, :T], in_=psn[:, :T], func=AF.Copy),
                              reads=[pn], writes=[ob])
                        kb.dma("sp", QN[h * 128:(h + 1) * 128, t0:t0 + T], O[:, :T], reads=[ob])
                        ob, O = on[oi % 4]
                        oi += 1
                        if lat:
                            (b1, A1), (b2, A2) = tm
                            kb.op("dve", lambda e, A1=A1, psa=psa, T=T: e.tensor_tensor(out=A1[0:64, :T], in0=psa[0:64, :T], in1=MT[0:64, 0:T], op=ALU.mult),
                                  reads=[pa, mtb], writes=[b1])
                            kb.op("dve", lambda e, A2=A2, psb=psb, T=T: e.tensor_tensor(out=A2[0:64, :T], in0=psb[0:64, :T], in1=MT[0:64, 512:512 + T], op=ALU.mult),
                                  reads=[pb, mtb], writes=[b2])
                            kb.op("dve", lambda e, A1=A1, A2=A2, O=O, T=T: e.tensor_tensor(out=O[0:64, :T], in0=A1[0:64, :T], in1=A2[0:64, :T], op=ALU.add),
                                  reads=[b1, b2], writes=[ob])
                        else:
                            kb.op("act", lambda e, O=O, psa=psa, T=T: e.activation(out=O[0:64, :T], in_=psa[0:64, :T], func=AF.Copy),
                                  reads=[pa], writes=[ob])
                        kb.dma("sp", QR[h * 64:(h + 1) * 64, t0:t0 + T], O[0:64, :T], reads=[ob])
            for blk in range(4):
                wb, W = ws.load([(w_ukv_k[j][:, blk * 512:(blk + 1) * 512], 0, 512)], 4, 512)
                for mm in range(4):
                    h = blk * 4 + mm
                    pp, pps = kb.ps()
                    for kc in range(4):
                        kb.op("pe", lambda e, pps=pps, W=W, XK=XK, kc=kc, mm=mm, T=T: e.matmul(
                            pps[:, :T], W[:, kc * 512 + mm * 128:kc * 512 + mm * 128 + 128], XK[:, kc * 512:kc * 512 + T],
                            start=(kc == 0), stop=(kc == 3)), reads=[wb, xkb], writes=[pp])
                    ob, O = on[oi % 4]
                    oi += 1
                    kb.op("act", lambda e, O=O, pps=pps, T=T: e.activation(out=O[:, :T], in_=pps[:, :T], func=AF.Copy),
                          reads=[pp], writes=[ob])
                    kb.dma("sp", KN[h * 128:(h + 1) * 128, t0:t0 + T], O[:, :T], reads=[ob])
            for blk in range(4):
                wb, W = ws.load([(w_ukv_v[j][:, blk * 512:(blk + 1) * 512], 0, 512)], 4, 512)
                for tq in range(T // 128):
                    pp, pps = kb.ps()
                    for kc in range(4):
                        kb.op("pe", lambda e, pps=pps, W=W, XK=XK, kc=kc, tq=tq: e.matmul(
                            pps, XK[:, kc * 512 + tq * 128:kc * 512 + tq * 128 + 128], W[:, kc * 512:(kc + 1) * 512],
                            start=(kc == 0), stop=(kc == 3)), reads=[wb, xkb], writes=[pp])
                    ob, O = on[oi % 4]
                    oi += 1
                    kb.op("act", lambda e, O=O, pps=pps: e.activation(out=O, in_=pps, func=AF.Copy), reads=[pp], writes=[ob])
                    r0 = t0 + tq * 128
                    kb.dma("sp", VA[r0:r0 + 128, blk * 512:(blk + 1) * 512], O, reads=[ob])
        kb.barrier()

    def phase_mla_attn(qtiles):
        krb, KR = kb.alloc16(TOK, "KR")
        kb.dma("sp", KR[0:64, :], KRT, writes=[krb])
        kn = [kb.alloc16(TOK, "kn%d" % i) for i in range(2)]
        vh = [kb.alloc16(34 * 130, "vh%d" % i) for i in range(2)]
        for (vb, VH) in vh:
            kb.op("pool", lambda e, VH=VH: e.memset(VH, 1.0), writes=[vb])
        qn = [kb.alloc16(512, "qn%d" % i) for i in range(2)]
        qr = [kb.alloc16(512, "qr%d" % i) for i in range(2)]
        pt = [kb.alloc16(512, "pt%d" % i) for i in range(3)]
        rcb, RC = kb.alloc32(4, "rc")
        otm = [kb.alloc16(128, "otm%d" % i) for i in range(2)]
        ott = [kb.alloc16(512, "ott%d" % i) for i in range(2)]
        qi_ = 0
        pi = 0
        for h in range(16):
            knb, KNh = kn[h % 2]
            vb, VH = vh[h % 2]
            kb.dma("sp", KNh, KN[h * 128:(h + 1) * 128, :], writes=[knb])
            kb.dma("sp", VH.rearrange("p (c d) -> p c d", d=130)[:, :, 0:128],
                   VA[:, h * 128:(h + 1) * 128].rearrange("(c p) d -> p c d", p=128), writes=[vb])
            for (t0, T) in qtiles:
                keys = list(range(34)) if t0 < NLAT else [32, 33]
                qnb, QNt = qn[qi_ % 2]
                qrb, QRt = qr[qi_ % 2]
                qi_ += 1
                kb.dma("sp", QNt[:, :T], QN[h * 128:(h + 1) * 128, t0:t0 + T], writes=[qnb])
                kb.dma("sp", QRt[0:64, :T], QR[h * 64:(h + 1) * 64, t0:t0 + T], writes=[qrb])
                nq = T // 128
                pob = [(kb.pbufs[i], kb.psum[:, i, :]) for i in range((nq + 1) // 2)]
                for idx, kc in enumerate(keys):
                    si = 3 + (pi % 5)
                    pbs, pss = kb.pbufs[si], kb.psum[:, si, :]
                    kb.op("pe", lambda e, pss=pss, KNh=KNh, QNt=QNt, kc=kc, T=T: e.matmul(
                        pss[:, :T], KNh[:, kc * 128:(kc + 1) * 128], QNt[:, :T], start=True, stop=False),
                        reads=[knb, qnb], writes=[pbs])
                    kb.op("pe", lambda e, pss=pss, QRt=QRt, kc=kc, T=T: e.matmul(
                        pss[:, :T], KR[0:64, kc * 128:(kc + 1) * 128], QRt[0:64, :T], start=False, stop=True),
                        reads=[krb, qrb], writes=[pbs])
                    ptb, PT = pt[pi % 3]
                    pi += 1
                    kb.op("act", lambda e, PT=PT, pss=pss, T=T: e.activation(out=PT[:, :T], in_=pss[:, :T], func=AF.Exp, scale=MLA_SCALE),
                          reads=[pbs], writes=[ptb])
                    for qi in range(nq):
                        pbo, pso = pob[qi // 2]
                        kb.op("pe", lambda e, pso=pso, PT=PT, VH=VH, qi=qi, kc=kc, idx=idx, keys=keys: e.matmul(
                            pso[:, (qi % 2) * 129:(qi % 2) * 129 + 129], PT[:, qi * 128:(qi + 1) * 128],
                            VH[:, kc * 130:kc * 130 + 129], start=(idx == 0), stop=(idx == len(keys) - 1)),
                            reads=[ptb, vb], writes=[pbo])
                otb, OTt = ott[qi_ % 2]
                pbt, pst = kb.pbufs[2], kb.psum[:, 2, :]
                psT = pst.bitcast(BF16)
                for qi in range(nq):
                    pbo, pso = pob[qi // 2]
                    o0 = (qi % 2) * 129
                    kb.op("dve", lambda e, pso=pso, o0=o0, qi=qi: e.reciprocal(out=RC[:, qi:qi + 1], in_=pso[:, o0 + 128:o0 + 129]),
                          reads=[pbo], writes=[rcb])
                    ob, OM = otm[qi % 2]
                    kb.op("act", lambda e, OM=OM, pso=pso, o0=o0, qi=qi: e.activation(
                        out=OM, in_=pso[:, o0:o0 + 128], func=AF.Copy, scale=RC[:, qi:qi + 1]), reads=[pbo, rcb], writes=[ob])
                    kb.op("pe", lambda e, psT=psT, OM=OM, qi=qi: e.transpose(psT[:, qi * 128:(qi + 1) * 128], OM, ident),
                          reads=[ob, id_b], writes=[pbt])
                kb.op("dve", lambda e, OTt=OTt, psT=psT, T=T: e.tensor_copy(out=OTt[:, :T], in_=psT[:, :T]), reads=[pbt], writes=[otb])
                kb.dma("sp", OT[h * 128:(h + 1) * 128, t0:t0 + T], OTt[:, :T], reads=[otb])
        kb.barrier()

    phase_mod0()
    for l in range(nlayers):
        j = l // 2
        ctx_out = l < DEPTH - 1
        tiles = TILES if ctx_out else TILES[:8]
        if l % 2 == 0:
            phase_ret_proj(j)
            phase_ret_scan(j)
            phase_out_ln(l, 0, YT, 32, ret_w_o[j], tiles, True)
        else:
            phase_mla_down(j)
            phase_mla_up(j, tiles)
            phase_mla_attn(tiles)
            phase_out_ln(l, 0, OT, 16, mla_w_o[j], tiles, True)
        phase_ffn_in(l, tiles)
        last = (l == DEPTH - 1)
        phase_out_ln(l, 1, HID, 44, ffn_w_out[l], tiles, not last, final=last)
    if nlayers < DEPTH:
        db_ = Buf("dump")
        kb.bufs.append(db_)
        kb.dma("sp", outT, H[:, 0:NLAT], sb=db_)
    for name, (src, dst) in dbg_out.items():
        b = Buf("dbg" + name)
        kb.bufs.append(b)
        kb.dma("sp", dst, src, sb=b)
    fin = []
    seen = set()
    for b in kb.bufs:
        if b.sem is not None and id(b.sem) not in seen:
            seen.add(id(b.sem))
            fin.append(("d", b.sem, kb.sem_cnt[id(b.sem)]))
    kb.ins["sp"].append(dict(fn=None, deps=fin))
    kb.emit()
    stack.close()
    return nc


def _prep_inputs(x, c, ctx, c_ctx, ada_w, ada_b, ln_g, ln_b, ret_w_qkv, ret_w_g, ret_decay_logit, ret_w_o,
                 mla_w_dq, mla_g_q, mla_w_uq, mla_w_dkv, mla_g_kv, mla_w_ukv, mla_w_o, ffn_w_in, ffn_w_out):
    f = np.float32
    A = lambda a: np.ascontiguousarray(np.asarray(a, dtype=f))
    t = np.arange(NLAT, dtype=f)
    ret_inv = (f(10000.0) ** (-np.linspace(0.0, 1.0, 128, dtype=f))).astype(f)
    ang = (t[:, None] * ret_inv[None, :]).astype(f)
    cs, sn = np.cos(ang).astype(f).T, np.sin(ang).astype(f).T
    rtab = A(np.stack([cs, sn, cs * f(1.0 / 16), sn * f(1.0 / 16)]))
    rows = (np.arange(NLAT) // 64).astype(f)
    cols = (np.arange(NLAT) % 64).astype(f)
    ax_inv = (f(10000.0) ** (-np.arange(8, dtype=f) * f(2.0) / f(16))).astype(f)
    ax_inv = (f(10000.0) ** (-np.arange(16, dtype=f) * f(2.0) / f(32))).astype(f)
    ra = (rows[:, None] * ax_inv[None, :]).astype(f)
    ca = (cols[:, None] * ax_inv[None, :]).astype(f)
    rc, rs_, cc, cs_ = np.cos(ra).T, np.sin(ra).T, np.cos(ca).T, np.sin(ca).T
    mtab = A(np.stack([np.concatenate([rc, cc, rc, cc], 0), np.concatenate([-rs_, -cs_, rs_, cs_], 0)]))
    ident = np.eye(128, dtype=f)
    fm16 = lambda v: A(np.asarray(v, dtype=f).reshape(16, 128).T)
    common = dict(
        ada_w=A(ada_w),
        ada_bT=A(np.asarray(ada_b, dtype=f).reshape(DEPTH, 96, 128).transpose(0, 2, 1)),
        ln_gT=A(np.asarray(ln_g, dtype=f).reshape(DEPTH, 2, 16, 128).transpose(0, 1, 3, 2)),
        ln_bT=A(np.asarray(ln_b, dtype=f).reshape(DEPTH, 2, 16, 128).transpose(0, 1, 3, 2)),
        ret_w_qkv=A(ret_w_qkv), ret_w_g=A(ret_w_g), ret_w_o=A(ret_w_o),
        dlog=A(np.broadcast_to(np.asarray(ret_decay_logit, dtype=f).reshape(2, 1, 16), (2, 128, 16))),
        w_dq=A(mla_w_dq), mla_w_o=A(mla_w_o), ffn_w_in=A(ffn_w_in), ffn_w_out=A(ffn_w_out),
        g_qT=A(np.asarray(mla_g_q, dtype=f).reshape(2, 4, 128).transpose(0, 2, 1)),
        g_kvT=A(np.asarray(mla_g_kv, dtype=f).reshape(2, 4, 128).transpose(0, 2, 1)),
        rtab=rtab, mtab=mtab, ident=ident,
    )
    pa = np.concatenate([np.arange(0, 16), np.arange(32, 48), np.arange(16, 32), np.arange(48, 64)])
    pb = np.concatenate([np.arange(16, 32), np.arange(48, 64), np.arange(0, 16), np.arange(32, 48)])
    wdkv = np.asarray(mla_w_dkv, dtype=f)
    common["w_dkv"] = A(np.concatenate([wdkv[:, :, :512], wdkv[:, :, 512 + pa], wdkv[:, :, 512 + pb]], axis=2))
    wuq = np.asarray(mla_w_uq, dtype=f).reshape(2, 512, 16, 192)
    common["w_uq"] = A(np.concatenate([wuq[..., :128], wuq[..., 128 + pa], wuq[..., 128 + pb]], axis=3).reshape(2, 512, 16 * 256))
    wukv = np.asarray(mla_w_ukv, dtype=f).reshape(2, 512, 16, 256)
    common["w_ukv_k"] = A(wukv[..., :128].reshape(2, 512, 2048))
    common["w_ukv_v"] = A(wukv[..., 128:].reshape(2, 512, 2048))
    maps = []
    x = np.asarray(x, dtype=f)
    ctx = np.asarray(ctx, dtype=f)
    c = np.asarray(c, dtype=f)
    c_ctx = np.asarray(c_ctx, dtype=f)
    for core in range(8):
        b = core % 4
        m = dict(common)
        m["xT"] = A(np.concatenate([x[b].T, ctx[b].T], axis=1))
        cond = np.stack([fm16(c[b]), fm16(c_ctx)], axis=2)
        m["condT"] = A(cond.reshape(128, 32))
        maps.append(m)
    return maps


def kernel(**inputs):
    maps = _prep_inputs(**inputs)
    nc = build_program()
    res = run_bass_kernel_spmd(nc, maps, core_ids=list(range(8)))
    out = np.stack([np.ascontiguousarray(res.results[b]["outT"].T) for b in range(4)], axis=0)
    return out.astype(np.float32)
```
